# Optimizing a Trainium2 kernel written in Bass

```python
import jax, jax.numpy as jnp
from jax import lax
import numpy as np

D_MODEL = 1024
BATCH = 4
SEQ = 4096
DEPTH = 1

HEAD_DIM = 128
DSA_HEADS = 4
MOBA_HEADS = 4
IDX_HEADS = 8
IDX_DIM = 64
DSA_TOPK_MAX = 256
MOBA_BLOCK = 256
MOBA_TOPK = 3
D_FF = -(-8 * D_MODEL // 768) * 256
ROPE_THETA = 10000.0
EPS = 1e-6
NEG = -1e30
DSA_Q_BLOCK = 128
MOBA_Q_BLOCK = 32
DSA_W = DSA_HEADS * HEAD_DIM
MOBA_W = MOBA_HEADS * HEAD_DIM
IN_SIZES = (DSA_W, DSA_W, DSA_W, IDX_HEADS * IDX_DIM, IDX_DIM, IDX_HEADS,
            MOBA_W, MOBA_W, MOBA_W, D_MODEL, D_MODEL)
IN_W = sum(IN_SIZES)
N_MOD = 6

kernel_name = "hybrid_dsa_moba_gated_block"


def rms_norm(x, gain):
    xf = x.astype(jnp.float32)
    y = xf * lax.rsqrt(jnp.mean(xf * xf, axis=-1, keepdims=True) + EPS)
    return (y * gain.astype(jnp.float32)).astype(x.dtype)


def rope_tables(seq, dim):
    inv = ROPE_THETA ** (-jnp.arange(0, dim, 2, dtype=jnp.float32) / dim)
    ang = jnp.arange(seq, dtype=jnp.float32)[:, None] * inv[None, :]
    return jnp.cos(ang)[:, None, :], jnp.sin(ang)[:, None, :]


def apply_rope(x, cos, sin):
    x1, x2 = jnp.split(x.astype(jnp.float32), 2, axis=-1)
    return jnp.concatenate([x1 * cos - x2 * sin, x2 * cos + x1 * sin], axis=-1).astype(x.dtype)


def dsa_attention(q, k, v, qi, ki, wi):
    B, S, H, Dh = q.shape
    topk = min(DSA_TOPK_MAX, S // 4)
    nq = S // DSA_Q_BLOCK
    scale = Dh ** -0.5
    key_pos = jnp.arange(S)
    bi = jnp.arange(B)[:, None, None]

    def chunks(a):
        return jnp.moveaxis(a.reshape(a.shape[0], nq, DSA_Q_BLOCK, *a.shape[2:]), 1, 0)

    def block(args):
        qb, qib, wib, t = args
        logits = jnp.einsum('bqhd,bsd->bqhs', qib, ki).astype(jnp.float32)
        score = jnp.einsum('bqh,bqhs->bqs', wib.astype(jnp.float32), jax.nn.relu(logits))
        score = jnp.where((key_pos[None, :] <= t[:, None])[None], score, NEG)
        _, sel = lax.top_k(score, topk)
        ks = k[bi, sel]
        vs = v[bi, sel]
        s = jnp.einsum('bqhd,bqkhd->bhqk', qb, ks).astype(jnp.float32) * scale
        valid = sel <= t[None, :, None]
        s = jnp.where(valid[:, None], s, NEG)
        p = jax.nn.softmax(s, axis=-1).astype(v.dtype)
        return jnp.einsum('bhqk,bqkhd->bqhd', p, vs)

    t_c = jnp.arange(S).reshape(nq, DSA_Q_BLOCK)
    out = lax.map(block, (chunks(q), chunks(qi), chunks(wi), t_c))
    return jnp.moveaxis(out, 0, 1).reshape(B, S, H * Dh)


def moba_attention(q, k, v):
    B, S, H, Dh = q.shape
    nb = -(-S // MOBA_BLOCK)
    pad = nb * MOBA_BLOCK - S
    scale = Dh ** -0.5
    qh = q.transpose(0, 2, 1, 3)
    padw = ((0, 0), (0, 0), (0, pad), (0, 0))
    kh = jnp.pad(k.transpose(0, 2, 1, 3), padw).reshape(B, H, nb, MOBA_BLOCK, Dh)
    vh = jnp.pad(v.transpose(0, 2, 1, 3), padw).reshape(B, H, nb, MOBA_BLOCK, Dh)
    own = (jnp.arange(S) // MOBA_BLOCK).astype(jnp.int32)
    own_b = jnp.broadcast_to(own[:, None], (B, H, S, 1))
    n_sel = min(MOBA_TOPK, nb - 1)
    if n_sel > 0:
        kmean = jnp.mean(kh.astype(jnp.float32), axis=3)
        gate = jnp.einsum('bhsd,bhnd->bhsn', qh.astype(jnp.float32), kmean)
        past = jnp.arange(nb)[None, :] < own[:, None]
        gate = jnp.where(past, gate, NEG)
        _, sel = lax.top_k(gate, n_sel)
        blocks = jnp.concatenate([sel.astype(jnp.int32), own_b], axis=-1)
        bvalid = jnp.concatenate([sel < own[:, None], jnp.ones((B, H, S, 1), bool)], axis=-1)
    else:
        blocks = own_b
        bvalid = jnp.ones((B, H, S, 1), bool)
    nq = S // MOBA_Q_BLOCK
    bi = jnp.arange(B)[:, None, None, None]
    hi = jnp.arange(H)[None, :, None, None]
    offs = jnp.arange(MOBA_BLOCK)

    def chunks(a):
        return jnp.moveaxis(a.reshape(B, H, nq, MOBA_Q_BLOCK, *a.shape[3:]), 2, 0)

    def block(args):
        qb, bl, bv, t = args
        kg = kh[bi, hi, bl]
        vg = vh[bi, hi, bl]
        s = jnp.einsum('bhqd,bhqnkd->bhqnk', qb, kg).astype(jnp.float32) * scale
        kpos = bl[..., None] * MOBA_BLOCK + offs
        mask = bv[..., None] & (kpos <= t[:, None, None])
        s = jnp.where(mask, s, NEG)
        shp = s.shape
        p = jax.nn.softmax(s.reshape(shp[0], shp[1], shp[2], -1), axis=-1).reshape(shp).astype(v.dtype)
        return jnp.einsum('bhqnk,bhqnkd->bhqd', p, vg)

    t_c = jnp.arange(S).reshape(nq, MOBA_Q_BLOCK)
    out = lax.map(block, (chunks(qh), chunks(blocks), chunks(bvalid), t_c))
    out = jnp.moveaxis(out, 0, 2).reshape(B, H, S, Dh)
    return out.transpose(0, 2, 1, 3).reshape(B, S, H * Dh)


def setup_inputs(seed: int = 0) -> dict:
    key = jax.random.key(seed)
    ks = jax.random.split(key, 16)
    f32 = jnp.float32

    def w(k, shape, fan_in):
        return jax.random.normal(k, shape, f32) * fan_in ** -0.5

    def gain(k, shape):
        return 1.0 + 0.02 * jax.random.normal(k, shape, f32)

    return {
        "x": jax.random.normal(ks[0], (BATCH, SEQ, D_MODEL), f32),
        "c": jax.random.normal(ks[1], (BATCH, D_MODEL), f32),
        "w_mod": w(ks[2], (DEPTH, D_MODEL, N_MOD * D_MODEL), D_MODEL),
        "b_mod": 0.02 * jax.random.normal(ks[3], (DEPTH, N_MOD * D_MODEL), f32),
        "g_mix_norm": gain(ks[4], (DEPTH, D_MODEL)),
        "g_ffn_norm": gain(ks[5], (DEPTH, D_MODEL)),
        "w_in": w(ks[6], (DEPTH, D_MODEL, IN_W), D_MODEL),
        "g_q_dsa": gain(ks[7], (DEPTH, HEAD_DIM)),
        "g_k_dsa": gain(ks[8], (DEPTH, HEAD_DIM)),
        "g_q_moba": gain(ks[9], (DEPTH, HEAD_DIM)),
        "g_k_moba": gain(ks[10], (DEPTH, HEAD_DIM)),
        "w_br_dsa": w(ks[11], (DEPTH, DSA_W, D_MODEL), DSA_W),
        "w_br_moba": w(ks[12], (DEPTH, MOBA_W, D_MODEL), MOBA_W),
        "w_out": w(ks[13], (DEPTH, D_MODEL, D_MODEL), D_MODEL),
        "w_gate_up": w(ks[14], (DEPTH, D_MODEL, 2 * D_FF), D_MODEL),
        "w_down": w(ks[15], (DEPTH, D_FF, D_MODEL), D_FF),
    }


def reference(x, c, w_mod, b_mod, g_mix_norm, g_ffn_norm, w_in, g_q_dsa, g_k_dsa,
              g_q_moba, g_k_moba, w_br_dsa, w_br_moba, w_out, w_gate_up, w_down):
    B, S, _ = x.shape
    cos_h, sin_h = rope_tables(S, HEAD_DIM)
    cos_i, sin_i = rope_tables(S, IDX_DIM)
    split_at = np.cumsum(IN_SIZES)[:-1].tolist()
    idx_w_scale = (IDX_HEADS ** -0.5) * (IDX_DIM ** -0.5)
    for l in range(DEPTH):
        mod = jax.nn.silu(c) @ w_mod[l] + b_mod[l]
        sh_m, sc_m, gt_m, sh_f, sc_f, gt_f = [m[:, None, :] for m in jnp.split(mod, N_MOD, axis=-1)]

        h = rms_norm(x, g_mix_norm[l]) * (1.0 + sc_m) + sh_m
        proj = h @ w_in[l]
        (q_a, k_a, v_a, q_i, k_i, w_i, q_b, k_b, v_b, gate_a, gate_b) = jnp.split(proj, split_at, axis=-1)
        q_a = apply_rope(rms_norm(q_a.reshape(B, S, DSA_HEADS, HEAD_DIM), g_q_dsa[l]), cos_h, sin_h)
        k_a = apply_rope(rms_norm(k_a.reshape(B, S, DSA_HEADS, HEAD_DIM), g_k_dsa[l]), cos_h, sin_h)
        v_a = v_a.reshape(B, S, DSA_HEADS, HEAD_DIM)
        q_i = apply_rope(q_i.reshape(B, S, IDX_HEADS, IDX_DIM), cos_i, sin_i)
        k_i = apply_rope(k_i[:, :, None, :], cos_i, sin_i)[:, :, 0, :]
        w_i = w_i * idx_w_scale
        o_a = dsa_attention(q_a, k_a, v_a, q_i, k_i, w_i)

        q_b = apply_rope(rms_norm(q_b.reshape(B, S, MOBA_HEADS, HEAD_DIM), g_q_moba[l]), cos_h, sin_h)
        k_b = apply_rope(rms_norm(k_b.reshape(B, S, MOBA_HEADS, HEAD_DIM), g_k_moba[l]), cos_h, sin_h)
        v_b = v_b.reshape(B, S, MOBA_HEADS, HEAD_DIM)
        o_b = moba_attention(q_b, k_b, v_b)

        merged = (jax.nn.sigmoid(gate_a) * (o_a @ w_br_dsa[l])
                  + jax.nn.sigmoid(gate_b) * (o_b @ w_br_moba[l]))
        x = x + gt_m * (merged @ w_out[l])

        h = rms_norm(x, g_ffn_norm[l]) * (1.0 + sc_f) + sh_f
        g_ff, u_ff = jnp.split(h @ w_gate_up[l], 2, axis=-1)
        x = x + gt_f * ((jax.nn.silu(g_ff) * u_ff) @ w_down[l])
    return x
```

```python
import contextlib
import numpy as np
import concourse.bass as bass
import concourse.mybir as mybir
from concourse.bass_utils import run_bass_kernel_spmd

F32 = mybir.dt.float32
BF16 = mybir.dt.bfloat16
ALU = mybir.AluOpType
AF = mybir.ActivationFunctionType
AX = mybir.AxisListType

ENGS = ["pe", "act", "dve", "pool", "sp"]
DMA_SLOTS = {"sp": 24, "pool": 8, "act": 8}

D = 1024
S = 4096
NT = 32
NO = 16
DFF = 2816
IN_W = 5704
EPS = 1e-6
NB = 16
MNEG = -30000.0
ATT_SCALE = 128 ** -0.5
IDX_W_SCALE = (8 ** -0.5) * (64 ** -0.5)


class Prog:
    def __init__(self, nc):
        self.nc = nc
        self.ins = {e: [] for e in ENGS}
        self.last_w = {}
        self.readers = {}
        self.known = {e: {} for e in ENGS}
        self.dma_count = {q: 0 for q in DMA_SLOTS}
        self.dma_last = {}
        self.last_real = {}

    def _deps_for(self, reads, writes):
        deps = []
        for k in reads:
            w = self.last_w.get(k)
            if w is not None:
                deps.append(w)
        for k in writes:
            w = self.last_w.get(k)
            if w is not None:
                deps.append(w)
            deps.extend(self.readers.get(k, ()))
        return deps

    def _commit(self, me, reads, writes):
        for k in reads:
            self.readers.setdefault(k, []).append(me)
        for k in writes:
            self.last_w[k] = me
            self.readers[k] = []

    def _waits(self, eng, deps):
        kn = self.known[eng]
        best = {}
        for d in deps:
            if d[0] == "dma":
                key = d[:3]
                val = d[3]
            else:
                if d[0] == eng and eng == "pe":
                    continue
                key = d[0]
                val = d[1]
            if kn.get(key, -1) >= val:
                continue
            if best.get(key, -1) < val:
                best[key] = val
        out = []
        for key, val in best.items():
            kn[key] = val
            out.append((key, val))
        return out

    def op(self, eng, emit, reads=(), writes=()):
        seq = len(self.ins[eng])
        waits = self._waits(eng, self._deps_for(reads, writes))
        self.ins[eng].append(dict(emit=emit, waits=waits, signal=False, dma=None))
        self.last_real[eng] = seq
        self._commit((eng, seq), reads, writes)

    def dma(self, q, emit, reads=(), writes=()):
        n = self.dma_count[q]
        self.dma_count[q] = n + 1
        slot = n % DMA_SLOTS[q]
        gen = n // DMA_SLOTS[q]
        deps = self._deps_for(reads, writes)
        if gen > 0:
            deps.append(("dma", q, slot, gen - 1))
        waits = self._waits(q, deps)
        self.ins[q].append(dict(emit=emit, waits=waits, signal=False, dma=(q, slot, gen)))
        self.dma_last[(q, slot)] = gen
        self._commit(("dma", q, slot, gen), reads, writes)

    def barrier(self):
        deps = [(e, s) for e, s in self.last_real.items()]
        deps += [("dma", q, slot, gen) for (q, slot), gen in self.dma_last.items()]
        for e in ENGS:
            waits = self._waits(e, deps)
            if waits:
                self.ins[e].append(dict(emit=None, waits=waits, signal=False, dma=None))
        self.last_w = {}
        self.readers = {}

    def finalize(self):
        nc = self.nc
        self.barrier()
        for e in ENGS:
            for rec in self.ins[e]:
                for key, val in rec["waits"]:
                    if isinstance(key, str):
                        self.ins[key][val]["signal"] = True
        for e in ENGS:
            c = 0
            for rec in self.ins[e]:
                if rec["signal"]:
                    c += 1
                rec["cnt"] = c
        with contextlib.ExitStack() as st:
            esem = {e: st.enter_context(nc.semaphore("s_" + e)) for e in ENGS}
            dsem = {}
            for q, n in DMA_SLOTS.items():
                for s in range(n):
                    dsem[(q, s)] = st.enter_context(nc.semaphore(f"d_{q}_{s}"))
            block = st.enter_context(nc.Block())

            def run(ename):
                def f(eng):
                    for rec in self.ins[ename]:
                        for key, val in rec["waits"]:
                            if isinstance(key, str):
                                eng.wait_ge(esem[key], self.ins[key][val]["cnt"])
                            else:
                                _, q, slot = key
                                eng.wait_ge(dsem[(q, slot)], 16 * (val + 1))
                        if rec["emit"] is None:
                            continue
                        bi = rec["emit"](eng)
                        if rec["dma"] is not None:
                            q, slot, gen = rec["dma"]
                            bi.then_inc(dsem[(q, slot)], 16)
                        elif rec["signal"]:
                            bi.then_inc(esem[ename], 1)
                return f

            block.tensor(run("pe"))
            block.scalar(run("act"))
            block.vector(run("dve"))
            block.gpsimd(run("pool"))
            block.sync(run("sp"))


def _L(name, *args, **kwargs):
    def f(eng):
        return getattr(eng, name)(*args, **kwargs)
    return f


def run_pipeline(items, stages):
    n, ns = len(items), len(stages)
    ctxs = [dict() for _ in items]
    for step in range(n + ns - 1):
        for si, fn in enumerate(stages):
            j = step - si
            if 0 <= j < n:
                fn(items[j], ctxs[j])


def pipeline_gen(items, stages):
    n, ns = len(items), len(stages)
    ctxs = [dict() for _ in items]
    for step in range(n + ns - 1):
        for si, fn in enumerate(stages):
            j = step - si
            if 0 <= j < n:
                fn(items[j], ctxs[j])
        yield


def interleave_all(*gens):
    gens = [g for g in gens if g is not None]
    while gens:
        for g in list(gens):
            try:
                next(g)
            except StopIteration:
                gens.remove(g)


class Rot:
    def __init__(self, st, nc, name, n, shape, dt):
        self.t = [st.enter_context(nc.sbuf_tensor(f"{name}{i}", shape, dt)) for i in range(n)]
        self.k = [f"{name}{i}" for i in range(n)]
        self.i = 0

    def next(self):
        j = self.i % len(self.t)
        self.i += 1
        return self.t[j], self.k[j]


STOP = None
DEBUG = False


class _Stop(Exception):
    pass


def build():
    nc = bass.Bass("TRN2", target_bir_lowering=False)
    P = Prog(nc)
    try:
        _build(nc, P)
    except _Stop:
        pass
    return nc


def _build(nc, P):

    def chk(name):
        if STOP == name:
            P.finalize()
            raise _Stop()

    def din(name, shape):
        return nc.dram_tensor(name, shape, F32, kind="ExternalInput")

    x_d = din("x", [S, D])
    cT_d = din("cT", [128, 8])
    wmod_d = din("w_mod", [D, 6 * D])
    bmod_d = din("b_mod", [1, 6 * D])
    gmix_d = din("g_mix", [1, D])
    gffn_d = din("g_ffn", [1, D])
    win_d = din("w_in", [D, IN_W])
    gq_d = {"a": din("g_q_dsa", [1, 128]), "b": din("g_q_moba", [1, 128])}
    gk_d = {"a": din("g_k_dsa", [1, 128]), "b": din("g_k_moba", [1, 128])}
    wbr_d = {"a": din("w_br_dsa", [512, D]), "b": din("w_br_moba", [512, D])}
    wout_d = din("w_out", [D, D])
    wgu_d = din("w_gate_up", [D, 2 * DFF])
    wdn_d = din("w_down", [DFF, D])
    ropeH_d = din("ropeH", [S, 256])
    ropeI_d = din("ropeI", [S, 128])
    pb_d = din("pb", [128, 128])
    out_d = nc.dram_tensor("out", [NO * 128, D], F32, kind="ExternalOutput")

    def scr(name, shape, dt):
        return nc.dram_tensor(name, shape, dt, kind=("ExternalOutput" if DEBUG else "Internal"))

    KT_d = {"a": scr("KaT_d", [NT, 128, 512], BF16), "b": scr("KbT_d", [NT, 128, 512], BF16)}
    V_d = {"a": scr("Va_d", [NT, 128, 528], BF16), "b": scr("Vb_d", [NT, 128, 528], BF16)}
    QT_d = {"a": scr("QaT_d", [NO, 128, 512], BF16), "b": scr("QbT_d", [NO, 128, 512], BF16)}
    qiT_d = scr("qiT_d", [NO, 128, 512], BF16)
    kiT_d = scr("kiT_d", [NT, 128, 128], BF16)
    g_d = {"a": scr("gA_d", [8, 128, NO * 128], BF16), "b": scr("gB_d", [8, 128, NO * 128], BF16)}
    oT_d = {"a": scr("oaT_d", [NO, 128, 512], BF16), "b": scr("obT_d", [NO, 128, 512], BF16)}
    modd = scr("modd", [4, 128, D], F32)
    x1_d = scr("x1_d", [NO * 128, D], F32)

    def A(t, off, pat):
        return bass.AP(t, off, [list(p) for p in pat])

    with contextlib.ExitStack() as top:
        def sb(name, shape, dt, st=top):
            return st.enter_context(nc.sbuf_tensor(name, shape, dt))

        def psum(name, shape, dt):
            return top.enter_context(nc.psum_tensor(name, shape, dt))

        pA = [psum(f"pA{i}", [128, 512], F32) for i in range(2)]
        pB = [psum(f"pB{i}", [128, 512], F32) for i in range(2)]
        pO = [psum(f"pO{i}", [128, 512], F32) for i in range(2)]
        pT = [psum(f"pT{i}", [128, 1024], BF16) for i in range(1)]
        pC = [psum(f"pC{i}", [128, 512], F32) for i in range(1)]
        pcnt = {"A": 0, "B": 0, "O": 0, "T": 0, "C": 0}

        def nxt(which):
            arr = {"A": pA, "B": pB, "O": pO, "T": pT, "C": pC}[which]
            j = pcnt[which] % len(arr)
            pcnt[which] += 1
            return arr[j], f"p{which}{j}"

        ident_f = sb("ident_f", [128, 128], F32)
        ident_b = sb("ident_b", [128, 128], BF16)
        caus_f = sb("caus_f", [128, 128], F32)
        caus_b = sb("caus_b", [128, 128], BF16)
        pb_f = sb("pb_f", [128, 128], F32)
        pb_b = sb("pb_b", [128, 128], BF16)
        pow2c = sb("pow2c", [128, NB + 1], F32)
        wi = sb("wi", [128, NO, 8], F32)
        ssm = sb("ssm", [128, 8], F32)
        eps_t = sb("eps_t", [128, 1], F32)

        P.op("pool", _L("memset", ident_f[:], 1.0), writes=["ident_f"])
        P.op("pool", _L("affine_select", out=ident_f[:], in_=ident_f[:], pattern=[[-1, 128]],
                                                compare_op=ALU.is_equal, fill=0.0, base=0, channel_multiplier=1),
             writes=["ident_f"])
        P.op("pool", _L("tensor_copy", out=ident_b[:], in_=ident_f[:]), reads=["ident_f"], writes=["ident_b"])
        P.op("pool", _L("memset", caus_f[:], 0.0), writes=["caus_f"])
        P.op("pool", _L("affine_select", out=caus_f[:], in_=caus_f[:], pattern=[[-1, 128]],
                                                compare_op=ALU.is_ge, fill=-1e30, base=0, channel_multiplier=1),
             writes=["caus_f"])
        P.op("pool", _L("tensor_scalar", out=caus_b[:], in0=caus_f[:], scalar1=MNEG, scalar2=None, op0=ALU.max),
             reads=["caus_f"], writes=["caus_b"])
        P.dma("sp", _L("dma_start", out=pb_f[:], in_=pb_d.ap()), writes=["pb_f"])
        P.op("pool", _L("tensor_scalar", out=pb_b[:], in0=pb_f[:], scalar1=MNEG, scalar2=None, op0=ALU.max),
             reads=["pb_f"], writes=["pb_b"])
        P.op("pool", _L("memset", eps_t[:], EPS), writes=["eps_t"])
        for n in range(NB + 1):
            P.op("pool", _L("memset", pow2c[:, n:n + 1], 2.0 ** -(n + 1)), writes=["pow2c"])

        with contextlib.ExitStack() as ph:
            def sbp(name, shape, dt):
                return sb(name, shape, dt, st=ph)

            G1m = sbp("G1m", [128, D], F32)
            SHm = sbp("SHm", [128, D], F32)

            hT = sbp("hT", [128, 8, S], BF16)
            xbuf = Rot(ph, nc, "xbuf", 3, [128, D], F32)
            tmpf = Rot(ph, nc, "tmpf", 2, [128, D], F32)
            hb = Rot(ph, nc, "hb", 3, [128, D], BF16)
            junk = sbp("junkb", [128, D], BF16)
            mkeys = ["SHm0", "SHm1", "G1m0", "G1m1"]
            ssm_r = Rot(ph, nc, "ssmr", 3, [128, 8], F32)

            def p1_s0(t, c):
                xt, xk = xbuf.next()
                P.dma("sp", _L("dma_start", out=xt[:], in_=x_d[t * 128:(t + 1) * 128, :]), writes=[xk])
                sm, smk = ssm_r.next()
                c["x"], c["sm"] = (xt, xk), (sm, smk)
                P.op("act", _L("activation", out=junk[:], in_=xt[:], func=AF.Square, accum_out=sm[:, 0:1]),
                     reads=[xk], writes=["junkb", smk])
                P.op("dve", _L("tensor_scalar", out=sm[:, 1:2], in0=sm[:, 0:1], scalar1=1.0 / D, scalar2=EPS,
                               op0=ALU.mult, op1=ALU.add), writes=[smk])
                P.op("act", _L("activation", out=sm[:, 2:3], in_=sm[:, 1:2], func=AF.Ln), writes=[smk])
                P.op("act", _L("activation", out=sm[:, 3:4], in_=sm[:, 2:3], func=AF.Exp, scale=-0.5), writes=[smk])

            def p1_s1(t, c):
                (xt, xk), (sm, smk) = c["x"], c["sm"]
                tf, tk = tmpf.next()
                P.op("dve", _L("scalar_tensor_tensor", out=tf[:], in0=xt[:], scalar=sm[:, 3:4], in1=G1m[:],
                               op0=ALU.mult, op1=ALU.mult), reads=[xk, smk] + mkeys, writes=[tk])
                hbt, hk = hb.next()
                P.op("dve", _L("tensor_tensor", out=hbt[:], in0=tf[:], in1=SHm[:], op=ALU.add),
                     reads=[tk] + mkeys, writes=[hk])
                c["hb"] = (hbt, hk)

            def p1_s2(t, c):
                hbt, hk = c["hb"]
                pt, pk = nxt("T")
                for k in range(8):
                    P.op("pe", _L("transpose", out=pt[:, k * 128:(k + 1) * 128], in_=hbt[:, k * 128:(k + 1) * 128],
                                  identity=ident_b[:]), reads=[hk, "ident_b"], writes=[pk])
                P.op("act", _L("copy", out=A(hT, t * 128, [[8 * S, 128], [S, 8], [1, 128]]),
                               in_=A(pt, 0, [[1024, 128], [128, 8], [1, 128]])), writes=[pk, ("hT", t)])

            with contextlib.ExitStack() as p0:
                wstg = Rot(p0, nc, "wstg0", 2, [128, 8 * 512], F32)
                sc_t = sb("sc_t", [128, 8], F32, st=p0)
                scb = sb("scb", [128, 8 * 128], F32, st=p0)
                bm = sb("bm", [128, 6 * D], F32, st=p0)
                gmx = sb("gmx", [128, D], F32, st=p0)
                gff = sb("gff", [128, D], F32, st=p0)
                mt = [sb(f"mt{i}", [128, D], F32, st=p0) for i in range(4)]
                P.dma("sp", _L("dma_start", out=sc_t[:], in_=cT_d.ap()), writes=["sc_t"])
                P.dma("sp", _L("dma_start", out=bm[:], in_=A(bmod_d, 0, [[0, 128], [1, 6 * D]])), writes=["bm"])
                P.dma("sp", _L("dma_start", out=gmx[:], in_=A(gmix_d, 0, [[0, 128], [1, D]])), writes=["gmx"])
                P.dma("sp", _L("dma_start", out=gff[:], in_=A(gffn_d, 0, [[0, 128], [1, D]])), writes=["gff"])
                P.op("act", _L("activation", out=sc_t[:], in_=sc_t[:], func=AF.Silu), reads=["sc_t"], writes=["sc_t"])
                P.op("dve", _L("tensor_copy", out=A(scb, 0, [[1024, 128], [128, 8], [1, 128]]),
                                                    in_=A(sc_t, 0, [[8, 128], [1, 8], [0, 128]])),
                     reads=["sc_t"], writes=["scb"])
                wm_src = wmod_d.ap().rearrange("(k p) n -> p k n", p=128)
                dest = {0: (SHm, "SHm", None), 1: (G1m, "G1m", gmx), 2: (mt[0], "mt0", None),
                        3: (mt[1], "mt1", None), 4: (mt[2], "mt2", gff), 5: (mt[3], "mt3", None)}
                def p0_chunks():
                  for nch in range(12):
                    stg, sk = wstg.next()
                    sv = A(stg, 0, [[4096, 128], [512, 8], [1, 512]])
                    P.dma("sp", _L("dma_start", out=sv, in_=wm_src[:, :, nch * 512:(nch + 1) * 512]),
                          writes=[sk])
                    ps, pk = nxt("A")
                    for k in range(8):
                        P.op("pe", _L("matmul",
                            ps[:, :], lhsT=scb[:, k * 128:(k + 1) * 128], rhs=stg[:, k * 512:(k + 1) * 512],
                            start=(k == 0), stop=(k == 7)), reads=[sk, "scb"], writes=[pk])
                    m, half = nch // 2, nch % 2
                    dt_, dk, gg = dest[m]
                    P.op("dve", _L("tensor_tensor",
                        out=dt_[:, half * 512:(half + 1) * 512], in0=ps[:, :], in1=bm[:, nch * 512:(nch + 1) * 512],
                        op=ALU.add), reads=["bm"], writes=[pk, dk + str(half)])
                    if gg is not None:
                        P.op("dve", _L("scalar_tensor_tensor",
                            out=dt_[:, half * 512:(half + 1) * 512], in0=dt_[:, half * 512:(half + 1) * 512], scalar=1.0,
                            in1=gg[:, half * 512:(half + 1) * 512], op0=ALU.add, op1=ALU.mult),
                            reads=["gmx", "gff"], writes=[dk + str(half)])
                    yield

                g0 = p0_chunks()
                for _ in range(4):
                    next(g0)
                interleave_all(g0, pipeline_gen(list(range(NT)), [p1_s0, p1_s1, p1_s2]))
                for i in range(4):
                    P.dma("sp", _L("dma_start", out=modd[i], in_=mt[i][:]), reads=[f"mt{i}0", f"mt{i}1"],
                          writes=[("modd", i)])
                P.barrier()
                chk("p0")

            wstg = Rot(ph, nc, "wstg", 1, [128, 8 * 512], F32)
            wbf = Rot(ph, nc, "wbf", 2, [128, 8 * 512], BF16)


            P.barrier()
            chk("p1")
            win_src = win_d.ap().rearrange("(k p) n -> p k n", p=128)
            xs_r = Rot(ph, nc, "xs", 4, [128, 512], F32)
            sq_r = Rot(ph, nc, "sq", 2, [128, 512], F32)
            xg_r = Rot(ph, nc, "xg", 2, [128, 512], F32)
            t1_r = Rot(ph, nc, "t1", 3, [128, 512], F32)
            t2_r = Rot(ph, nc, "t2", 3, [128, 512], F32)
            ob_r = Rot(ph, nc, "ob", 3, [128, 512], BF16)
            st_r = Rot(ph, nc, "stT", 3, [128, 512], BF16)
            rt_r = Rot(ph, nc, "rt", 4, [128, 256], F32)
            vst_r = Rot(ph, nc, "vst", 2, [128, 528], BF16)
            gbc = sbp("gbc", [128, 128], F32)
            ss4_r = Rot(ph, nc, "ss4r", 3, [128, 16], F32)
            for vt in vst_r.t:
                P.op("pool", _L("memset", vt[:], 0.0), writes=["vinit"])
                P.op("pool", _L("memset", A(vt, 128, [[528, 128], [132, 4], [1, 1]]), 1.0), writes=["vinit"])
            P.barrier()

            def load_w(col0, ncols):
                stg, sk = wstg.next()
                wb, wk = wbf.next()
                sv = A(stg, 0, [[4096, 128], [ncols, 8], [1, ncols]])
                wv = A(wb, 0, [[4096, 128], [ncols, 8], [1, ncols]])
                P.dma("sp", _L("dma_start", out=sv, in_=win_src[:, :, col0:col0 + ncols]), writes=[sk])
                P.op("pool", _L("tensor_copy", out=wv, in_=sv), reads=[sk], writes=[wk])
                return wb, wk

            def proj_tile(t, wb, wk, ncols):
                ps, pk = nxt("A")
                for k in range(8):
                    P.op("pe", _L("matmul", ps[:, 0:ncols], lhsT=hT[:, k, t * 128:(t + 1) * 128],
                                                       rhs=wb[:, k * ncols:(k + 1) * ncols], start=(k == 0), stop=(k == 7)),
                         reads=[("hT", t), wk], writes=[pk])
                return ps, pk

            def rope(xin, xk_, nh, hd, rt, rk, defer_add=False):
                hh = hd // 2
                t1, k1 = t1_r.next()
                t2, k2 = t2_r.next()
                ob, ok = ob_r.next()
                W = nh * hd
                nh1 = nh // 2 if nh >= 2 else nh
                P.op("dve", _L("tensor_tensor", out=A(t1, 0, [[512, 128], [hd, nh1], [1, hd]]),
                                                       in0=A(xin, 0, [[512, 128], [hd, nh1], [1, hd]]),
                                                       in1=A(rt, 0, [[256, 128], [0, nh1], [1, hd]]), op=ALU.mult),
                     reads=[xk_, rk], writes=[(k1, 0)])
                if nh1 < nh:
                    P.op("pool", _L("tensor_tensor", out=A(t1, nh1 * hd, [[512, 128], [hd, nh - nh1], [1, hd]]),
                                                           in0=A(xin, nh1 * hd, [[512, 128], [hd, nh - nh1], [1, hd]]),
                                                           in1=A(rt, 0, [[256, 128], [0, nh - nh1], [1, hd]]), op=ALU.mult),
                         reads=[xk_, rk], writes=[(k1, 1)])
                P.op("pool", _L("tensor_tensor", out=A(t2, 0, [[512, 128], [hd, nh], [1, hh]]),
                                                       in0=A(xin, hh, [[512, 128], [hd, nh], [1, hh]]),
                                                       in1=A(rt, hd, [[256, 128], [0, nh], [1, hh]]), op=ALU.mult),
                     reads=[xk_, rk], writes=[k2])
                P.op("pool", _L("tensor_tensor", out=A(t2, hh, [[512, 128], [hd, nh], [1, hh]]),
                                                       in0=A(xin, 0, [[512, 128], [hd, nh], [1, hh]]),
                                                       in1=A(rt, hd + hh, [[256, 128], [0, nh], [1, hh]]), op=ALU.mult),
                     reads=[xk_, rk], writes=[k2])
                def fin():
                    P.op("dve", _L("tensor_tensor", out=ob[:, 0:W], in0=t1[:, 0:W], in1=t2[:, 0:W], op=ALU.add),
                         reads=[(k1, 0), (k1, 1), k2], writes=[ok])
                    return ob, ok
                if defer_add:
                    return fin
                return fin()

            def transpose_store(ob, ok, nblk, dst_ap):
                pt, pk = nxt("T")
                for h in range(nblk):
                    P.op("pe", _L("transpose", out=pt[:, h * 128:(h + 1) * 128], in_=ob[:, h * 128:(h + 1) * 128],
                                                          identity=ident_b[:]), reads=[ok, "ident_b"], writes=[pk])
                stt, sk = st_r.next()
                P.op("act", _L("copy", out=stt[:, 0:nblk * 128], in_=pt[:, 0:nblk * 128]), writes=[pk, sk])
                P.dma("act", _L("dma_start", out=dst_ap, in_=stt[:, 0:nblk * 128]), reads=[sk], writes=[("scr", id(dst_ap))])

            qbanks = [(pA[0], "pA0"), (pA[1], "pA1"), (pB[0], "pB0"), (pB[1], "pB1")]
            qcnt = [0]

            def qk_group(col0, tiles, g_dram, dst, as_gen=False):
                wb, wk = load_w(col0, 512)
                P.dma("sp", _L("dma_start", out=gbc[:], in_=A(g_dram, 0, [[0, 128], [1, 128]])), writes=["gbc"])

                def s0(t, c):
                    ps, pk = qbanks[qcnt[0] % 4]
                    qcnt[0] += 1
                    for k in range(8):
                        P.op("pe", _L("matmul", ps[:, 0:512], lhsT=hT[:, k, t * 128:(t + 1) * 128],
                                      rhs=wb[:, k * 512:(k + 1) * 512], start=(k == 0), stop=(k == 7)),
                             reads=[("hT", t), wk], writes=[pk])
                    ss4, s4k = ss4_r.next()
                    for h in range(4):
                        P.op("act", _L("activation", out=junk[:, 0:128], in_=ps[:, h * 128:(h + 1) * 128], func=AF.Square,
                                       accum_out=ss4[:, h:h + 1]), writes=[pk, "junkb", s4k])
                    rt, rk = rt_r.next()
                    P.dma("sp", _L("dma_start", out=rt[:], in_=ropeH_d[t * 128:(t + 1) * 128, :]), writes=[rk])
                    c["ps"], c["rt"], c["ss4"] = (ps, pk), (rt, rk), (ss4, s4k)

                def s1(t, c):
                    ss4, s4k = c["ss4"]
                    P.op("act", _L("activation", out=ss4[:, 8:12], in_=ss4[:, 0:4], func=AF.Ln, scale=1.0 / 128,
                                   bias=eps_t[:, 0:1]), reads=["eps_t"], writes=[s4k])
                    P.op("act", _L("activation", out=ss4[:, 12:16], in_=ss4[:, 8:12], func=AF.Exp, scale=-0.5), writes=[s4k])

                def s2a(t, c):
                    (ps, pk), (ss4, s4k), (rt, rk) = c["ps"], c["ss4"], c["rt"]
                    xg, gk_ = xg_r.next()
                    for h in range(4):
                        P.op("dve", _L("scalar_tensor_tensor", out=xg[:, h * 128:(h + 1) * 128], in0=ps[:, h * 128:(h + 1) * 128],
                                       scalar=ss4[:, 12 + h:13 + h], in1=gbc[:], op0=ALU.mult, op1=ALU.mult),
                             reads=[s4k, "gbc"], writes=[pk, gk_])
                    c["rp"] = rope(xg, gk_, 4, 128, rt, rk, defer_add=True)

                def s2b(t, c):
                    c["ob"] = c["rp"]()

                def s3(t, c):
                    ob, ok = c["ob"]
                    transpose_store(ob, ok, 4, dst[t])

                if as_gen:
                    return pipeline_gen(list(tiles), [s0, s1, s2a, s2b, s3])
                run_pipeline(list(tiles), [s0, s1, s2a, s2b, s3])

            def v_group(col0, dst):
                wb, wk = load_w(col0, 512)
                for t in range(NT):
                    ps, pk = nxt("O")
                    for k in range(8):
                        P.op("pe", _L("matmul", ps[:, 0:512], lhsT=hT[:, k, t * 128:(t + 1) * 128],
                                      rhs=wb[:, k * 512:(k + 1) * 512], start=(k == 0), stop=(k == 7)),
                             reads=[("hT", t), wk], writes=[pk])
                    vt, vk = vst_r.next()
                    P.op("act", _L("copy", out=A(vt, 0, [[528, 128], [132, 4], [1, 128]]),
                                   in_=A(ps, 0, [[512, 128], [128, 4], [1, 128]])), writes=[pk, vk])
                    P.dma("act", _L("dma_start", out=dst[t], in_=vt[:]), reads=[vk], writes=[("vscr", t, id(dst))])
                    yield

            def qi_group():
                wb, wk = load_w(1536, 512)

                def s0(t, c):
                    ps, pk = proj_tile(t, wb, wk, 512)
                    xs, xk_ = xs_r.next()
                    P.op("act", _L("copy", out=xs[:], in_=ps[:, :]), writes=[pk, xk_])
                    rt, rk = rt_r.next()
                    P.dma("sp", _L("dma_start", out=rt[:, 0:128], in_=ropeI_d[t * 128:(t + 1) * 128, :]), writes=[rk])
                    c["xs"], c["rt"] = (xs, xk_), (rt, rk)

                def s1(t, c):
                    (xs, xk_), (rt, rk) = c["xs"], c["rt"]
                    c["ob"] = rope(xs, xk_, 8, 64, rt, rk)

                def s2(t, c):
                    ob, ok = c["ob"]
                    transpose_store(ob, ok, 4, qiT_d[t])

                run_pipeline(list(range(NO)), [s0, s1, s2])

            def kiwi_group():
                wb, wk = load_w(2048, 72)

                def s0(t, c):
                    ps, pk = proj_tile(t, wb, wk, 72)
                    xs, xk_ = xs_r.next()
                    P.op("act", _L("copy", out=xs[:, 0:72], in_=ps[:, 0:72]), writes=[pk, xk_])
                    if t < NO:
                        P.op("dve", _L("tensor_scalar", out=wi[:, t, :], in0=xs[:, 64:72], scalar1=IDX_W_SCALE,
                                       scalar2=None, op0=ALU.mult), reads=[xk_], writes=[("wi", t)])
                    rt, rk = rt_r.next()
                    P.dma("sp", _L("dma_start", out=rt[:, 0:128], in_=ropeI_d[t * 128:(t + 1) * 128, :]), writes=[rk])
                    c["xs"], c["rt"] = (xs, xk_), (rt, rk)

                def s1(t, c):
                    (xs, xk_), (rt, rk) = c["xs"], c["rt"]
                    ob, ok = rope(xs, xk_, 1, 64, rt, rk)
                    P.op("dve", _L("tensor_copy", out=ob[:, 64:128], in_=ob[:, 0:64]), reads=[ok], writes=[ok])
                    c["ob"] = (ob, ok)

                def s2(t, c):
                    ob, ok = c["ob"]
                    transpose_store(ob, ok, 1, kiT_d[t])

                run_pipeline(list(range(NT)), [s0, s1, s2])

            def gate_group(col0, dst):
                for hf in range(2):
                    wb, wk = load_w(col0 + hf * 512, 512)
                    for fcl in range(4):
                        fc = hf * 4 + fcl
                        for tg in range(4):
                            ps, pk = nxt("O")
                            for k in range(8):
                                P.op("pe", _L("matmul",
                                    ps[:, :], lhsT=wb[:, k * 512 + fcl * 128:k * 512 + (fcl + 1) * 128],
                                    rhs=hT[:, k, tg * 512:(tg + 1) * 512], start=(k == 0), stop=(k == 7)),
                                    reads=[wk] + [("hT", tg * 4 + j) for j in range(4)], writes=[pk])
                            stt, sk = st_r.next()
                            P.op("act", _L("activation", out=stt[:], in_=ps[:, :], func=AF.Sigmoid), writes=[pk, sk])
                            P.dma("act", _L("dma_start", out=dst[fc][:, tg * 512:(tg + 1) * 512], in_=stt[:]),
                                  reads=[sk], writes=[("gscr", fc, tg, id(dst))])

            own = list(range(NO))
            alls = list(range(NT))
            qk_group(0, own, gq_d["a"], QT_d["a"])
            chk("p2_1")
            interleave_all(qk_group(512, alls, gk_d["a"], KT_d["a"], as_gen=True), v_group(1024, V_d["a"]))
            chk("p2_2")
            chk("p2_3")
            qi_group()
            chk("p2_4")
            kiwi_group()
            chk("p2_5")
            qk_group(2120, own, gq_d["b"], QT_d["b"])
            interleave_all(qk_group(2632, alls, gk_d["b"], KT_d["b"], as_gen=True), v_group(3144, V_d["b"]))
            gate_group(3656, g_d["a"])
            chk("p2a")
            gate_group(4680, g_d["b"])
            P.barrier()
            chk("p2")

        with contextlib.ExitStack() as ph:
            def sbp(name, shape, dt):
                return sb(name, shape, dt, st=ph)

            KTs = {"a": sbp("KTa", [128, NT, 512], BF16)}
            Vs = {"a": sbp("Va", [128, NT, 528], BF16)}
            kv_r = Rot(ph, nc, "kvb", 4, [128, 2 * 512 + 2 * 528], BF16)
            kiT = sbp("kiT", [128, NT * 128], BF16)
            sc_r = Rot(ph, nc, "sc", 2, [128, S], F32)
            oaT_r = Rot(ph, nc, "oaT", 2, [128, 512], BF16)
            MB_r = Rot(ph, nc, "MB", 3, [128, S], BF16)
            rl_r = Rot(ph, nc, "rl", 4, [128, 512], BF16)
            Dw_r = Rot(ph, nc, "Dw", 2, [128, 8 * 128], BF16)
            PT_r = Rot(ph, nc, "PT", 4, [128, 512], BF16)
            qt_r = Rot(ph, nc, "qt", 4, [128, 512], BF16)
            qi_r = Rot(ph, nc, "qit", 2, [128, 512], BF16)
            oa_r = Rot(ph, nc, "oa", 3, [128, 512], BF16)
            bs = sbp("bs", [128, 8], F32)
            Wn = sbp("Wn", [128, NB + 1], F32)
            rden = sbp("rden", [128, 8], F32)
            ksum = sbp("ksum", [128, 128], F32)
            kmf = sbp("kmf", [128, 64], F32)
            kmT = sbp("kmT", [128, 64], BF16)
            gsb = sbp("gsb", [128, 64], F32)
            m8 = sbp("m8", [128, 32], F32)
            mbk = sbp("mbk", [128, 64], F32)
            mbkb = sbp("mbkb", [128, 64], BF16)

            def load_kv(kind):
                order = [(s__ // 2) + 16 * (s__ % 2) for s__ in range(NT)]
                for s_ in order:
                    P.dma("sp", _L("dma_start", out=KTs[kind][:, s_, :], in_=KT_d[kind][s_]), writes=[("KT" + kind, s_)])
                for s_ in order:
                    P.dma("sp", _L("dma_start", out=Vs[kind][:, s_, :], in_=V_d[kind][s_]), writes=[("V" + kind, s_)])

            def slot_of(i, cb):
                return cb if cb <= i else 16 + (cb - (i + 1))

            def attention(i, qt, qk_, bias_fn, dstT):
                ncb = 2 * (i + 1)
                oa, ok = oa_r.next()
                units = []
                for h in range(4):
                    for g0 in range(0, ncb, 4):
                        units.append((h, list(range(g0, min(g0 + 4, ncb)))))
                Obank = {}
                pend_evac = []

                def emit_st(u):
                    h, cbs = u
                    ST, Sk = pB[0], "pB0"
                    for e_, cb in enumerate(cbs):
                        s_ = slot_of(i, cb)
                        bias = bias_fn(h, cb)
                        P.op("pe", _L("matmul", ST[:, e_ * 128:(e_ + 1) * 128], lhsT=KTs["a"][:, s_, h * 128:(h + 1) * 128],
                                      rhs=qt[:, h * 128:(h + 1) * 128], start=True, stop=(bias is None)),
                             reads=[("KTa", s_), qk_], writes=[Sk])
                        if bias is not None:
                            P.op("pe", _L("matmul", ST[:, e_ * 128:(e_ + 1) * 128], lhsT=bias[0], rhs=bias[1],
                                          start=False, stop=True), reads=bias[2], writes=[Sk])
                    n = len(cbs) * 128
                    pt, pk = PT_r.next()
                    P.op("act", _L("activation", out=pt[:, 0:n], in_=ST[:, 0:n], func=AF.Exp, scale=ATT_SCALE),
                         writes=[Sk, pk])
                    return pt, pk

                def emit_pv(u, ptk):
                    h, cbs = u
                    pt, pk = ptk
                    if h not in Obank:
                        Obank[h] = (pO[0], "pO0")
                    O, Ok = Obank[h]
                    for e_, cb in enumerate(cbs):
                        s_ = slot_of(i, cb)
                        P.op("pe", _L("matmul", O[:, 0:129], lhsT=pt[:, e_ * 128:(e_ + 1) * 128],
                                      rhs=Vs["a"][:, s_, h * 132:h * 132 + 129], start=(cb == 0), stop=(cb == ncb - 1)),
                             reads=[pk, ("Va", s_)], writes=[Ok])
                    if cbs[-1] == ncb - 1:
                        def evac(h=h, O=O, Ok=Ok):
                            P.op("dve", _L("reciprocal", out=rden[:, h:h + 1], in_=O[:, 128:129]), writes=[Ok, ("rden", h)])
                            P.op("dve", _L("tensor_scalar", out=oa[:, h * 128:(h + 1) * 128], in0=O[:, 0:128],
                                           scalar1=rden[:, h:h + 1], scalar2=None, op0=ALU.mult),
                                 reads=[("rden", h)], writes=[Ok, ok])
                        pend_evac.append(evac)
                        while pend_evac:
                            pend_evac.pop(0)()

                prev = None
                for u in units:
                    ptk = emit_st(u)
                    if prev is not None:
                        emit_pv(*prev)
                    prev = (u, ptk)
                    yield
                emit_pv(*prev)
                yield
                while pend_evac:
                    pend_evac.pop(0)()
                pt_, pk_ = nxt("T")
                for h in range(4):
                    P.op("pe", _L("transpose", out=pt_[:, h * 128:(h + 1) * 128], in_=oa[:, h * 128:(h + 1) * 128],
                                  identity=ident_b[:]), reads=[ok, "ident_b"], writes=[pk_])
                ost, osk = oaT_r.next()
                P.op("act", _L("copy", out=ost[:], in_=pt_[:, 0:512]), writes=[pk_, osk])
                P.dma("act", _L("dma_start", out=oT_d[dstT][i], in_=ost[:]), reads=[osk], writes=[("oT", dstT, i)])
                yield

            def interleave(*gens):
                gens = [g for g in gens if g is not None]
                while gens:
                    for g in list(gens):
                        try:
                            next(g)
                        except StopIteration:
                            gens.remove(g)

            load_kv("a")
            for s_ in range(NT):
                P.dma("sp", _L("dma_start", out=kiT[:, s_ * 128:(s_ + 1) * 128], in_=kiT_d[s_]), writes=[("ki", s_)])

            def dsa_scores(i, scres):
                L = (i + 1) * 128
                sc, sk = scres
                qit, qik = qi_r.next()
                P.dma("sp", _L("dma_start", out=qit[:], in_=qiT_d[i]), writes=[qik])
                Dw, Dk = Dw_r.next()
                for h in range(8):
                    P.op("pool", _L("tensor_scalar", out=Dw[:, h * 128:(h + 1) * 128], in0=ident_b[:],
                                   scalar1=wi[:, i, h:h + 1], scalar2=None, op0=ALU.mult),
                         reads=["ident_b", ("wi", i)], writes=[(Dk, h)])
                for seg in range(2):
                    base = seg * 2048
                    for c0 in range(0, L, 512):
                        n = min(512, L - c0)
                        kis = [("ki", seg * 16 + (c0 + j) // 128) for j in range(0, n, 128)]
                        acc, acck = nxt("C")
                        pend = []
                        for hp in range(4):
                            cur = []
                            for hf in range(2):
                                h = 2 * hp + hf
                                ps, pk = nxt("A")
                                P.op("pe", _L("matmul", ps[:, 0:n], lhsT=qit[64 * hf:64 * hf + 64, hp * 128:(hp + 1) * 128],
                                              rhs=kiT[64 * hf:64 * hf + 64, base + c0:base + c0 + n], start=True, stop=True),
                                     reads=[qik] + kis, writes=[pk])
                                cur.append((h, ps, pk))
                            for (h, ps, pk) in cur:
                                rl, rk = rl_r.next()
                                P.op("act", _L("activation", out=rl[:, 0:n], in_=ps[:, 0:n], func=AF.Relu), writes=[pk, rk])
                                pend.append((_L("matmul", acc[:, 0:n], lhsT=Dw[:, h * 128:(h + 1) * 128], rhs=rl[:, 0:n],
                                                start=(h == 0), stop=(h == 7)), [(Dk, h), rk]))
                            while len(pend) > 2:
                                e_, r_ = pend.pop(0)
                                P.op("pe", e_, reads=r_, writes=[acck])
                            if hp % 2 == 1:
                                yield
                        while pend:
                            e_, r_ = pend.pop(0)
                            P.op("pe", e_, reads=r_, writes=[acck])
                        d0 = seg * L + c0
                        P.op("act", _L("copy", out=sc[:, d0:d0 + n], in_=acc[:, 0:n]), writes=[acck, (sk, d0)])
                        yield

            def dsa_bisect(i, scres, res):
                L = (i + 1) * 128
                W2 = 2 * L
                mb, mk = res
                if i == 0:
                    P.op("pool", _L("tensor_copy", out=mb[:, 0:128], in_=caus_b[:]), reads=["caus_b"], writes=[mk])
                    P.op("pool", _L("tensor_copy", out=mb[:, 128:256], in_=pb_b[:]), reads=["pb_b"], writes=[mk])
                    return
                sc, sk = scres
                allsc = [(sk, seg * L + c0) for seg in range(2) for c0 in range(0, L, 512)]
                P.op("dve", _L("max", out=m8[:, 0:8], in_=sc[:, 0:W2]), reads=allsc, writes=["m8"])
                P.op("dve", _L("tensor_copy", out=bs[:, 0:1], in_=m8[:, 0:1]), reads=["m8"], writes=[("bs", 0)])
                P.op("dve", _L("tensor_reduce", out=bs[:, 1:2], in_=sc[:, 0:W2], axis=AX.X, op=ALU.min),
                     reads=allsc, writes=[("bs", 1)])
                yield
                P.op("dve", _L("tensor_tensor", out=bs[:, 2:3], in0=bs[:, 0:1], in1=bs[:, 1:2], op=ALU.subtract),
                     reads=[("bs", 0), ("bs", 1)], writes=["bs"])
                P.op("dve", _L("tensor_scalar", out=Wn[:], in0=pow2c[:], scalar1=bs[:, 2:3], scalar2=None, op0=ALU.mult),
                     reads=["pow2c", "bs"], writes=["Wn"])
                P.op("dve", _L("tensor_tensor", out=bs[:, 3:4], in0=bs[:, 1:2], in1=Wn[:, 0:1], op=ALU.add),
                     reads=["Wn", ("bs", 1)], writes=["bs"])
                P.op("dve", _L("tensor_tensor", out=sc[:, i * 128:(i + 1) * 128], in0=sc[:, i * 128:(i + 1) * 128],
                               in1=caus_f[:], op=ALU.add), reads=["caus_f", ("bs", 0), ("bs", 1)], writes=allsc)
                P.op("dve", _L("tensor_tensor", out=sc[:, L + i * 128:L + (i + 1) * 128],
                               in0=sc[:, L + i * 128:L + (i + 1) * 128], in1=pb_f[:], op=ALU.add),
                     reads=["pb_f"], writes=allsc)
                yield
                for n_ in range(NB):
                    P.op("dve", _L("tensor_scalar", out=mb[:, 0:W2], in0=sc[:, 0:W2], scalar1=bs[:, 3:4], scalar2=None,
                                   op0=ALU.is_ge, op1=ALU.add, accum_out=bs[:, 4:5]),
                         reads=allsc + ["bs"], writes=[mk, "bs"])
                    P.op("dve", _L("tensor_scalar", out=bs[:, 5:6], in0=bs[:, 4:5], scalar1=256.0, scalar2=-0.5,
                                   op0=ALU.is_ge, op1=ALU.add), writes=["bs"])
                    P.op("dve", _L("scalar_tensor_tensor", out=bs[:, 3:4], in0=bs[:, 5:6], scalar=Wn[:, n_:n_ + 1],
                                   in1=bs[:, 3:4], op0=ALU.mult, op1=ALU.add), reads=["Wn"], writes=["bs"])
                    yield
                P.op("dve", _L("tensor_tensor", out=bs[:, 6:7], in0=bs[:, 3:4], in1=Wn[:, NB:NB + 1], op=ALU.subtract),
                     reads=["Wn"], writes=["bs"])
                P.op("dve", _L("tensor_scalar", out=mb[:, 0:W2], in0=sc[:, 0:W2], scalar1=bs[:, 6:7], scalar2=MNEG,
                               op0=ALU.is_lt, op1=ALU.mult), reads=allsc + ["bs"], writes=[mk])
                yield

            def load_q(kind, i):
                qt, qk_ = qt_r.next()
                P.dma("sp", _L("dma_start", out=qt[:], in_=QT_d[kind][i]), writes=[qk_])
                return qt, qk_

            for s_ in range(NT):
                kv, kvk = kv_r.next()
                P.dma("sp", _L("dma_start", out=kv[:, 0:512], in_=KT_d["b"][s_]), writes=[(kvk, "k")])
                P.op("dve", _L("tensor_reduce", out=ksum[:, s_ * 4:(s_ + 1) * 4], in_=A(kv, 0, [[2080, 128], [128, 4], [1, 128]]),
                               axis=AX.X, op=ALU.add), reads=[(kvk, "k")], writes=[("ksum", s_)])
            P.op("dve", _L("tensor_tensor", out=A(kmf, 0, [[64, 128], [16, 4], [1, 16]]),
                           in0=A(ksum, 0, [[128, 128], [1, 4], [4, 16]]),
                           in1=A(ksum, 64, [[128, 128], [1, 4], [4, 16]]), op=ALU.add),
                 reads=[("ksum", s_) for s_ in range(NT)], writes=["kmf"])
            P.op("dve", _L("tensor_scalar", out=kmT[:], in0=kmf[:], scalar1=1.0 / 256, scalar2=None, op0=ALU.mult),
                 reads=["kmf"], writes=["kmT"])

            sel_r = Rot(ph, nc, "selr", 2, [128, 64], F32)
            acc_r = Rot(ph, nc, "accr", 8, [128, 132], F32)
            tmpb_r = Rot(ph, nc, "tmpb", 4, [128, 132], F32)

            def moba_prep(i, qt, qk_):
                selt, selk = sel_r.next()
                if i < 4:
                    P.op("dve", _L("memset", selt[:], 1.0), writes=[selk])
                    return selt, selk
                ps, pk = nxt("A")
                for h in range(4):
                    P.op("pe", _L("matmul", ps[:, h * 16:(h + 1) * 16], lhsT=qt[:, h * 128:(h + 1) * 128],
                                  rhs=kmT[:, h * 16:(h + 1) * 16], start=True, stop=True), reads=[qk_, "kmT"], writes=[pk])
                P.op("dve", _L("tensor_copy", out=gsb[:], in_=ps[:, 0:64]), writes=[pk, "gsb"])
                P.op("dve", _L("memset", A(gsb, i, [[64, 128], [16, 4], [1, 16 - i]]), -1e30), writes=["gsb"])
                for h in range(4):
                    P.op("dve", _L("max", out=m8[:, h * 8:(h + 1) * 8], in_=gsb[:, h * 16:(h + 1) * 16]),
                         reads=["gsb"], writes=["m8"])
                P.op("dve", _L("tensor_tensor", out=A(selt, 0, [[64, 128], [16, 4], [1, 16]]),
                               in0=A(gsb, 0, [[64, 128], [16, 4], [1, 16]]),
                               in1=A(m8, 2, [[32, 128], [8, 4], [0, 16]]), op=ALU.is_ge), reads=["gsb", "m8"], writes=[selk])
                return selt, selk

            def moba_attention(i, qt, qk_, sel):
                selt, selk = sel
                oa, ok = oa_r.next()
                blocks = [("own", i)] + [("past", n) for n in range(i)]
                accs = [acc_r.next() for _ in range(4)]
                kvs = {}

                def get_kv(bl):
                    if bl not in kvs:
                        kv, kvk = kv_r.next()
                        n = bl[1]
                        P.dma("sp", _L("dma_start", out=A(kv, 0, [[2080, 128], [512, 2], [1, 512]]),
                                       in_=A(KT_d["b"], n * 128 * 512, [[512, 128], [16 * 128 * 512, 2], [1, 512]])),
                              writes=[(kvk, "k")])
                        P.dma("sp", _L("dma_start", out=A(kv, 1024, [[2080, 128], [528, 2], [1, 528]]),
                                       in_=A(V_d["b"], n * 128 * 528, [[528, 128], [16 * 128 * 528, 2], [1, 528]])),
                              writes=[(kvk, "v")])
                        kvs[bl] = (kv, kvk)
                    return kvs[bl]

                units = [(bl, h) for bl in blocks for h in range(4)]

                def emit_st(u):
                    bl, h = u
                    kv, kvk = get_kv(bl)
                    own = (bl[0] == "own")
                    ST, Sk = pB[1], "pB1"
                    for si in range(2):
                        P.op("pe", _L("matmul", ST[:, si * 128:(si + 1) * 128],
                                      lhsT=kv[:, si * 512 + h * 128:si * 512 + (h + 1) * 128],
                                      rhs=qt[:, h * 128:(h + 1) * 128], start=True, stop=(not own)),
                             reads=[(kvk, "k"), qk_], writes=[Sk])
                        if own:
                            bt, bk_ = (caus_b, "caus_b") if si == 0 else (pb_b, "pb_b")
                            P.op("pe", _L("matmul", ST[:, si * 128:(si + 1) * 128], lhsT=bt[:], rhs=ident_b[:],
                                          start=False, stop=True), reads=[bk_, "ident_b"], writes=[Sk])
                    pt, pk = PT_r.next()
                    P.op("act", _L("activation", out=pt[:, 0:256], in_=ST[:, 0:256], func=AF.Exp, scale=ATT_SCALE),
                         writes=[Sk, pk])
                    return pt, pk

                def emit_pv(u, ptk):
                    bl, h = u
                    pt, pk = ptk
                    kv, kvk = get_kv(bl)
                    acc, ack = accs[h]
                    O, Ok = pO[1], "pO1"
                    for si in range(2):
                        P.op("pe", _L("matmul", O[:, 0:129], lhsT=pt[:, si * 128:(si + 1) * 128],
                                      rhs=kv[:, 1024 + si * 528 + h * 132:1024 + si * 528 + h * 132 + 129],
                                      start=(si == 0), stop=(si == 1)), reads=[pk, (kvk, "v")], writes=[Ok])
                    if bl[0] == "own":
                        P.op("act", _L("copy", out=acc[:, 0:129], in_=O[:, 0:129]), writes=[Ok, ack])
                    else:
                        j = h * 16 + bl[1]
                        tb, tbk = tmpb_r.next()
                        P.op("act", _L("activation", out=tb[:, 0:129], in_=O[:, 0:129], func=AF.Identity,
                                       scale=selt[:, j:j + 1]), reads=[selk], writes=[Ok, tbk])
                        P.op("pool", _L("tensor_tensor", out=acc[:, 0:129], in0=acc[:, 0:129], in1=tb[:, 0:129], op=ALU.add),
                             reads=[tbk], writes=[ack])
                    if bl == blocks[-1]:
                        P.op("dve", _L("reciprocal", out=rden[:, 4 + h:5 + h], in_=acc[:, 128:129]), reads=[ack],
                             writes=[("rdenb", h)])
                        P.op("dve", _L("tensor_scalar", out=oa[:, h * 128:(h + 1) * 128], in0=acc[:, 0:128],
                                       scalar1=rden[:, 4 + h:5 + h], scalar2=None, op0=ALU.mult),
                             reads=[("rdenb", h), ack], writes=[ok])

                prev = None
                for u in units:
                    ptk = emit_st(u)
                    if prev is not None:
                        emit_pv(*prev)
                    prev = (u, ptk)
                    yield
                emit_pv(*prev)
                pt_, pk_ = nxt("T")
                for h in range(4):
                    P.op("pe", _L("transpose", out=pt_[:, h * 128:(h + 1) * 128], in_=oa[:, h * 128:(h + 1) * 128],
                                  identity=ident_b[:]), reads=[ok, "ident_b"], writes=[pk_])
                ost, osk = oaT_r.next()
                P.op("act", _L("copy", out=ost[:], in_=pt_[:, 0:512]), writes=[pk_, osk])
                P.dma("act", _L("dma_start", out=oT_d["b"][i], in_=ost[:]), reads=[osk], writes=[("oT", "b", i)])
                yield

            scs, mbs = {}, {}
            mprep = {}
            for t in range(-2, NO):
                gens = []
                j = t + 2
                if 1 <= j < NO:
                    scs[j] = sc_r.next()
                    gens.append(dsa_scores(j, scs[j]))
                j = t + 1
                if 0 <= j < NO:
                    mbs[j] = MB_r.next()
                    gens.append(dsa_bisect(j, scs.get(j), mbs[j]))
                    qb = load_q("b", j)
                    mprep[j] = (qb, moba_prep(j, *qb))
                if t >= 0:
                    qt, qk_ = load_q("a", t)
                    mb, mk = mbs[t]
                    gens.append(attention(t, qt, qk_,
                                          lambda h, cb, mb=mb, mk=mk: (mb[:, cb * 128:(cb + 1) * 128], ident_b[:], [mk, "ident_b"]),
                                          "a"))
                    qb, pb_ = mprep[t]
                    gens.append(moba_attention(t, qb[0], qb[1], pb_))
                interleave(*gens)
            P.barrier()
            chk("p5")

        if True:
            h2T = sb("h2T", [128, 8, NO * 128], BF16)
            with contextlib.ExitStack() as p6:
                def sb6(name, shape, dt):
                    return sb(name, shape, dt, st=p6)
                wstg = Rot(p6, nc, "wstg6", 2, [128, 1024], F32)
                wbr = {"a": sb6("wbra", [128, 4, D], BF16), "b": sb6("wbrb", [128, 4, D], BF16)}
                wout = sb6("wout", [128, 8, D], BF16)
                GTm = sb6("GTm", [128, D], F32)
                SHf = sb6("SHf", [128, D], F32)
                G1f = sb6("G1f", [128, D], F32)
                gt_r = {"a": Rot(p6, nc, "gta", 1, [128, 8 * 512], BF16), "b": Rot(p6, nc, "gtb", 1, [128, 8 * 512], BF16)}
                os_r = {"a": Rot(p6, nc, "osa", 2, [128, 4 * 512], BF16), "b": Rot(p6, nc, "osb", 2, [128, 4 * 512], BF16)}
                m1_r = Rot(p6, nc, "m1", 2, [128, 512], F32)
                m2_r = Rot(p6, nc, "m2", 2, [128, 512], F32)
                mTall = sb6("mTall", [128, 8 * NO * 128], BF16)
                xbuf = Rot(p6, nc, "xb6", 3, [128, D], F32)
                yt_r = Rot(p6, nc, "yt", 2, [128, D], F32)
                x1_r = Rot(p6, nc, "x1t", 3, [128, D], F32)
                tmpf = Rot(p6, nc, "tmpf6", 2, [128, D], F32)
                hb = Rot(p6, nc, "hb6", 3, [128, D], BF16)
                junk = sb6("junk6", [128, D], BF16)

                def load_rows(dst3, dkey, src_d, r0, nrows_chunks):
                    for c in range(nrows_chunks):
                        stg, sk = wstg.next()
                        P.dma("sp", _L("dma_start", out=stg[:], in_=src_d[r0 + c * 128:r0 + (c + 1) * 128, :]),
                              writes=[sk])
                        P.op("pool", _L("tensor_copy", out=dst3[:, c, :], in_=stg[:]), reads=[sk],
                             writes=[(dkey, c)])

                load_rows(wbr["a"], "wbra", wbr_d["a"], 0, 4)
                load_rows(wbr["b"], "wbrb", wbr_d["b"], 0, 4)
                load_rows(wout, "wout", wout_d, 0, 8)
                P.dma("sp", _L("dma_start", out=GTm[:], in_=modd[0]), reads=[("modd", 0)], writes=["GTm"])
                P.dma("sp", _L("dma_start", out=SHf[:], in_=modd[1]), reads=[("modd", 1)], writes=["SHf"])
                P.dma("sp", _L("dma_start", out=G1f[:], in_=modd[2]), reads=[("modd", 2)], writes=["G1f"])

                def norm_mod_T6(xt, xk, col0):
                    P.op("act", _L("activation", out=junk[:], in_=xt[:], func=AF.Square, accum_out=ssm[:, 0:1]),
                         reads=[xk], writes=["junk6", "ssm"])
                    P.op("dve", _L("tensor_scalar", out=ssm[:, 1:2], in0=ssm[:, 0:1], scalar1=1.0 / D, scalar2=EPS,
                                                          op0=ALU.mult, op1=ALU.add), writes=["ssm"])
                    P.op("act", _L("activation", out=ssm[:, 2:3], in_=ssm[:, 1:2], func=AF.Ln), writes=["ssm"])
                    P.op("act", _L("activation", out=ssm[:, 3:4], in_=ssm[:, 2:3], func=AF.Exp, scale=-0.5), writes=["ssm"])
                    tf, tk = tmpf.next()
                    P.op("dve", _L("scalar_tensor_tensor", out=tf[:], in0=xt[:], scalar=ssm[:, 3:4], in1=G1f[:],
                                                                 op0=ALU.mult, op1=ALU.mult), reads=[xk, "ssm", "G1f"], writes=[tk])
                    hbt, hk = hb.next()
                    P.op("dve", _L("tensor_tensor", out=hbt[:], in0=tf[:], in1=SHf[:], op=ALU.add),
                         reads=[tk, "SHf"], writes=[hk])
                    pt, pk = nxt("T")
                    for k in range(8):
                        P.op("pe", _L("transpose", out=pt[:, k * 128:(k + 1) * 128], in_=hbt[:, k * 128:(k + 1) * 128],
                                                              identity=ident_b[:]), reads=[hk, "ident_b"], writes=[pk])
                    P.op("act", _L("copy", out=A(h2T, col0, [[8 * NO * 128, 128], [NO * 128, 8], [1, 128]]),
                                                 in_=A(pt, 0, [[1024, 128], [128, 8], [1, 128]])),
                         writes=[pk, ("h2T", col0 // 128)])

                for tg in range(4):
                    gts = {}
                    for kind in ("a", "b"):
                        gt, gk_ = gt_r[kind].next()
                        for fc in range(8):
                            P.dma("sp", _L("dma_start",
                                out=gt[:, fc * 512:(fc + 1) * 512], in_=g_d[kind][fc][:, tg * 512:(tg + 1) * 512]),
                                writes=[(gk_, fc)])
                        gts[kind] = (gt, gk_)
                    oss = {}
                    for kind in ("a", "b"):
                        ot_, otk_ = os_r[kind].next()
                        for tl in range(4):
                            P.dma("sp", _L("dma_start", out=A(ot_, tl * 128, [[2048, 128], [512, 4], [1, 128]]),
                                           in_=A(oT_d[kind], (tg * 4 + tl) * 128 * 512, [[512, 128], [128, 4], [1, 128]])),
                                  reads=[("oT", kind, tg * 4 + tl)], writes=[(otk_, tl)])
                        oss[kind] = (ot_, otk_)
                    mT, mTk = mTall, ("mT", tg)
                    for fc in range(8):
                        pss = {}
                        for kind in ("a", "b"):
                            ps, pk = nxt("O")
                            for h in range(4):
                                P.op("pe", _L("matmul",
                                    ps[:, :], lhsT=wbr[kind][:, h, fc * 128:(fc + 1) * 128],
                                    rhs=oss[kind][0][:, h * 512:(h + 1) * 512], start=(h == 0), stop=(h == 3)),
                                    reads=[("wbr" + kind, h)] + [(oss[kind][1], j) for j in range(4)], writes=[pk])
                            pss[kind] = (ps, pk)
                        m1, k1 = m1_r.next()
                        m2, k2 = m2_r.next()
                        P.op("dve", _L("tensor_tensor", out=m1[:], in0=pss["a"][0][:, :],
                                                                           in1=gts["a"][0][:, fc * 512:(fc + 1) * 512], op=ALU.mult),
                             reads=[(gts["a"][1], fc)], writes=[pss["a"][1], k1])
                        P.op("dve", _L("tensor_tensor", out=m2[:], in0=pss["b"][0][:, :],
                                                                           in1=gts["b"][0][:, fc * 512:(fc + 1) * 512], op=ALU.mult),
                             reads=[(gts["b"][1], fc)], writes=[pss["b"][1], k2])
                        P.op("pool", _L("tensor_tensor",
                            out=mT[:, fc * 2048 + tg * 512:fc * 2048 + (tg + 1) * 512], in0=m1[:], in1=m2[:], op=ALU.add), reads=[k1, k2], writes=[(mTk, fc)])
                ssm_r6 = Rot(p6, nc, "ssm6", 3, [128, 8], F32)

                def t_s0(tile, c):
                    tg = tile // 4
                    xt, xk = xbuf.next()
                    P.dma("sp", _L("dma_start", out=xt[:], in_=x_d[tile * 128:(tile + 1) * 128, :]), writes=[xk])
                    yt, yk = yt_r.next()
                    for hf in range(2):
                        ps, pk = nxt("A")
                        for k in range(8):
                            P.op("pe", _L("matmul", ps[:, :], lhsT=mTall[:, k * 2048 + tile * 128:k * 2048 + (tile + 1) * 128],
                                          rhs=wout[:, k, hf * 512:(hf + 1) * 512], start=(k == 0), stop=(k == 7)),
                                 reads=[(("mT", tg), k), ("wout", k)], writes=[pk])
                        P.op("dve", _L("tensor_tensor", out=yt[:, hf * 512:(hf + 1) * 512], in0=ps[:, :],
                                       in1=GTm[:, hf * 512:(hf + 1) * 512], op=ALU.mult), reads=["GTm"], writes=[pk, (yk, hf)])
                    c["x"], c["y"] = (xt, xk), (yt, yk)

                def t_s1(tile, c):
                    (xt, xk), (yt, yk) = c["x"], c["y"]
                    x1t, x1k = x1_r.next()
                    P.op("pool", _L("tensor_tensor", out=x1t[:], in0=yt[:], in1=xt[:], op=ALU.add),
                         reads=[(yk, 0), (yk, 1), xk], writes=[x1k])
                    P.dma("sp", _L("dma_start", out=x1_d[tile * 128:(tile + 1) * 128, :], in_=x1t[:]),
                          reads=[x1k], writes=[("x1d", tile)])
                    sm, smk = ssm_r6.next()
                    P.op("act", _L("activation", out=junk[:], in_=x1t[:], func=AF.Square, accum_out=sm[:, 0:1]),
                         reads=[x1k], writes=["junk6", smk])
                    P.op("dve", _L("tensor_scalar", out=sm[:, 1:2], in0=sm[:, 0:1], scalar1=1.0 / D, scalar2=EPS,
                                   op0=ALU.mult, op1=ALU.add), writes=[smk])
                    P.op("act", _L("activation", out=sm[:, 2:3], in_=sm[:, 1:2], func=AF.Ln), writes=[smk])
                    P.op("act", _L("activation", out=sm[:, 3:4], in_=sm[:, 2:3], func=AF.Exp, scale=-0.5), writes=[smk])
                    c["x1"], c["sm"] = (x1t, x1k), (sm, smk)

                def t_s2(tile, c):
                    (x1t, x1k), (sm, smk) = c["x1"], c["sm"]
                    tf, tk = tmpf.next()
                    P.op("dve", _L("scalar_tensor_tensor", out=tf[:], in0=x1t[:], scalar=sm[:, 3:4], in1=G1f[:],
                                   op0=ALU.mult, op1=ALU.mult), reads=[x1k, smk, "G1f"], writes=[tk])
                    hbt, hk = hb.next()
                    P.op("dve", _L("tensor_tensor", out=hbt[:], in0=tf[:], in1=SHf[:], op=ALU.add), reads=[tk, "SHf"], writes=[hk])
                    c["hb"] = (hbt, hk)

                def t_s3(tile, c):
                    hbt, hk = c["hb"]
                    pt, pk = nxt("T")
                    for k in range(8):
                        P.op("pe", _L("transpose", out=pt[:, k * 128:(k + 1) * 128], in_=hbt[:, k * 128:(k + 1) * 128],
                                      identity=ident_b[:]), reads=[hk, "ident_b"], writes=[pk])
                    P.op("act", _L("copy", out=A(h2T, tile * 128, [[8 * NO * 128, 128], [NO * 128, 8], [1, 128]]),
                                   in_=A(pt, 0, [[1024, 128], [128, 8], [1, 128]])), writes=[pk, ("h2T", tile)])

                run_pipeline(list(range(NO)), [t_s0, t_s1, t_s2, t_s3])
                P.barrier()
                chk("p6")

            aT = sb("aT", [128, 22, NO * 128], BF16)
            wd = sb("wd", [128, 22, D], BF16)
            wstg8 = Rot(top, nc, "wstg8", 2, [128, 1024], F32)

            def load_wd(c):
                stg, sk = wstg8.next()
                P.dma("sp", _L("dma_start", out=stg[:], in_=wdn_d[c * 128:(c + 1) * 128, :]), writes=[sk])
                P.op("pool", _L("tensor_copy", out=wd[:, c, :], in_=stg[:]), reads=[sk], writes=[("wd", c)])

            with contextlib.ExitStack() as p7:
                wstg = Rot(p7, nc, "wstg7", 2, [128, 8 * 256], F32)
                wgu = Rot(p7, nc, "wgu", 2, [128, 8 * 256], BF16)
                sg_r = Rot(p7, nc, "sg", 2, [128, 512], F32)
                gu_src = wgu_d.ap().rearrange("(k p) n -> p k n", p=128)
                for c in range(22):
                    stg, sk = wstg.next()
                    wb, wk = wgu.next()
                    P.dma("sp", _L("dma_start", out=A(stg, 0, [[2048, 128], [256, 8], [1, 128]]),
                                                                   in_=gu_src[:, :, c * 128:(c + 1) * 128]), writes=[(sk, 0)])
                    P.dma("sp", _L("dma_start", out=A(stg, 128, [[2048, 128], [256, 8], [1, 128]]),
                                                                   in_=gu_src[:, :, DFF + c * 128:DFF + (c + 1) * 128]), writes=[(sk, 1)])
                    P.op("pool", _L("tensor_copy", out=wb[:], in_=stg[:]), reads=[(sk, 0), (sk, 1)], writes=[wk])
                    for tg in range(4):
                        pg, pgk = nxt("A")
                        pu, puk = nxt("O")
                        hk = [("h2T", tg * 4 + j) for j in range(4)]
                        for k in range(8):
                            P.op("pe", _L("matmul",
                                pg[:, :], lhsT=wb[:, k * 256:k * 256 + 128], rhs=h2T[:, k, tg * 512:(tg + 1) * 512],
                                start=(k == 0), stop=(k == 7)), reads=[wk] + hk, writes=[pgk])
                        for k in range(8):
                            P.op("pe", _L("matmul",
                                pu[:, :], lhsT=wb[:, k * 256 + 128:k * 256 + 256], rhs=h2T[:, k, tg * 512:(tg + 1) * 512],
                                start=(k == 0), stop=(k == 7)), reads=[wk] + hk, writes=[puk])
                        sg, sgk = sg_r.next()
                        P.op("act", _L("activation", out=sg[:], in_=pg[:, :], func=AF.Silu), writes=[pgk, sgk])
                        P.op("dve", _L("tensor_tensor",
                            out=aT[:, c, tg * 512:(tg + 1) * 512], in0=pu[:, :], in1=sg[:], op=ALU.mult),
                            reads=[sgk], writes=[puk, ("aT", c, tg)])
                    load_wd(c)
                P.barrier()
                chk("p7")

            with contextlib.ExitStack() as p8:
                GTf = sb("GTf", [128, D], F32, st=p8)
                xbuf = Rot(p8, nc, "xb8", 2, [128, D], F32)
                yt_r = Rot(p8, nc, "yt8", 1, [128, D], F32)
                ot_r = Rot(p8, nc, "ot8", 2, [128, D], F32)
                P.dma("sp", _L("dma_start", out=GTf[:], in_=modd[3]), reads=[("modd", 3)], writes=["GTf"])
                for tile in range(NO):
                    xt, xk = xbuf.next()
                    P.dma("sp", _L("dma_start", out=xt[:], in_=x1_d[tile * 128:(tile + 1) * 128, :]),
                          reads=[("x1d", tile)], writes=[xk])
                    yt, yk = yt_r.next()
                    for hf in range(2):
                        ps, pk = nxt("A")
                        for c in range(22):
                            P.op("pe", _L("matmul",
                                ps[:, :], lhsT=aT[:, c, tile * 128:(tile + 1) * 128], rhs=wd[:, c, hf * 512:(hf + 1) * 512],
                                start=(c == 0), stop=(c == 21)), reads=[("aT", c, tile // 4), ("wd", c)], writes=[pk])
                        P.op("dve", _L("tensor_tensor",
                            out=yt[:, hf * 512:(hf + 1) * 512], in0=ps[:, :], in1=GTf[:, hf * 512:(hf + 1) * 512], op=ALU.mult),
                            reads=["GTf"], writes=[pk, (yk, hf)])
                    ot, otk = ot_r.next()
                    P.op("pool", _L("tensor_tensor", out=ot[:], in0=yt[:], in1=xt[:], op=ALU.add),
                         reads=[(yk, 0), (yk, 1), xk], writes=[otk])
                    P.dma("sp", _L("dma_start", out=out_d[tile * 128:(tile + 1) * 128, :], in_=ot[:]),
                          reads=[otk], writes=[("outd", tile)])
                P.barrier()
        P.finalize()


def _rope_table(pos, dim):
    inv = (np.float32(10000.0) ** (-np.arange(0, dim, 2, dtype=np.float32) / np.float32(dim))).astype(np.float32)
    ang = pos.astype(np.float32)[:, None] * inv[None, :]
    c = np.cos(ang).astype(np.float32)
    s = np.sin(ang).astype(np.float32)
    return np.concatenate([c, c, -s, s], axis=1).astype(np.float32)


def kernel(x, c, w_mod, b_mod, g_mix_norm, g_ffn_norm, w_in, g_q_dsa, g_k_dsa, g_q_moba, g_k_moba,
           w_br_dsa, w_br_moba, w_out, w_gate_up, w_down, _ret_maps=False):
    f = lambda a: np.ascontiguousarray(np.asarray(a, dtype=np.float32))
    x = f(x); c = f(c)
    shared = {
        "w_mod": f(w_mod[0]), "b_mod": f(b_mod[0]).reshape(1, -1), "g_mix": f(g_mix_norm[0]).reshape(1, -1),
        "g_ffn": f(g_ffn_norm[0]).reshape(1, -1), "w_in": f(w_in[0]),
        "g_q_dsa": f(g_q_dsa[0]).reshape(1, -1), "g_k_dsa": f(g_k_dsa[0]).reshape(1, -1),
        "g_q_moba": f(g_q_moba[0]).reshape(1, -1), "g_k_moba": f(g_k_moba[0]).reshape(1, -1),
        "w_br_dsa": f(w_br_dsa[0]), "w_br_moba": f(w_br_moba[0]), "w_out": f(w_out[0]),
        "w_gate_up": f(w_gate_up[0]), "w_down": f(w_down[0]),
    }
    in_maps = []
    perms = []
    for core in range(8):
        b, p = core // 2, core % 2
        tiles = [2 * i + p for i in range(NO)] + [2 * i + 1 - p for i in range(NO)]
        pos = np.concatenate([np.arange(t * 128, (t + 1) * 128) for t in tiles])
        perms.append(pos)
        m = dict(shared)
        m["x"] = np.ascontiguousarray(x[b][pos])
        m["cT"] = np.ascontiguousarray(c[b].reshape(8, 128).T)
        m["ropeH"] = _rope_table(pos, 128)
        m["ropeI"] = _rope_table(pos, 64)
        m["pb"] = np.full((128, 128), 0.0 if p == 1 else -1e30, dtype=np.float32)
        in_maps.append(m)
    if _ret_maps:
        return in_maps, perms
    nc = build()
    res = run_bass_kernel_spmd(nc, in_maps, core_ids=list(range(8)))
    out = np.empty((4, S, D), dtype=np.float32)
    for core in range(8):
        b = core // 2
        out[b, perms[core][:NO * 128]] = res.results[core]["out"]
    return out
```

```python
import contextlib
import numpy as np
import concourse.bass as bass
import concourse.mybir as mybir
from concourse.bass_utils import run_bass_kernel_spmd

F32 = mybir.dt.float32
BF16 = mybir.dt.bfloat16
ALU = mybir.AluOpType
AF = mybir.ActivationFunctionType
AX = mybir.AxisListType

ENGS = ["pe", "act", "dve", "pool", "sp"]
DMA_SLOTS = {"sp": 24, "pool": 8, "act": 8}

D = 1024
S = 4096
NT = 32
NO = 16
DFF = 2816
IN_W = 5704
EPS = 1e-6
NB = 16
MNEG = -30000.0
ATT_SCALE = 128 ** -0.5
IDX_W_SCALE = (8 ** -0.5) * (64 ** -0.5)


class Prog:
    def __init__(self, nc):
        self.nc = nc
        self.ins = {e: [] for e in ENGS}
        self.last_w = {}
        self.readers = {}
        self.known = {e: {} for e in ENGS}
        self.dma_count = {q: 0 for q in DMA_SLOTS}
        self.dma_last = {}
        self.last_real = {}

    def _deps_for(self, reads, writes):
        deps = []
        for k in reads:
            w = self.last_w.get(k)
            if w is not None:
                deps.append(w)
        for k in writes:
            w = self.last_w.get(k)
            if w is not None:
                deps.append(w)
            deps.extend(self.readers.get(k, ()))
        return deps

    def _commit(self, me, reads, writes):
        for k in reads:
            self.readers.setdefault(k, []).append(me)
        for k in writes:
            self.last_w[k] = me
            self.readers[k] = []

    def _waits(self, eng, deps):
        kn = self.known[eng]
        best = {}
        for d in deps:
            if d[0] == "dma":
                key = d[:3]
                val = d[3]
            else:
                if d[0] == eng and eng == "pe":
                    continue
                key = d[0]
                val = d[1]
            if kn.get(key, -1) >= val:
                continue
            if best.get(key, -1) < val:
                best[key] = val
        out = []
        for key, val in best.items():
            kn[key] = val
            out.append((key, val))
        return out

    def op(self, eng, emit, reads=(), writes=()):
        seq = len(self.ins[eng])
        waits = self._waits(eng, self._deps_for(reads, writes))
        self.ins[eng].append(dict(emit=emit, waits=waits, signal=False, dma=None))
        self.last_real[eng] = seq
        self._commit((eng, seq), reads, writes)

    def dma(self, q, emit, reads=(), writes=()):
        n = self.dma_count[q]
        self.dma_count[q] = n + 1
        slot = n % DMA_SLOTS[q]
        gen = n // DMA_SLOTS[q]
        deps = self._deps_for(reads, writes)
        if gen > 0:
            deps.append(("dma", q, slot, gen - 1))
        waits = self._waits(q, deps)
        self.ins[q].append(dict(emit=emit, waits=waits, signal=False, dma=(q, slot, gen)))
        self.dma_last[(q, slot)] = gen
        self._commit(("dma", q, slot, gen), reads, writes)

    def barrier(self):
        deps = [(e, s) for e, s in self.last_real.items()]
        deps += [("dma", q, slot, gen) for (q, slot), gen in self.dma_last.items()]
        for e in ENGS:
            waits = self._waits(e, deps)
            if waits:
                self.ins[e].append(dict(emit=None, waits=waits, signal=False, dma=None))
        self.last_w = {}
        self.readers = {}

    def finalize(self):
        nc = self.nc
        self.barrier()
        for e in ENGS:
            for rec in self.ins[e]:
                for key, val in rec["waits"]:
                    if isinstance(key, str):
                        self.ins[key][val]["signal"] = True
        for e in ENGS:
            c = 0
            for rec in self.ins[e]:
                if rec["signal"]:
                    c += 1
                rec["cnt"] = c
        with contextlib.ExitStack() as st:
            esem = {e: st.enter_context(nc.semaphore("s_" + e)) for e in ENGS}
            dsem = {}
            for q, n in DMA_SLOTS.items():
                for s in range(n):
                    dsem[(q, s)] = st.enter_context(nc.semaphore(f"d_{q}_{s}"))
            block = st.enter_context(nc.Block())

            def run(ename):
                def f(eng):
                    for rec in self.ins[ename]:
                        for key, val in rec["waits"]:
                            if isinstance(key, str):
                                eng.wait_ge(esem[key], self.ins[key][val]["cnt"])
                            else:
                                _, q, slot = key
                                eng.wait_ge(dsem[(q, slot)], 16 * (val + 1))
                        if rec["emit"] is None:
                            continue
                        bi = rec["emit"](eng)
                        if rec["dma"] is not None:
                            q, slot, gen = rec["dma"]
                            bi.then_inc(dsem[(q, slot)], 16)
                        elif rec["signal"]:
                            bi.then_inc(esem[ename], 1)
                return f

            block.tensor(run("pe"))
            block.scalar(run("act"))
            block.vector(run("dve"))
            block.gpsimd(run("pool"))
            block.sync(run("sp"))


def _L(name, *args, **kwargs):
    def f(eng):
        return getattr(eng, name)(*args, **kwargs)
    return f


def run_pipeline(items, stages):
    n, ns = len(items), len(stages)
    ctxs = [dict() for _ in items]
    for step in range(n + ns - 1):
        for si, fn in enumerate(stages):
            j = step - si
            if 0 <= j < n:
                fn(items[j], ctxs[j])


def pipeline_gen(items, stages):
    n, ns = len(items), len(stages)
    ctxs = [dict() for _ in items]
    for step in range(n + ns - 1):
        for si, fn in enumerate(stages):
            j = step - si
            if 0 <= j < n:
                fn(items[j], ctxs[j])
        yield


def interleave_all(*gens):
    gens = [g for g in gens if g is not None]
    while gens:
        for g in list(gens):
            try:
                next(g)
            except StopIteration:
                gens.remove(g)


class Rot:
    def __init__(self, st, nc, name, n, shape, dt):
        self.t = [st.enter_context(nc.sbuf_tensor(f"{name}{i}", shape, dt)) for i in range(n)]
        self.k = [f"{name}{i}" for i in range(n)]
        self.i = 0

    def next(self):
        j = self.i % len(self.t)
        self.i += 1
        return self.t[j], self.k[j]


STOP = None
DEBUG = False


class _Stop(Exception):
    pass


def build():
    nc = bass.Bass("TRN2", target_bir_lowering=False)
    P = Prog(nc)
    try:
        _build(nc, P)
    except _Stop:
        pass
    return nc


def _build(nc, P):

    def chk(name):
        if STOP == name:
            P.finalize()
            raise _Stop()

    def din(name, shape):
        return nc.dram_tensor(name, shape, F32, kind="ExternalInput")

    x_d = din("x", [S, D])
    cT_d = din("cT", [128, 8])
    wmod_d = din("w_mod", [D, 6 * D])
    bmod_d = din("b_mod", [1, 6 * D])
    gmix_d = din("g_mix", [1, D])
    gffn_d = din("g_ffn", [1, D])
    win_d = din("w_in", [D, IN_W])
    gq_d = {"a": din("g_q_dsa", [1, 128]), "b": din("g_q_moba", [1, 128])}
    gk_d = {"a": din("g_k_dsa", [1, 128]), "b": din("g_k_moba", [1, 128])}
    wbr_d = {"a": din("w_br_dsa", [512, D]), "b": din("w_br_moba", [512, D])}
    wout_d = din("w_out", [D, D])
    wgu_d = din("w_gate_up", [D, 2 * DFF])
    wdn_d = din("w_down", [DFF, D])
    ropeH_d = din("ropeH", [S, 256])
    ropeI_d = din("ropeI", [S, 128])
    pb_d = din("pb", [128, 128])
    out_d = nc.dram_tensor("out", [NO * 128, D], F32, kind="ExternalOutput")

    def scr(name, shape, dt):
        return nc.dram_tensor(name, shape, dt, kind=("ExternalOutput" if DEBUG else "Internal"))

    KT_d = {"a": scr("KaT_d", [NT, 128, 512], BF16), "b": scr("KbT_d", [NT, 128, 512], BF16)}
    V_d = {"a": scr("Va_d", [NT, 128, 528], BF16), "b": scr("Vb_d", [NT, 128, 528], BF16)}
    QT_d = {"a": scr("QaT_d", [NO, 128, 512], BF16), "b": scr("QbT_d", [NO, 128, 512], BF16)}
    qiT_d = scr("qiT_d", [NO, 128, 512], BF16)
    kiT_d = scr("kiT_d", [NT, 128, 128], BF16)
    g_d = {"a": scr("gA_d", [8, 128, NO * 128], BF16), "b": scr("gB_d", [8, 128, NO * 128], BF16)}
    oT_d = {"a": scr("oaT_d", [NO, 128, 512], BF16), "b": scr("obT_d", [NO, 128, 512], BF16)}
    modd = scr("modd", [4, 128, D], F32)
    x1_d = scr("x1_d", [NO * 128, D], F32)

    def A(t, off, pat):
        return bass.AP(t, off, [list(p) for p in pat])

    with contextlib.ExitStack() as top:
        def sb(name, shape, dt, st=top):
            return st.enter_context(nc.sbuf_tensor(name, shape, dt))

        def psum(name, shape, dt):
            return top.enter_context(nc.psum_tensor(name, shape, dt))

        pA = [psum(f"pA{i}", [128, 512], F32) for i in range(2)]
        pB = [psum(f"pB{i}", [128, 512], F32) for i in range(2)]
        pO = [psum(f"pO{i}", [128, 512], F32) for i in range(2)]
        pT = [psum(f"pT{i}", [128, 1024], BF16) for i in range(1)]
        pC = [psum(f"pC{i}", [128, 512], F32) for i in range(1)]
        pcnt = {"A": 0, "B": 0, "O": 0, "T": 0, "C": 0}

        def nxt(which):
            arr = {"A": pA, "B": pB, "O": pO, "T": pT, "C": pC}[which]
            j = pcnt[which] % len(arr)
            pcnt[which] += 1
            return arr[j], f"p{which}{j}"

        ident_f = sb("ident_f", [128, 128], F32)
        ident_b = sb("ident_b", [128, 128], BF16)
        caus_f = sb("caus_f", [128, 128], F32)
        caus_b = sb("caus_b", [128, 128], BF16)
        pb_f = sb("pb_f", [128, 128], F32)
        pb_b = sb("pb_b", [128, 128], BF16)
        pow2c = sb("pow2c", [128, NB + 1], F32)
        wi = sb("wi", [128, NO, 8], F32)
        ssm = sb("ssm", [128, 8], F32)
        eps_t = sb("eps_t", [128, 1], F32)

        P.op("pool", _L("memset", ident_f[:], 1.0), writes=["ident_f"])
        P.op("pool", _L("affine_select", out=ident_f[:], in_=ident_f[:], pattern=[[-1, 128]],
                                                compare_op=ALU.is_equal, fill=0.0, base=0, channel_multiplier=1),
             writes=["ident_f"])
        P.op("pool", _L("tensor_copy", out=ident_b[:], in_=ident_f[:]), reads=["ident_f"], writes=["ident_b"])
        P.op("pool", _L("memset", caus_f[:], 0.0), writes=["caus_f"])
        P.op("pool", _L("affine_select", out=caus_f[:], in_=caus_f[:], pattern=[[-1, 128]],
                                                compare_op=ALU.is_ge, fill=-1e30, base=0, channel_multiplier=1),
             writes=["caus_f"])
        P.op("pool", _L("tensor_scalar", out=caus_b[:], in0=caus_f[:], scalar1=MNEG, scalar2=None, op0=ALU.max),
             reads=["caus_f"], writes=["caus_b"])
        P.dma("sp", _L("dma_start", out=pb_f[:], in_=pb_d.ap()), writes=["pb_f"])
        P.op("pool", _L("tensor_scalar", out=pb_b[:], in0=pb_f[:], scalar1=MNEG, scalar2=None, op0=ALU.max),
             reads=["pb_f"], writes=["pb_b"])
        P.op("pool", _L("memset", eps_t[:], EPS), writes=["eps_t"])
        for n in range(NB + 1):
            P.op("pool", _L("memset", pow2c[:, n:n + 1], 2.0 ** -(n + 1)), writes=["pow2c"])

        with contextlib.ExitStack() as ph:
            def sbp(name, shape, dt):
                return sb(name, shape, dt, st=ph)

            G1m = sbp("G1m", [128, D], F32)
            SHm = sbp("SHm", [128, D], F32)

            hT = sbp("hT", [128, 8, S], BF16)
            xbuf = Rot(ph, nc, "xbuf", 3, [128, D], F32)
            tmpf = Rot(ph, nc, "tmpf", 2, [128, D], F32)
            hb = Rot(ph, nc, "hb", 3, [128, D], BF16)
            junk = sbp("junkb", [128, D], BF16)
            mkeys = ["SHm0", "SHm1", "G1m0", "G1m1"]
            ssm_r = Rot(ph, nc, "ssmr", 3, [128, 8], F32)

            def p1_s0(t, c):
                xt, xk = xbuf.next()
                P.dma("sp", _L("dma_start", out=xt[:], in_=x_d[t * 128:(t + 1) * 128, :]), writes=[xk])
                sm, smk = ssm_r.next()
                c["x"], c["sm"] = (xt, xk), (sm, smk)
                P.op("act", _L("activation", out=junk[:], in_=xt[:], func=AF.Square, accum_out=sm[:, 0:1]),
                     reads=[xk], writes=["junkb", smk])
                P.op("dve", _L("tensor_scalar", out=sm[:, 1:2], in0=sm[:, 0:1], scalar1=1.0 / D, scalar2=EPS,
                               op0=ALU.mult, op1=ALU.add), writes=[smk])
                P.op("act", _L("activation", out=sm[:, 2:3], in_=sm[:, 1:2], func=AF.Ln), writes=[smk])
                P.op("act", _L("activation", out=sm[:, 3:4], in_=sm[:, 2:3], func=AF.Exp, scale=-0.5), writes=[smk])

            def p1_s1(t, c):
                (xt, xk), (sm, smk) = c["x"], c["sm"]
                tf, tk = tmpf.next()
                P.op("dve", _L("scalar_tensor_tensor", out=tf[:], in0=xt[:], scalar=sm[:, 3:4], in1=G1m[:],
                               op0=ALU.mult, op1=ALU.mult), reads=[xk, smk] + mkeys, writes=[tk])
                hbt, hk = hb.next()
                P.op("dve", _L("tensor_tensor", out=hbt[:], in0=tf[:], in1=SHm[:], op=ALU.add),
                     reads=[tk] + mkeys, writes=[hk])
                c["hb"] = (hbt, hk)

            def p1_s2(t, c):
                hbt, hk = c["hb"]
                pt, pk = nxt("T")
                for k in range(8):
                    P.op("pe", _L("transpose", out=pt[:, k * 128:(k + 1) * 128], in_=hbt[:, k * 128:(k + 1) * 128],
                                  identity=ident_b[:]), reads=[hk, "ident_b"], writes=[pk])
                P.op("act", _L("copy", out=A(hT, t * 128, [[8 * S, 128], [S, 8], [1, 128]]),
                               in_=A(pt, 0, [[1024, 128], [128, 8], [1, 128]])), writes=[pk, ("hT", t)])

            with contextlib.ExitStack() as p0:
                wstg = Rot(p0, nc, "wstg0", 2, [128, 8 * 512], F32)
                sc_t = sb("sc_t", [128, 8], F32, st=p0)
                scb = sb("scb", [128, 8 * 128], F32, st=p0)
                bm = sb("bm", [128, 6 * D], F32, st=p0)
                gmx = sb("gmx", [128, D], F32, st=p0)
                gff = sb("gff", [128, D], F32, st=p0)
                mt = [sb(f"mt{i}", [128, D], F32, st=p0) for i in range(4)]
                P.dma("sp", _L("dma_start", out=sc_t[:], in_=cT_d.ap()), writes=["sc_t"])
                P.dma("sp", _L("dma_start", out=bm[:], in_=A(bmod_d, 0, [[0, 128], [1, 6 * D]])), writes=["bm"])
                P.dma("sp", _L("dma_start", out=gmx[:], in_=A(gmix_d, 0, [[0, 128], [1, D]])), writes=["gmx"])
                P.dma("sp", _L("dma_start", out=gff[:], in_=A(gffn_d, 0, [[0, 128], [1, D]])), writes=["gff"])
                P.op("act", _L("activation", out=sc_t[:], in_=sc_t[:], func=AF.Silu), reads=["sc_t"], writes=["sc_t"])
                P.op("dve", _L("tensor_copy", out=A(scb, 0, [[1024, 128], [128, 8], [1, 128]]),
                                                    in_=A(sc_t, 0, [[8, 128], [1, 8], [0, 128]])),
                     reads=["sc_t"], writes=["scb"])
                wm_src = wmod_d.ap().rearrange("(k p) n -> p k n", p=128)
                dest = {0: (SHm, "SHm", None), 1: (G1m, "G1m", gmx), 2: (mt[0], "mt0", None),
                        3: (mt[1], "mt1", None), 4: (mt[2], "mt2", gff), 5: (mt[3], "mt3", None)}
                def p0_chunks():
                  for nch in range(12):
                    stg, sk = wstg.next()
                    sv = A(stg, 0, [[4096, 128], [512, 8], [1, 512]])
                    P.dma("sp", _L("dma_start", out=sv, in_=wm_src[:, :, nch * 512:(nch + 1) * 512]),
                          writes=[sk])
                    ps, pk = nxt("A")
                    for k in range(8):
                        P.op("pe", _L("matmul",
                            ps[:, :], lhsT=scb[:, k * 128:(k + 1) * 128], rhs=stg[:, k * 512:(k + 1) * 512],
                            start=(k == 0), stop=(k == 7)), reads=[sk, "scb"], writes=[pk])
                    m, half = nch // 2, nch % 2
                    dt_, dk, gg = dest[m]
                    P.op("dve", _L("tensor_tensor",
                        out=dt_[:, half * 512:(half + 1) * 512], in0=ps[:, :], in1=bm[:, nch * 512:(nch + 1) * 512],
                        op=ALU.add), reads=["bm"], writes=[pk, dk + str(half)])
                    if gg is not None:
                        P.op("dve", _L("scalar_tensor_tensor",
                            out=dt_[:, half * 512:(half + 1) * 512], in0=dt_[:, half * 512:(half + 1) * 512], scalar=1.0,
                            in1=gg[:, half * 512:(half + 1) * 512], op0=ALU.add, op1=ALU.mult),
                            reads=["gmx", "gff"], writes=[dk + str(half)])
                    yield

                g0 = p0_chunks()
                for _ in range(4):
                    next(g0)
                interleave_all(g0, pipeline_gen(list(range(NT)), [p1_s0, p1_s1, p1_s2]))
                for i in range(4):
                    P.dma("sp", _L("dma_start", out=modd[i], in_=mt[i][:]), reads=[f"mt{i}0", f"mt{i}1"],
                          writes=[("modd", i)])
                P.barrier()
                chk("p0")

            wstg = Rot(ph, nc, "wstg", 1, [128, 8 * 512], F32)
            wbf = Rot(ph, nc, "wbf", 2, [128, 8 * 512], BF16)


            P.barrier()
            chk("p1")
            win_src = win_d.ap().rearrange("(k p) n -> p k n", p=128)
            xs_r = Rot(ph, nc, "xs", 4, [128, 512], F32)
            sq_r = Rot(ph, nc, "sq", 2, [128, 512], F32)
            xg_r = Rot(ph, nc, "xg", 2, [128, 512], F32)
            t1_r = Rot(ph, nc, "t1", 3, [128, 512], F32)
            t2_r = Rot(ph, nc, "t2", 3, [128, 512], F32)
            ob_r = Rot(ph, nc, "ob", 3, [128, 512], BF16)
            st_r = Rot(ph, nc, "stT", 3, [128, 512], BF16)
            rt_r = Rot(ph, nc, "rt", 4, [128, 256], F32)
            vst_r = Rot(ph, nc, "vst", 2, [128, 528], BF16)
            gbc = sbp("gbc", [128, 128], F32)
            ss4_r = Rot(ph, nc, "ss4r", 3, [128, 16], F32)
            for vt in vst_r.t:
                P.op("pool", _L("memset", vt[:], 0.0), writes=["vinit"])
                P.op("pool", _L("memset", A(vt, 128, [[528, 128], [132, 4], [1, 1]]), 1.0), writes=["vinit"])
            P.barrier()

            def load_w(col0, ncols):
                stg, sk = wstg.next()
                wb, wk = wbf.next()
                sv = A(stg, 0, [[4096, 128], [ncols, 8], [1, ncols]])
                wv = A(wb, 0, [[4096, 128], [ncols, 8], [1, ncols]])
                P.dma("sp", _L("dma_start", out=sv, in_=win_src[:, :, col0:col0 + ncols]), writes=[sk])
                P.op("act", _L("copy", out=wv, in_=sv), reads=[sk], writes=[wk])
                return wb, wk

            def proj_tile(t, wb, wk, ncols):
                ps, pk = nxt("A")
                for k in range(8):
                    P.op("pe", _L("matmul", ps[:, 0:ncols], lhsT=hT[:, k, t * 128:(t + 1) * 128],
                                                       rhs=wb[:, k * ncols:(k + 1) * ncols], start=(k == 0), stop=(k == 7)),
                         reads=[("hT", t), wk], writes=[pk])
                return ps, pk

            def rope(xin, xk_, nh, hd, rt, rk, defer_add=False):
                hh = hd // 2
                t1, k1 = t1_r.next()
                t2, k2 = t2_r.next()
                ob, ok = ob_r.next()
                W = nh * hd
                nh1 = nh // 2 if nh >= 2 else nh
                P.op("dve", _L("tensor_tensor", out=A(t1, 0, [[512, 128], [hd, nh1], [1, hd]]),
                                                       in0=A(xin, 0, [[512, 128], [hd, nh1], [1, hd]]),
                                                       in1=A(rt, 0, [[256, 128], [0, nh1], [1, hd]]), op=ALU.mult),
                     reads=[xk_, rk], writes=[(k1, 0)])
                if nh1 < nh:
                    P.op("pool", _L("tensor_tensor", out=A(t1, nh1 * hd, [[512, 128], [hd, nh - nh1], [1, hd]]),
                                                           in0=A(xin, nh1 * hd, [[512, 128], [hd, nh - nh1], [1, hd]]),
                                                           in1=A(rt, 0, [[256, 128], [0, nh - nh1], [1, hd]]), op=ALU.mult),
                         reads=[xk_, rk], writes=[(k1, 1)])
                P.op("pool", _L("tensor_tensor", out=A(t2, 0, [[512, 128], [hd, nh], [1, hh]]),
                                                       in0=A(xin, hh, [[512, 128], [hd, nh], [1, hh]]),
                                                       in1=A(rt, hd, [[256, 128], [0, nh], [1, hh]]), op=ALU.mult),
                     reads=[xk_, rk], writes=[k2])
                P.op("pool", _L("tensor_tensor", out=A(t2, hh, [[512, 128], [hd, nh], [1, hh]]),
                                                       in0=A(xin, 0, [[512, 128], [hd, nh], [1, hh]]),
                                                       in1=A(rt, hd + hh, [[256, 128], [0, nh], [1, hh]]), op=ALU.mult),
                     reads=[xk_, rk], writes=[k2])
                def fin():
                    P.op("dve", _L("tensor_tensor", out=ob[:, 0:W], in0=t1[:, 0:W], in1=t2[:, 0:W], op=ALU.add),
                         reads=[(k1, 0), (k1, 1), k2], writes=[ok])
                    return ob, ok
                if defer_add:
                    return fin
                return fin()

            def transpose_store(ob, ok, nblk, dst_ap):
                pt, pk = nxt("T")
                for h in range(nblk):
                    P.op("pe", _L("transpose", out=pt[:, h * 128:(h + 1) * 128], in_=ob[:, h * 128:(h + 1) * 128],
                                                          identity=ident_b[:]), reads=[ok, "ident_b"], writes=[pk])
                stt, sk = st_r.next()
                P.op("act", _L("copy", out=stt[:, 0:nblk * 128], in_=pt[:, 0:nblk * 128]), writes=[pk, sk])
                P.dma("act", _L("dma_start", out=dst_ap, in_=stt[:, 0:nblk * 128]), reads=[sk], writes=[("scr", id(dst_ap))])

            qbanks = [(pA[0], "pA0"), (pA[1], "pA1"), (pB[0], "pB0"), (pB[1], "pB1")]
            qcnt = [0]

            def qk_group(col0, tiles, g_dram, dst, as_gen=False):
                wb, wk = load_w(col0, 512)
                P.dma("sp", _L("dma_start", out=gbc[:], in_=A(g_dram, 0, [[0, 128], [1, 128]])), writes=["gbc"])

                def s0(t, c):
                    ps, pk = qbanks[qcnt[0] % 4]
                    qcnt[0] += 1
                    for k in range(8):
                        P.op("pe", _L("matmul", ps[:, 0:512], lhsT=hT[:, k, t * 128:(t + 1) * 128],
                                      rhs=wb[:, k * 512:(k + 1) * 512], start=(k == 0), stop=(k == 7)),
                             reads=[("hT", t), wk], writes=[pk])
                    ss4, s4k = ss4_r.next()
                    for h in range(4):
                        P.op("act", _L("activation", out=junk[:, 0:128], in_=ps[:, h * 128:(h + 1) * 128], func=AF.Square,
                                       accum_out=ss4[:, h:h + 1]), writes=[pk, "junkb", s4k])
                    rt, rk = rt_r.next()
                    P.dma("sp", _L("dma_start", out=rt[:], in_=ropeH_d[t * 128:(t + 1) * 128, :]), writes=[rk])
                    c["ps"], c["rt"], c["ss4"] = (ps, pk), (rt, rk), (ss4, s4k)

                def s1(t, c):
                    ss4, s4k = c["ss4"]
                    P.op("act", _L("activation", out=ss4[:, 8:12], in_=ss4[:, 0:4], func=AF.Ln, scale=1.0 / 128,
                                   bias=eps_t[:, 0:1]), reads=["eps_t"], writes=[s4k])
                    P.op("act", _L("activation", out=ss4[:, 12:16], in_=ss4[:, 8:12], func=AF.Exp, scale=-0.5), writes=[s4k])

                def s2a(t, c):
                    (ps, pk), (ss4, s4k), (rt, rk) = c["ps"], c["ss4"], c["rt"]
                    xg, gk_ = xg_r.next()
                    for h in range(4):
                        P.op("dve", _L("scalar_tensor_tensor", out=xg[:, h * 128:(h + 1) * 128], in0=ps[:, h * 128:(h + 1) * 128],
                                       scalar=ss4[:, 12 + h:13 + h], in1=gbc[:], op0=ALU.mult, op1=ALU.mult),
                             reads=[s4k, "gbc"], writes=[pk, gk_])
                    c["rp"] = rope(xg, gk_, 4, 128, rt, rk, defer_add=True)

                def s2b(t, c):
                    c["ob"] = c["rp"]()

                def s3(t, c):
                    ob, ok = c["ob"]
                    transpose_store(ob, ok, 4, dst[t])

                if as_gen:
                    return pipeline_gen(list(tiles), [s0, s1, s2a, s2b, s3])
                run_pipeline(list(tiles), [s0, s1, s2a, s2b, s3])

            def v_group(col0, dst):
                wb, wk = load_w(col0, 512)
                for t in range(NT):
                    ps, pk = nxt("O")
                    for k in range(8):
                        P.op("pe", _L("matmul", ps[:, 0:512], lhsT=hT[:, k, t * 128:(t + 1) * 128],
                                      rhs=wb[:, k * 512:(k + 1) * 512], start=(k == 0), stop=(k == 7)),
                             reads=[("hT", t), wk], writes=[pk])
                    vt, vk = vst_r.next()
                    P.op("act", _L("copy", out=A(vt, 0, [[528, 128], [132, 4], [1, 128]]),
                                   in_=A(ps, 0, [[512, 128], [128, 4], [1, 128]])), writes=[pk, vk])
                    P.dma("act", _L("dma_start", out=dst[t], in_=vt[:]), reads=[vk], writes=[("vscr", t, id(dst))])
                    yield

            def qi_group():
                wb, wk = load_w(1536, 512)

                def s0(t, c):
                    ps, pk = proj_tile(t, wb, wk, 512)
                    xs, xk_ = xs_r.next()
                    P.op("act", _L("copy", out=xs[:], in_=ps[:, :]), writes=[pk, xk_])
                    rt, rk = rt_r.next()
                    P.dma("sp", _L("dma_start", out=rt[:, 0:128], in_=ropeI_d[t * 128:(t + 1) * 128, :]), writes=[rk])
                    c["xs"], c["rt"] = (xs, xk_), (rt, rk)

                def s1(t, c):
                    (xs, xk_), (rt, rk) = c["xs"], c["rt"]
                    c["ob"] = rope(xs, xk_, 8, 64, rt, rk)

                def s2(t, c):
                    ob, ok = c["ob"]
                    transpose_store(ob, ok, 4, qiT_d[t])

                run_pipeline(list(range(NO)), [s0, s1, s2])

            def kiwi_group():
                wb, wk = load_w(2048, 72)

                def s0(t, c):
                    ps, pk = proj_tile(t, wb, wk, 72)
                    xs, xk_ = xs_r.next()
                    P.op("act", _L("copy", out=xs[:, 0:72], in_=ps[:, 0:72]), writes=[pk, xk_])
                    if t < NO:
                        P.op("dve", _L("tensor_scalar", out=wi[:, t, :], in0=xs[:, 64:72], scalar1=IDX_W_SCALE,
                                       scalar2=None, op0=ALU.mult), reads=[xk_], writes=[("wi", t)])
                    rt, rk = rt_r.next()
                    P.dma("sp", _L("dma_start", out=rt[:, 0:128], in_=ropeI_d[t * 128:(t + 1) * 128, :]), writes=[rk])
                    c["xs"], c["rt"] = (xs, xk_), (rt, rk)

                def s1(t, c):
                    (xs, xk_), (rt, rk) = c["xs"], c["rt"]
                    ob, ok = rope(xs, xk_, 1, 64, rt, rk)
                    P.op("dve", _L("tensor_copy", out=ob[:, 64:128], in_=ob[:, 0:64]), reads=[ok], writes=[ok])
                    c["ob"] = (ob, ok)

                def s2(t, c):
                    ob, ok = c["ob"]
                    transpose_store(ob, ok, 1, kiT_d[t])

                run_pipeline(list(range(NT)), [s0, s1, s2])

            def gate_group(col0, dst):
                for hf in range(2):
                    wb, wk = load_w(col0 + hf * 512, 512)
                    for fcl in range(4):
                        fc = hf * 4 + fcl
                        for tg in range(4):
                            ps, pk = nxt("O")
                            for k in range(8):
                                P.op("pe", _L("matmul",
                                    ps[:, :], lhsT=wb[:, k * 512 + fcl * 128:k * 512 + (fcl + 1) * 128],
                                    rhs=hT[:, k, tg * 512:(tg + 1) * 512], start=(k == 0), stop=(k == 7)),
                                    reads=[wk] + [("hT", tg * 4 + j) for j in range(4)], writes=[pk])
                            stt, sk = st_r.next()
                            P.op("act", _L("activation", out=stt[:], in_=ps[:, :], func=AF.Sigmoid), writes=[pk, sk])
                            P.dma("act", _L("dma_start", out=dst[fc][:, tg * 512:(tg + 1) * 512], in_=stt[:]),
                                  reads=[sk], writes=[("gscr", fc, tg, id(dst))])

            own = list(range(NO))
            alls = list(range(NT))
            qk_group(0, own, gq_d["a"], QT_d["a"])
            chk("p2_1")
            interleave_all(qk_group(512, alls, gk_d["a"], KT_d["a"], as_gen=True), v_group(1024, V_d["a"]))
            chk("p2_2")
            chk("p2_3")
            qi_group()
            chk("p2_4")
            kiwi_group()
            chk("p2_5")
            qk_group(2120, own, gq_d["b"], QT_d["b"])
            interleave_all(qk_group(2632, alls, gk_d["b"], KT_d["b"], as_gen=True), v_group(3144, V_d["b"]))
            gate_group(3656, g_d["a"])
            chk("p2a")
            gate_group(4680, g_d["b"])
            P.barrier()
            chk("p2")

        with contextlib.ExitStack() as ph:
            def sbp(name, shape, dt):
                return sb(name, shape, dt, st=ph)

            KTs = {"a": sbp("KTa", [128, NT, 512], BF16)}
            Vs = {"a": sbp("Va", [128, NT, 528], BF16)}
            kv_r = Rot(ph, nc, "kvb", 4, [128, 2 * 512 + 2 * 528], BF16)
            kiT = sbp("kiT", [128, NT * 128], BF16)
            sc_r = Rot(ph, nc, "sc", 2, [128, S], F32)
            oaT_r = Rot(ph, nc, "oaT", 2, [128, 512], BF16)
            MB_r = Rot(ph, nc, "MB", 3, [128, S], BF16)
            rl_r = Rot(ph, nc, "rl", 4, [128, 512], BF16)
            Dw_r = Rot(ph, nc, "Dw", 2, [128, 8 * 128], BF16)
            PT_r = Rot(ph, nc, "PT", 4, [128, 512], BF16)
            qt_r = Rot(ph, nc, "qt", 4, [128, 512], BF16)
            qi_r = Rot(ph, nc, "qit", 2, [128, 512], BF16)
            oa_r = Rot(ph, nc, "oa", 3, [128, 512], BF16)
            bs = sbp("bs", [128, 8], F32)
            Wn = sbp("Wn", [128, NB + 1], F32)
            rden = sbp("rden", [128, 8], F32)
            ksum = sbp("ksum", [128, 128], F32)
            kmf = sbp("kmf", [128, 64], F32)
            kmT = sbp("kmT", [128, 64], BF16)
            gsb = sbp("gsb", [128, 64], F32)
            m8 = sbp("m8", [128, 32], F32)
            mbk = sbp("mbk", [128, 64], F32)
            mbkb = sbp("mbkb", [128, 64], BF16)

            def load_kv(kind):
                order = [(s__ // 2) + 16 * (s__ % 2) for s__ in range(NT)]
                for s_ in order:
                    P.dma("sp", _L("dma_start", out=KTs[kind][:, s_, :], in_=KT_d[kind][s_]), writes=[("KT" + kind, s_)])
                for s_ in order:
                    P.dma("sp", _L("dma_start", out=Vs[kind][:, s_, :], in_=V_d[kind][s_]), writes=[("V" + kind, s_)])

            def slot_of(i, cb):
                return cb if cb <= i else 16 + (cb - (i + 1))

            def attention(i, qt, qk_, bias_fn, dstT):
                ncb = 2 * (i + 1)
                oa, ok = oa_r.next()
                units = []
                for h in range(4):
                    for g0 in range(0, ncb, 4):
                        units.append((h, list(range(g0, min(g0 + 4, ncb)))))
                Obank = {}
                pend_evac = []

                def emit_st(u):
                    h, cbs = u
                    ST, Sk = pB[0], "pB0"
                    for e_, cb in enumerate(cbs):
                        s_ = slot_of(i, cb)
                        bias = bias_fn(h, cb)
                        P.op("pe", _L("matmul", ST[:, e_ * 128:(e_ + 1) * 128], lhsT=KTs["a"][:, s_, h * 128:(h + 1) * 128],
                                      rhs=qt[:, h * 128:(h + 1) * 128], start=True, stop=(bias is None)),
                             reads=[("KTa", s_), qk_], writes=[Sk])
                        if bias is not None:
                            P.op("pe", _L("matmul", ST[:, e_ * 128:(e_ + 1) * 128], lhsT=bias[0], rhs=bias[1],
                                          start=False, stop=True), reads=bias[2], writes=[Sk])
                    n = len(cbs) * 128
                    pt, pk = PT_r.next()
                    P.op("act", _L("activation", out=pt[:, 0:n], in_=ST[:, 0:n], func=AF.Exp, scale=ATT_SCALE),
                         writes=[Sk, pk])
                    return pt, pk

                def emit_pv(u, ptk):
                    h, cbs = u
                    pt, pk = ptk
                    if h not in Obank:
                        Obank[h] = (pO[0], "pO0")
                    O, Ok = Obank[h]
                    for e_, cb in enumerate(cbs):
                        s_ = slot_of(i, cb)
                        P.op("pe", _L("matmul", O[:, 0:129], lhsT=pt[:, e_ * 128:(e_ + 1) * 128],
                                      rhs=Vs["a"][:, s_, h * 132:h * 132 + 129], start=(cb == 0), stop=(cb == ncb - 1)),
                             reads=[pk, ("Va", s_)], writes=[Ok])
                    if cbs[-1] == ncb - 1:
                        def evac(h=h, O=O, Ok=Ok):
                            P.op("dve", _L("reciprocal", out=rden[:, h:h + 1], in_=O[:, 128:129]), writes=[Ok, ("rden", h)])
                            P.op("dve", _L("tensor_scalar", out=oa[:, h * 128:(h + 1) * 128], in0=O[:, 0:128],
                                           scalar1=rden[:, h:h + 1], scalar2=None, op0=ALU.mult),
                                 reads=[("rden", h)], writes=[Ok, ok])
                        pend_evac.append(evac)
                        while pend_evac:
                            pend_evac.pop(0)()

                prev = None
                for u in units:
                    ptk = emit_st(u)
                    if prev is not None:
                        emit_pv(*prev)
                    prev = (u, ptk)
                    yield
                emit_pv(*prev)
                yield
                while pend_evac:
                    pend_evac.pop(0)()
                pt_, pk_ = nxt("T")
                for h in range(4):
                    P.op("pe", _L("transpose", out=pt_[:, h * 128:(h + 1) * 128], in_=oa[:, h * 128:(h + 1) * 128],
                                  identity=ident_b[:]), reads=[ok, "ident_b"], writes=[pk_])
                ost, osk = oaT_r.next()
                P.op("act", _L("copy", out=ost[:], in_=pt_[:, 0:512]), writes=[pk_, osk])
                P.dma("act", _L("dma_start", out=oT_d[dstT][i], in_=ost[:]), reads=[osk], writes=[("oT", dstT, i)])
                yield

            def interleave(*gens):
                gens = [g for g in gens if g is not None]
                while gens:
                    for g in list(gens):
                        try:
                            next(g)
                        except StopIteration:
                            gens.remove(g)

            load_kv("a")
            for s_ in range(NT):
                P.dma("sp", _L("dma_start", out=kiT[:, s_ * 128:(s_ + 1) * 128], in_=kiT_d[s_]), writes=[("ki", s_)])

            def dsa_scores(i, scres):
                L = (i + 1) * 128
                sc, sk = scres
                qit, qik = qi_r.next()
                P.dma("sp", _L("dma_start", out=qit[:], in_=qiT_d[i]), writes=[qik])
                Dw, Dk = Dw_r.next()
                for h in range(8):
                    P.op("pool", _L("tensor_scalar", out=Dw[:, h * 128:(h + 1) * 128], in0=ident_b[:],
                                   scalar1=wi[:, i, h:h + 1], scalar2=None, op0=ALU.mult),
                         reads=["ident_b", ("wi", i)], writes=[(Dk, h)])
                for seg in range(2):
                    base = seg * 2048
                    for c0 in range(0, L, 512):
                        n = min(512, L - c0)
                        kis = [("ki", seg * 16 + (c0 + j) // 128) for j in range(0, n, 128)]
                        acc, acck = nxt("C")
                        pend = []
                        for hp in range(4):
                            cur = []
                            for hf in range(2):
                                h = 2 * hp + hf
                                ps, pk = nxt("A")
                                P.op("pe", _L("matmul", ps[:, 0:n], lhsT=qit[64 * hf:64 * hf + 64, hp * 128:(hp + 1) * 128],
                                              rhs=kiT[64 * hf:64 * hf + 64, base + c0:base + c0 + n], start=True, stop=True),
                                     reads=[qik] + kis, writes=[pk])
                                cur.append((h, ps, pk))
                            for (h, ps, pk) in cur:
                                rl, rk = rl_r.next()
                                P.op("act", _L("activation", out=rl[:, 0:n], in_=ps[:, 0:n], func=AF.Relu), writes=[pk, rk])
                                pend.append((_L("matmul", acc[:, 0:n], lhsT=Dw[:, h * 128:(h + 1) * 128], rhs=rl[:, 0:n],
                                                start=(h == 0), stop=(h == 7)), [(Dk, h), rk]))
                            while len(pend) > 2:
                                e_, r_ = pend.pop(0)
                                P.op("pe", e_, reads=r_, writes=[acck])
                            if hp % 2 == 1:
                                yield
                        while pend:
                            e_, r_ = pend.pop(0)
                            P.op("pe", e_, reads=r_, writes=[acck])
                        d0 = seg * L + c0
                        P.op("act", _L("copy", out=sc[:, d0:d0 + n], in_=acc[:, 0:n]), writes=[acck, (sk, d0)])
                        yield

            def dsa_bisect(i, scres, res):
                L = (i + 1) * 128
                W2 = 2 * L
                mb, mk = res
                if i == 0:
                    P.op("pool", _L("tensor_copy", out=mb[:, 0:128], in_=caus_b[:]), reads=["caus_b"], writes=[mk])
                    P.op("pool", _L("tensor_copy", out=mb[:, 128:256], in_=pb_b[:]), reads=["pb_b"], writes=[mk])
                    return
                sc, sk = scres
                allsc = [(sk, seg * L + c0) for seg in range(2) for c0 in range(0, L, 512)]
                P.op("dve", _L("max", out=m8[:, 0:8], in_=sc[:, 0:W2]), reads=allsc, writes=["m8"])
                P.op("dve", _L("tensor_copy", out=bs[:, 0:1], in_=m8[:, 0:1]), reads=["m8"], writes=[("bs", 0)])
                P.op("dve", _L("tensor_reduce", out=bs[:, 1:2], in_=sc[:, 0:W2], axis=AX.X, op=ALU.min),
                     reads=allsc, writes=[("bs", 1)])
                yield
                P.op("dve", _L("tensor_tensor", out=bs[:, 2:3], in0=bs[:, 0:1], in1=bs[:, 1:2], op=ALU.subtract),
                     reads=[("bs", 0), ("bs", 1)], writes=["bs"])
                P.op("dve", _L("tensor_scalar", out=Wn[:], in0=pow2c[:], scalar1=bs[:, 2:3], scalar2=None, op0=ALU.mult),
                     reads=["pow2c", "bs"], writes=["Wn"])
                P.op("dve", _L("tensor_tensor", out=bs[:, 3:4], in0=bs[:, 1:2], in1=Wn[:, 0:1], op=ALU.add),
                     reads=["Wn", ("bs", 1)], writes=["bs"])
                P.op("dve", _L("tensor_tensor", out=sc[:, i * 128:(i + 1) * 128], in0=sc[:, i * 128:(i + 1) * 128],
                               in1=caus_f[:], op=ALU.add), reads=["caus_f", ("bs", 0), ("bs", 1)], writes=allsc)
                P.op("dve", _L("tensor_tensor", out=sc[:, L + i * 128:L + (i + 1) * 128],
                               in0=sc[:, L + i * 128:L + (i + 1) * 128], in1=pb_f[:], op=ALU.add),
                     reads=["pb_f"], writes=allsc)
                yield
                for n_ in range(NB):
                    P.op("dve", _L("tensor_scalar", out=mb[:, 0:W2], in0=sc[:, 0:W2], scalar1=bs[:, 3:4], scalar2=None,
                                   op0=ALU.is_ge, op1=ALU.add, accum_out=bs[:, 4:5]),
                         reads=allsc + ["bs"], writes=[mk, "bs"])
                    P.op("dve", _L("tensor_scalar", out=bs[:, 5:6], in0=bs[:, 4:5], scalar1=256.0, scalar2=-0.5,
                                   op0=ALU.is_ge, op1=ALU.add), writes=["bs"])
                    P.op("dve", _L("scalar_tensor_tensor", out=bs[:, 3:4], in0=bs[:, 5:6], scalar=Wn[:, n_:n_ + 1],
                                   in1=bs[:, 3:4], op0=ALU.mult, op1=ALU.add), reads=["Wn"], writes=["bs"])
                    yield
                P.op("dve", _L("tensor_tensor", out=bs[:, 6:7], in0=bs[:, 3:4], in1=Wn[:, NB:NB + 1], op=ALU.subtract),
                     reads=["Wn"], writes=["bs"])
                P.op("dve", _L("tensor_scalar", out=mb[:, 0:W2], in0=sc[:, 0:W2], scalar1=bs[:, 6:7], scalar2=MNEG,
                               op0=ALU.is_lt, op1=ALU.mult), reads=allsc + ["bs"], writes=[mk])
                yield

            def load_q(kind, i):
                qt, qk_ = qt_r.next()
                P.dma("sp", _L("dma_start", out=qt[:], in_=QT_d[kind][i]), writes=[qk_])
                return qt, qk_

            for s_ in range(NT):
                kv, kvk = kv_r.next()
                P.dma("sp", _L("dma_start", out=kv[:, 0:512], in_=KT_d["b"][s_]), writes=[(kvk, "k")])
                P.op("dve", _L("tensor_reduce", out=ksum[:, s_ * 4:(s_ + 1) * 4], in_=A(kv, 0, [[2080, 128], [128, 4], [1, 128]]),
                               axis=AX.X, op=ALU.add), reads=[(kvk, "k")], writes=[("ksum", s_)])
            P.op("dve", _L("tensor_tensor", out=A(kmf, 0, [[64, 128], [16, 4], [1, 16]]),
                           in0=A(ksum, 0, [[128, 128], [1, 4], [4, 16]]),
                           in1=A(ksum, 64, [[128, 128], [1, 4], [4, 16]]), op=ALU.add),
                 reads=[("ksum", s_) for s_ in range(NT)], writes=["kmf"])
            P.op("dve", _L("tensor_scalar", out=kmT[:], in0=kmf[:], scalar1=1.0 / 256, scalar2=None, op0=ALU.mult),
                 reads=["kmf"], writes=["kmT"])

            sel_r = Rot(ph, nc, "selr", 2, [128, 64], F32)
            acc_r = Rot(ph, nc, "accr", 8, [128, 132], F32)

            def moba_prep(i, qt, qk_):
                selt, selk = sel_r.next()
                if i < 4:
                    P.op("dve", _L("memset", selt[:], 1.0), writes=[selk])
                    return selt, selk
                ps, pk = nxt("A")
                for h in range(4):
                    P.op("pe", _L("matmul", ps[:, h * 16:(h + 1) * 16], lhsT=qt[:, h * 128:(h + 1) * 128],
                                  rhs=kmT[:, h * 16:(h + 1) * 16], start=True, stop=True), reads=[qk_, "kmT"], writes=[pk])
                P.op("dve", _L("tensor_copy", out=gsb[:], in_=ps[:, 0:64]), writes=[pk, "gsb"])
                P.op("dve", _L("memset", A(gsb, i, [[64, 128], [16, 4], [1, 16 - i]]), -1e30), writes=["gsb"])
                for h in range(4):
                    P.op("dve", _L("max", out=m8[:, h * 8:(h + 1) * 8], in_=gsb[:, h * 16:(h + 1) * 16]),
                         reads=["gsb"], writes=["m8"])
                P.op("dve", _L("tensor_tensor", out=A(selt, 0, [[64, 128], [16, 4], [1, 16]]),
                               in0=A(gsb, 0, [[64, 128], [16, 4], [1, 16]]),
                               in1=A(m8, 2, [[32, 128], [8, 4], [0, 16]]), op=ALU.is_ge), reads=["gsb", "m8"], writes=[selk])
                return selt, selk

            def moba_attention(i, qt, qk_, sel):
                selt, selk = sel
                oa, ok = oa_r.next()
                blocks = [("own", i)] + [("past", n) for n in range(i)]
                accs = [acc_r.next() for _ in range(4)]
                kvs = {}

                def get_kv(bl):
                    if bl not in kvs:
                        kv, kvk = kv_r.next()
                        n = bl[1]
                        P.dma("sp", _L("dma_start", out=A(kv, 0, [[2080, 128], [512, 2], [1, 512]]),
                                       in_=A(KT_d["b"], n * 128 * 512, [[512, 128], [16 * 128 * 512, 2], [1, 512]])),
                              writes=[(kvk, "k")])
                        P.dma("sp", _L("dma_start", out=A(kv, 1024, [[2080, 128], [528, 2], [1, 528]]),
                                       in_=A(V_d["b"], n * 128 * 528, [[528, 128], [16 * 128 * 528, 2], [1, 528]])),
                              writes=[(kvk, "v")])
                        kvs[bl] = (kv, kvk)
                    return kvs[bl]

                units = [(bl, h) for bl in blocks for h in range(4)]

                def emit_st(u):
                    bl, h = u
                    kv, kvk = get_kv(bl)
                    own = (bl[0] == "own")
                    ST, Sk = pB[1], "pB1"
                    for si in range(2):
                        P.op("pe", _L("matmul", ST[:, si * 128:(si + 1) * 128],
                                      lhsT=kv[:, si * 512 + h * 128:si * 512 + (h + 1) * 128],
                                      rhs=qt[:, h * 128:(h + 1) * 128], start=True, stop=(not own)),
                             reads=[(kvk, "k"), qk_], writes=[Sk])
                        if own:
                            bt, bk_ = (caus_b, "caus_b") if si == 0 else (pb_b, "pb_b")
                            P.op("pe", _L("matmul", ST[:, si * 128:(si + 1) * 128], lhsT=bt[:], rhs=ident_b[:],
                                          start=False, stop=True), reads=[bk_, "ident_b"], writes=[Sk])
                    pt, pk = PT_r.next()
                    P.op("act", _L("activation", out=pt[:, 0:256], in_=ST[:, 0:256], func=AF.Exp, scale=ATT_SCALE),
                         writes=[Sk, pk])
                    return pt, pk

                def emit_pv(u, ptk):
                    bl, h = u
                    pt, pk = ptk
                    kv, kvk = get_kv(bl)
                    acc, ack = accs[h]
                    O, Ok = pO[1], "pO1"
                    for si in range(2):
                        P.op("pe", _L("matmul", O[:, 0:129], lhsT=pt[:, si * 128:(si + 1) * 128],
                                      rhs=kv[:, 1024 + si * 528 + h * 132:1024 + si * 528 + h * 132 + 129],
                                      start=(si == 0), stop=(si == 1)), reads=[pk, (kvk, "v")], writes=[Ok])
                    if bl[0] == "own":
                        P.op("dve", _L("tensor_copy", out=acc[:, 0:129], in_=O[:, 0:129]), writes=[Ok, ack])
                    else:
                        j = h * 16 + bl[1]
                        P.op("dve", _L("scalar_tensor_tensor", out=acc[:, 0:129], in0=O[:, 0:129], scalar=selt[:, j:j + 1],
                                       in1=acc[:, 0:129], op0=ALU.mult, op1=ALU.add), reads=[selk], writes=[Ok, ack])
                    if bl == blocks[-1]:
                        P.op("dve", _L("reciprocal", out=rden[:, 4 + h:5 + h], in_=acc[:, 128:129]), reads=[ack],
                             writes=[("rdenb", h)])
                        P.op("dve", _L("tensor_scalar", out=oa[:, h * 128:(h + 1) * 128], in0=acc[:, 0:128],
                                       scalar1=rden[:, 4 + h:5 + h], scalar2=None, op0=ALU.mult),
                             reads=[("rdenb", h), ack], writes=[ok])

                prev = None
                for u in units:
                    ptk = emit_st(u)
                    if prev is not None:
                        emit_pv(*prev)
                    prev = (u, ptk)
                    yield
                emit_pv(*prev)
                pt_, pk_ = nxt("T")
                for h in range(4):
                    P.op("pe", _L("transpose", out=pt_[:, h * 128:(h + 1) * 128], in_=oa[:, h * 128:(h + 1) * 128],
                                  identity=ident_b[:]), reads=[ok, "ident_b"], writes=[pk_])
                ost, osk = oaT_r.next()
                P.op("act", _L("copy", out=ost[:], in_=pt_[:, 0:512]), writes=[pk_, osk])
                P.dma("act", _L("dma_start", out=oT_d["b"][i], in_=ost[:]), reads=[osk], writes=[("oT", "b", i)])
                yield

            scs, mbs = {}, {}
            mprep = {}
            for t in range(-2, NO):
                gens = []
                j = t + 2
                if 1 <= j < NO:
                    scs[j] = sc_r.next()
                    gens.append(dsa_scores(j, scs[j]))
                j = t + 1
                if 0 <= j < NO:
                    mbs[j] = MB_r.next()
                    gens.append(dsa_bisect(j, scs.get(j), mbs[j]))
                    qb = load_q("b", j)
                    mprep[j] = (qb, moba_prep(j, *qb))
                if t >= 0:
                    qt, qk_ = load_q("a", t)
                    mb, mk = mbs[t]
                    gens.append(attention(t, qt, qk_,
                                          lambda h, cb, mb=mb, mk=mk: (mb[:, cb * 128:(cb + 1) * 128], ident_b[:], [mk, "ident_b"]),
                                          "a"))
                    qb, pb_ = mprep[t]
                    gens.append(moba_attention(t, qb[0], qb[1], pb_))
                interleave(*gens)
            P.barrier()
            chk("p5")

        if True:
            h2T = sb("h2T", [128, 8, NO * 128], BF16)
            with contextlib.ExitStack() as p6:
                def sb6(name, shape, dt):
                    return sb(name, shape, dt, st=p6)
                wstg = Rot(p6, nc, "wstg6", 2, [128, 1024], F32)
                wbr = {"a": sb6("wbra", [128, 4, D], BF16), "b": sb6("wbrb", [128, 4, D], BF16)}
                wout = sb6("wout", [128, 8, D], BF16)
                GTm = sb6("GTm", [128, D], F32)
                SHf = sb6("SHf", [128, D], F32)
                G1f = sb6("G1f", [128, D], F32)
                gt_r = {"a": Rot(p6, nc, "gta", 1, [128, 8 * 512], BF16), "b": Rot(p6, nc, "gtb", 1, [128, 8 * 512], BF16)}
                os_r = {"a": Rot(p6, nc, "osa", 2, [128, 4 * 512], BF16), "b": Rot(p6, nc, "osb", 2, [128, 4 * 512], BF16)}
                m1_r = Rot(p6, nc, "m1", 2, [128, 512], F32)
                m2_r = Rot(p6, nc, "m2", 2, [128, 512], F32)
                mTall = sb6("mTall", [128, 8 * NO * 128], BF16)
                xbuf = Rot(p6, nc, "xb6", 3, [128, D], F32)
                yt_r = Rot(p6, nc, "yt", 2, [128, D], F32)
                x1_r = Rot(p6, nc, "x1t", 3, [128, D], F32)
                tmpf = Rot(p6, nc, "tmpf6", 2, [128, D], F32)
                hb = Rot(p6, nc, "hb6", 3, [128, D], BF16)
                junk = sb6("junk6", [128, D], BF16)

                def load_rows(dst3, dkey, src_d, r0, nrows_chunks):
                    for c in range(nrows_chunks):
                        stg, sk = wstg.next()
                        P.dma("sp", _L("dma_start", out=stg[:], in_=src_d[r0 + c * 128:r0 + (c + 1) * 128, :]),
                              writes=[sk])
                        P.op("act", _L("copy", out=dst3[:, c, :], in_=stg[:]), reads=[sk],
                             writes=[(dkey, c)])

                load_rows(wbr["a"], "wbra", wbr_d["a"], 0, 4)
                load_rows(wbr["b"], "wbrb", wbr_d["b"], 0, 4)
                load_rows(wout, "wout", wout_d, 0, 8)
                P.dma("sp", _L("dma_start", out=GTm[:], in_=modd[0]), reads=[("modd", 0)], writes=["GTm"])
                P.dma("sp", _L("dma_start", out=SHf[:], in_=modd[1]), reads=[("modd", 1)], writes=["SHf"])
                P.dma("sp", _L("dma_start", out=G1f[:], in_=modd[2]), reads=[("modd", 2)], writes=["G1f"])

                def norm_mod_T6(xt, xk, col0):
                    P.op("act", _L("activation", out=junk[:], in_=xt[:], func=AF.Square, accum_out=ssm[:, 0:1]),
                         reads=[xk], writes=["junk6", "ssm"])
                    P.op("dve", _L("tensor_scalar", out=ssm[:, 1:2], in0=ssm[:, 0:1], scalar1=1.0 / D, scalar2=EPS,
                                                          op0=ALU.mult, op1=ALU.add), writes=["ssm"])
                    P.op("act", _L("activation", out=ssm[:, 2:3], in_=ssm[:, 1:2], func=AF.Ln), writes=["ssm"])
                    P.op("act", _L("activation", out=ssm[:, 3:4], in_=ssm[:, 2:3], func=AF.Exp, scale=-0.5), writes=["ssm"])
                    tf, tk = tmpf.next()
                    P.op("dve", _L("scalar_tensor_tensor", out=tf[:], in0=xt[:], scalar=ssm[:, 3:4], in1=G1f[:],
                                                                 op0=ALU.mult, op1=ALU.mult), reads=[xk, "ssm", "G1f"], writes=[tk])
                    hbt, hk = hb.next()
                    P.op("dve", _L("tensor_tensor", out=hbt[:], in0=tf[:], in1=SHf[:], op=ALU.add),
                         reads=[tk, "SHf"], writes=[hk])
                    pt, pk = nxt("T")
                    for k in range(8):
                        P.op("pe", _L("transpose", out=pt[:, k * 128:(k + 1) * 128], in_=hbt[:, k * 128:(k + 1) * 128],
                                                              identity=ident_b[:]), reads=[hk, "ident_b"], writes=[pk])
                    P.op("act", _L("copy", out=A(h2T, col0, [[8 * NO * 128, 128], [NO * 128, 8], [1, 128]]),
                                                 in_=A(pt, 0, [[1024, 128], [128, 8], [1, 128]])),
                         writes=[pk, ("h2T", col0 // 128)])

                for tg in range(4):
                    gts = {}
                    for kind in ("a", "b"):
                        gt, gk_ = gt_r[kind].next()
                        for fc in range(8):
                            P.dma("sp", _L("dma_start",
                                out=gt[:, fc * 512:(fc + 1) * 512], in_=g_d[kind][fc][:, tg * 512:(tg + 1) * 512]),
                                writes=[(gk_, fc)])
                        gts[kind] = (gt, gk_)
                    oss = {}
                    for kind in ("a", "b"):
                        ot_, otk_ = os_r[kind].next()
                        for tl in range(4):
                            P.dma("sp", _L("dma_start", out=A(ot_, tl * 128, [[2048, 128], [512, 4], [1, 128]]),
                                           in_=A(oT_d[kind], (tg * 4 + tl) * 128 * 512, [[512, 128], [128, 4], [1, 128]])),
                                  reads=[("oT", kind, tg * 4 + tl)], writes=[(otk_, tl)])
                        oss[kind] = (ot_, otk_)
                    mT, mTk = mTall, ("mT", tg)
                    for fc in range(8):
                        pss = {}
                        for kind in ("a", "b"):
                            ps, pk = nxt("O")
                            for h in range(4):
                                P.op("pe", _L("matmul",
                                    ps[:, :], lhsT=wbr[kind][:, h, fc * 128:(fc + 1) * 128],
                                    rhs=oss[kind][0][:, h * 512:(h + 1) * 512], start=(h == 0), stop=(h == 3)),
                                    reads=[("wbr" + kind, h)] + [(oss[kind][1], j) for j in range(4)], writes=[pk])
                            pss[kind] = (ps, pk)
                        m1, k1 = m1_r.next()
                        m2, k2 = m2_r.next()
                        P.op("dve", _L("tensor_tensor", out=m1[:], in0=pss["a"][0][:, :],
                                                                           in1=gts["a"][0][:, fc * 512:(fc + 1) * 512], op=ALU.mult),
                             reads=[(gts["a"][1], fc)], writes=[pss["a"][1], k1])
                        P.op("dve", _L("tensor_tensor", out=m2[:], in0=pss["b"][0][:, :],
                                                                           in1=gts["b"][0][:, fc * 512:(fc + 1) * 512], op=ALU.mult),
                             reads=[(gts["b"][1], fc)], writes=[pss["b"][1], k2])
                        P.op("pool", _L("tensor_tensor",
                            out=mT[:, fc * 2048 + tg * 512:fc * 2048 + (tg + 1) * 512], in0=m1[:], in1=m2[:], op=ALU.add), reads=[k1, k2], writes=[(mTk, fc)])
                ssm_r6 = Rot(p6, nc, "ssm6", 3, [128, 8], F32)

                def t_s0(tile, c):
                    tg = tile // 4
                    xt, xk = xbuf.next()
                    P.dma("sp", _L("dma_start", out=xt[:], in_=x_d[tile * 128:(tile + 1) * 128, :]), writes=[xk])
                    yt, yk = yt_r.next()
                    for hf in range(2):
                        ps, pk = nxt("A")
                        for k in range(8):
                            P.op("pe", _L("matmul", ps[:, :], lhsT=mTall[:, k * 2048 + tile * 128:k * 2048 + (tile + 1) * 128],
                                          rhs=wout[:, k, hf * 512:(hf + 1) * 512], start=(k == 0), stop=(k == 7)),
                                 reads=[(("mT", tg), k), ("wout", k)], writes=[pk])
                        P.op("dve", _L("tensor_tensor", out=yt[:, hf * 512:(hf + 1) * 512], in0=ps[:, :],
                                       in1=GTm[:, hf * 512:(hf + 1) * 512], op=ALU.mult), reads=["GTm"], writes=[pk, (yk, hf)])
                    c["x"], c["y"] = (xt, xk), (yt, yk)

                def t_s1(tile, c):
                    (xt, xk), (yt, yk) = c["x"], c["y"]
                    x1t, x1k = x1_r.next()
                    P.op("pool", _L("tensor_tensor", out=x1t[:], in0=yt[:], in1=xt[:], op=ALU.add),
                         reads=[(yk, 0), (yk, 1), xk], writes=[x1k])
                    P.dma("sp", _L("dma_start", out=x1_d[tile * 128:(tile + 1) * 128, :], in_=x1t[:]),
                          reads=[x1k], writes=[("x1d", tile)])
                    sm, smk = ssm_r6.next()
                    P.op("act", _L("activation", out=junk[:], in_=x1t[:], func=AF.Square, accum_out=sm[:, 0:1]),
                         reads=[x1k], writes=["junk6", smk])
                    P.op("dve", _L("tensor_scalar", out=sm[:, 1:2], in0=sm[:, 0:1], scalar1=1.0 / D, scalar2=EPS,
                                   op0=ALU.mult, op1=ALU.add), writes=[smk])
                    P.op("act", _L("activation", out=sm[:, 2:3], in_=sm[:, 1:2], func=AF.Ln), writes=[smk])
                    P.op("act", _L("activation", out=sm[:, 3:4], in_=sm[:, 2:3], func=AF.Exp, scale=-0.5), writes=[smk])
                    c["x1"], c["sm"] = (x1t, x1k), (sm, smk)

                def t_s2(tile, c):
                    (x1t, x1k), (sm, smk) = c["x1"], c["sm"]
                    tf, tk = tmpf.next()
                    P.op("dve", _L("scalar_tensor_tensor", out=tf[:], in0=x1t[:], scalar=sm[:, 3:4], in1=G1f[:],
                                   op0=ALU.mult, op1=ALU.mult), reads=[x1k, smk, "G1f"], writes=[tk])
                    hbt, hk = hb.next()
                    P.op("dve", _L("tensor_tensor", out=hbt[:], in0=tf[:], in1=SHf[:], op=ALU.add), reads=[tk, "SHf"], writes=[hk])
                    c["hb"] = (hbt, hk)

                def t_s3(tile, c):
                    hbt, hk = c["hb"]
                    pt, pk = nxt("T")
                    for k in range(8):
                        P.op("pe", _L("transpose", out=pt[:, k * 128:(k + 1) * 128], in_=hbt[:, k * 128:(k + 1) * 128],
                                      identity=ident_b[:]), reads=[hk, "ident_b"], writes=[pk])
                    P.op("act", _L("copy", out=A(h2T, tile * 128, [[8 * NO * 128, 128], [NO * 128, 8], [1, 128]]),
                                   in_=A(pt, 0, [[1024, 128], [128, 8], [1, 128]])), writes=[pk, ("h2T", tile)])

                run_pipeline(list(range(NO)), [t_s0, t_s1, t_s2, t_s3])
                P.barrier()
                chk("p6")

            aT = sb("aT", [128, 22, NO * 128], BF16)
            wd = sb("wd", [128, 22, D], BF16)
            wstg8 = Rot(top, nc, "wstg8", 2, [128, 1024], F32)

            def load_wd(c):
                stg, sk = wstg8.next()
                P.dma("sp", _L("dma_start", out=stg[:], in_=wdn_d[c * 128:(c + 1) * 128, :]), writes=[sk])
                P.op("pool", _L("tensor_copy", out=wd[:, c, :], in_=stg[:]), reads=[sk], writes=[("wd", c)])

            with contextlib.ExitStack() as p7:
                wstg = Rot(p7, nc, "wstg7", 2, [128, 8 * 256], F32)
                wgu = Rot(p7, nc, "wgu", 2, [128, 8 * 256], BF16)
                sg_r = Rot(p7, nc, "sg", 2, [128, 512], F32)
                gu_src = wgu_d.ap().rearrange("(k p) n -> p k n", p=128)
                for c in range(22):
                    stg, sk = wstg.next()
                    wb, wk = wgu.next()
                    P.dma("sp", _L("dma_start", out=A(stg, 0, [[2048, 128], [256, 8], [1, 128]]),
                                                                   in_=gu_src[:, :, c * 128:(c + 1) * 128]), writes=[(sk, 0)])
                    P.dma("sp", _L("dma_start", out=A(stg, 128, [[2048, 128], [256, 8], [1, 128]]),
                                                                   in_=gu_src[:, :, DFF + c * 128:DFF + (c + 1) * 128]), writes=[(sk, 1)])
                    P.op("pool", _L("tensor_copy", out=wb[:], in_=stg[:]), reads=[(sk, 0), (sk, 1)], writes=[wk])
                    for tg in range(4):
                        pg, pgk = nxt("A")
                        pu, puk = nxt("O")
                        hk = [("h2T", tg * 4 + j) for j in range(4)]
                        for k in range(8):
                            P.op("pe", _L("matmul",
                                pg[:, :], lhsT=wb[:, k * 256:k * 256 + 128], rhs=h2T[:, k, tg * 512:(tg + 1) * 512],
                                start=(k == 0), stop=(k == 7)), reads=[wk] + hk, writes=[pgk])
                        for k in range(8):
                            P.op("pe", _L("matmul",
                                pu[:, :], lhsT=wb[:, k * 256 + 128:k * 256 + 256], rhs=h2T[:, k, tg * 512:(tg + 1) * 512],
                                start=(k == 0), stop=(k == 7)), reads=[wk] + hk, writes=[puk])
                        sg, sgk = sg_r.next()
                        P.op("act", _L("activation", out=sg[:], in_=pg[:, :], func=AF.Silu), writes=[pgk, sgk])
                        P.op("dve", _L("tensor_tensor",
                            out=aT[:, c, tg * 512:(tg + 1) * 512], in0=pu[:, :], in1=sg[:], op=ALU.mult),
                            reads=[sgk], writes=[puk, ("aT", c, tg)])
                    load_wd(c)
                P.barrier()
                chk("p7")

            with contextlib.ExitStack() as p8:
                GTf = sb("GTf", [128, D], F32, st=p8)
                xbuf = Rot(p8, nc, "xb8", 2, [128, D], F32)
                yt_r = Rot(p8, nc, "yt8", 1, [128, D], F32)
                ot_r = Rot(p8, nc, "ot8", 2, [128, D], F32)
                P.dma("sp", _L("dma_start", out=GTf[:], in_=modd[3]), reads=[("modd", 3)], writes=["GTf"])
                for tile in range(NO):
                    xt, xk = xbuf.next()
                    P.dma("sp", _L("dma_start", out=xt[:], in_=x1_d[tile * 128:(tile + 1) * 128, :]),
                          reads=[("x1d", tile)], writes=[xk])
                    yt, yk = yt_r.next()
                    for hf in range(2):
                        ps, pk = nxt("A")
                        for c in range(22):
                            P.op("pe", _L("matmul",
                                ps[:, :], lhsT=aT[:, c, tile * 128:(tile + 1) * 128], rhs=wd[:, c, hf * 512:(hf + 1) * 512],
                                start=(c == 0), stop=(c == 21)), reads=[("aT", c, tile // 4), ("wd", c)], writes=[pk])
                        P.op("dve", _L("tensor_tensor",
                            out=yt[:, hf * 512:(hf + 1) * 512], in0=ps[:, :], in1=GTf[:, hf * 512:(hf + 1) * 512], op=ALU.mult),
                            reads=["GTf"], writes=[pk, (yk, hf)])
                    ot, otk = ot_r.next()
                    P.op("pool", _L("tensor_tensor", out=ot[:], in0=yt[:], in1=xt[:], op=ALU.add),
                         reads=[(yk, 0), (yk, 1), xk], writes=[otk])
                    P.dma("sp", _L("dma_start", out=out_d[tile * 128:(tile + 1) * 128, :], in_=ot[:]),
                          reads=[otk], writes=[("outd", tile)])
                P.barrier()
        P.finalize()


def _rope_table(pos, dim):
    inv = (np.float32(10000.0) ** (-np.arange(0, dim, 2, dtype=np.float32) / np.float32(dim))).astype(np.float32)
    ang = pos.astype(np.float32)[:, None] * inv[None, :]
    c = np.cos(ang).astype(np.float32)
    s = np.sin(ang).astype(np.float32)
    return np.concatenate([c, c, -s, s], axis=1).astype(np.float32)


def kernel(x, c, w_mod, b_mod, g_mix_norm, g_ffn_norm, w_in, g_q_dsa, g_k_dsa, g_q_moba, g_k_moba,
           w_br_dsa, w_br_moba, w_out, w_gate_up, w_down, _ret_maps=False):
    f = lambda a: np.ascontiguousarray(np.asarray(a, dtype=np.float32))
    x = f(x); c = f(c)
    shared = {
        "w_mod": f(w_mod[0]), "b_mod": f(b_mod[0]).reshape(1, -1), "g_mix": f(g_mix_norm[0]).reshape(1, -1),
        "g_ffn": f(g_ffn_norm[0]).reshape(1, -1), "w_in": f(w_in[0]),
        "g_q_dsa": f(g_q_dsa[0]).reshape(1, -1), "g_k_dsa": f(g_k_dsa[0]).reshape(1, -1),
        "g_q_moba": f(g_q_moba[0]).reshape(1, -1), "g_k_moba": f(g_k_moba[0]).reshape(1, -1),
        "w_br_dsa": f(w_br_dsa[0]), "w_br_moba": f(w_br_moba[0]), "w_out": f(w_out[0]),
        "w_gate_up": f(w_gate_up[0]), "w_down": f(w_down[0]),
    }
    in_maps = []
    perms = []
    for core in range(8):
        b, p = core // 2, core % 2
        tiles = [2 * i + p for i in range(NO)] + [2 * i + 1 - p for i in range(NO)]
        pos = np.concatenate([np.arange(t * 128, (t + 1) * 128) for t in tiles])
        perms.append(pos)
        m = dict(shared)
        m["x"] = np.ascontiguousarray(x[b][pos])
        m["cT"] = np.ascontiguousarray(c[b].reshape(8, 128).T)
        m["ropeH"] = _rope_table(pos, 128)
        m["ropeI"] = _rope_table(pos, 64)
        m["pb"] = np.full((128, 128), 0.0 if p == 1 else -1e30, dtype=np.float32)
        in_maps.append(m)
    if _ret_maps:
        return in_maps, perms
    nc = build()
    res = run_bass_kernel_spmd(nc, in_maps, core_ids=list(range(8)))
    out = np.empty((4, S, D), dtype=np.float32)
    for core in range(8):
        b = core // 2
        out[b, perms[core][:NO * 128]] = res.results[core]["out"]
    return out
```

```python
import contextlib
import numpy as np
import concourse.bass as bass
import concourse.mybir as mybir
from concourse.bass_utils import run_bass_kernel_spmd

F32 = mybir.dt.float32
BF16 = mybir.dt.bfloat16
ALU = mybir.AluOpType
AF = mybir.ActivationFunctionType
AX = mybir.AxisListType

ENGS = ["pe", "act", "dve", "pool", "sp"]
DMA_SLOTS = {"sp": 24, "pool": 8, "act": 8}

D = 1024
S = 4096
NT = 32
NO = 16
DFF = 2816
IN_W = 5704
EPS = 1e-6
NB = 16
MNEG = -30000.0
ATT_SCALE = 128 ** -0.5
IDX_W_SCALE = (8 ** -0.5) * (64 ** -0.5)


class Prog:
    def __init__(self, nc):
        self.nc = nc
        self.ins = {e: [] for e in ENGS}
        self.last_w = {}
        self.readers = {}
        self.known = {e: {} for e in ENGS}
        self.dma_count = {q: 0 for q in DMA_SLOTS}
        self.dma_last = {}
        self.last_real = {}

    def _deps_for(self, reads, writes):
        deps = []
        for k in reads:
            w = self.last_w.get(k)
            if w is not None:
                deps.append(w)
        for k in writes:
            w = self.last_w.get(k)
            if w is not None:
                deps.append(w)
            deps.extend(self.readers.get(k, ()))
        return deps

    def _commit(self, me, reads, writes):
        for k in reads:
            self.readers.setdefault(k, []).append(me)
        for k in writes:
            self.last_w[k] = me
            self.readers[k] = []

    def _waits(self, eng, deps):
        kn = self.known[eng]
        best = {}
        for d in deps:
            if d[0] == "dma":
                key = d[:3]
                val = d[3]
            else:
                if d[0] == eng and eng == "pe":
                    continue
                key = d[0]
                val = d[1]
            if kn.get(key, -1) >= val:
                continue
            if best.get(key, -1) < val:
                best[key] = val
        out = []
        for key, val in best.items():
            kn[key] = val
            out.append((key, val))
        return out

    def op(self, eng, emit, reads=(), writes=()):
        seq = len(self.ins[eng])
        waits = self._waits(eng, self._deps_for(reads, writes))
        self.ins[eng].append(dict(emit=emit, waits=waits, signal=False, dma=None))
        self.last_real[eng] = seq
        self._commit((eng, seq), reads, writes)

    def dma(self, q, emit, reads=(), writes=()):
        n = self.dma_count[q]
        self.dma_count[q] = n + 1
        slot = n % DMA_SLOTS[q]
        gen = n // DMA_SLOTS[q]
        deps = self._deps_for(reads, writes)
        if gen > 0:
            deps.append(("dma", q, slot, gen - 1))
        waits = self._waits(q, deps)
        self.ins[q].append(dict(emit=emit, waits=waits, signal=False, dma=(q, slot, gen)))
        self.dma_last[(q, slot)] = gen
        self._commit(("dma", q, slot, gen), reads, writes)

    def barrier(self):
        deps = [(e, s) for e, s in self.last_real.items()]
        deps += [("dma", q, slot, gen) for (q, slot), gen in self.dma_last.items()]
        for e in ENGS:
            waits = self._waits(e, deps)
            if waits:
                self.ins[e].append(dict(emit=None, waits=waits, signal=False, dma=None))
        self.last_w = {}
        self.readers = {}

    def finalize(self):
        nc = self.nc
        self.barrier()
        for e in ENGS:
            for rec in self.ins[e]:
                for key, val in rec["waits"]:
                    if isinstance(key, str):
                        self.ins[key][val]["signal"] = True
        for e in ENGS:
            c = 0
            for rec in self.ins[e]:
                if rec["signal"]:
                    c += 1
                rec["cnt"] = c
        with contextlib.ExitStack() as st:
            esem = {e: st.enter_context(nc.semaphore("s_" + e)) for e in ENGS}
            dsem = {}
            for q, n in DMA_SLOTS.items():
                for s in range(n):
                    dsem[(q, s)] = st.enter_context(nc.semaphore(f"d_{q}_{s}"))
            block = st.enter_context(nc.Block())

            def run(ename):
                def f(eng):
                    for rec in self.ins[ename]:
                        for key, val in rec["waits"]:
                            if isinstance(key, str):
                                eng.wait_ge(esem[key], self.ins[key][val]["cnt"])
                            else:
                                _, q, slot = key
                                eng.wait_ge(dsem[(q, slot)], 16 * (val + 1))
                        if rec["emit"] is None:
                            continue
                        bi = rec["emit"](eng)
                        if rec["dma"] is not None:
                            q, slot, gen = rec["dma"]
                            bi.then_inc(dsem[(q, slot)], 16)
                        elif rec["signal"]:
                            bi.then_inc(esem[ename], 1)
                return f

            block.tensor(run("pe"))
            block.scalar(run("act"))
            block.vector(run("dve"))
            block.gpsimd(run("pool"))
            block.sync(run("sp"))


def _L(name, *args, **kwargs):
    def f(eng):
        return getattr(eng, name)(*args, **kwargs)
    return f


def run_pipeline(items, stages):
    n, ns = len(items), len(stages)
    ctxs = [dict() for _ in items]
    for step in range(n + ns - 1):
        for si, fn in enumerate(stages):
            j = step - si
            if 0 <= j < n:
                fn(items[j], ctxs[j])


def pipeline_gen(items, stages):
    n, ns = len(items), len(stages)
    ctxs = [dict() for _ in items]
    for step in range(n + ns - 1):
        for si, fn in enumerate(stages):
            j = step - si
            if 0 <= j < n:
                fn(items[j], ctxs[j])
        yield


def interleave_all(*gens):
    gens = [g for g in gens if g is not None]
    while gens:
        for g in list(gens):
            try:
                next(g)
            except StopIteration:
                gens.remove(g)


class Rot:
    def __init__(self, st, nc, name, n, shape, dt):
        self.t = [st.enter_context(nc.sbuf_tensor(f"{name}{i}", shape, dt)) for i in range(n)]
        self.k = [f"{name}{i}" for i in range(n)]
        self.i = 0

    def next(self):
        j = self.i % len(self.t)
        self.i += 1
        return self.t[j], self.k[j]


STOP = None
DEBUG = False


class _Stop(Exception):
    pass


def build():
    nc = bass.Bass("TRN2", target_bir_lowering=False)
    P = Prog(nc)
    try:
        _build(nc, P)
    except _Stop:
        pass
    return nc


def _build(nc, P):

    def chk(name):
        if STOP == name:
            P.finalize()
            raise _Stop()

    def din(name, shape):
        return nc.dram_tensor(name, shape, F32, kind="ExternalInput")

    x_d = din("x", [S, D])
    cT_d = din("cT", [128, 8])
    wmod_d = din("w_mod", [D, 6 * D])
    bmod_d = din("b_mod", [1, 6 * D])
    gmix_d = din("g_mix", [1, D])
    gffn_d = din("g_ffn", [1, D])
    win_d = din("w_in", [D, IN_W])
    gq_d = {"a": din("g_q_dsa", [1, 128]), "b": din("g_q_moba", [1, 128])}
    gk_d = {"a": din("g_k_dsa", [1, 128]), "b": din("g_k_moba", [1, 128])}
    wbr_d = {"a": din("w_br_dsa", [512, D]), "b": din("w_br_moba", [512, D])}
    wout_d = din("w_out", [D, D])
    wgu_d = din("w_gate_up", [D, 2 * DFF])
    wdn_d = din("w_down", [DFF, D])
    ropeH_d = din("ropeH", [S, 256])
    ropeI_d = din("ropeI", [S, 128])
    pb_d = din("pb", [128, 128])
    out_d = nc.dram_tensor("out", [NO * 128, D], F32, kind="ExternalOutput")

    def scr(name, shape, dt):
        return nc.dram_tensor(name, shape, dt, kind=("ExternalOutput" if DEBUG else "Internal"))

    KT_d = {"a": scr("KaT_d", [NT, 128, 512], BF16), "b": scr("KbT_d", [NT, 128, 512], BF16)}
    V_d = {"a": scr("Va_d", [NT, 128, 528], BF16), "b": scr("Vb_d", [NT, 128, 528], BF16)}
    QT_d = {"a": scr("QaT_d", [NO, 128, 512], BF16), "b": scr("QbT_d", [NO, 128, 512], BF16)}
    qiT_d = scr("qiT_d", [NO, 128, 512], BF16)
    kiT_d = scr("kiT_d", [NT, 128, 128], BF16)
    g_d = {"a": scr("gA_d", [8, 128, NO * 128], BF16), "b": scr("gB_d", [8, 128, NO * 128], BF16)}
    oT_d = {"a": scr("oaT_d", [NO, 128, 512], BF16), "b": scr("obT_d", [NO, 128, 512], BF16)}
    modd = scr("modd", [4, 128, D], F32)
    x1_d = scr("x1_d", [NO * 128, D], F32)

    def A(t, off, pat):
        return bass.AP(t, off, [list(p) for p in pat])

    with contextlib.ExitStack() as top:
        def sb(name, shape, dt, st=top):
            return st.enter_context(nc.sbuf_tensor(name, shape, dt))

        def psum(name, shape, dt):
            return top.enter_context(nc.psum_tensor(name, shape, dt))

        pA = [psum(f"pA{i}", [128, 512], F32) for i in range(2)]
        pB = [psum(f"pB{i}", [128, 512], F32) for i in range(2)]
        pO = [psum(f"pO{i}", [128, 512], F32) for i in range(2)]
        pT = [psum(f"pT{i}", [128, 1024], BF16) for i in range(1)]
        pC = [psum(f"pC{i}", [128, 512], F32) for i in range(1)]
        pcnt = {"A": 0, "B": 0, "O": 0, "T": 0, "C": 0}

        def nxt(which):
            arr = {"A": pA, "B": pB, "O": pO, "T": pT, "C": pC}[which]
            j = pcnt[which] % len(arr)
            pcnt[which] += 1
            return arr[j], f"p{which}{j}"

        ident_f = sb("ident_f", [128, 128], F32)
        ident_b = sb("ident_b", [128, 128], BF16)
        caus_f = sb("caus_f", [128, 128], F32)
        caus_b = sb("caus_b", [128, 128], BF16)
        pb_f = sb("pb_f", [128, 128], F32)
        pb_b = sb("pb_b", [128, 128], BF16)
        pow2c = sb("pow2c", [128, NB + 1], F32)
        wi = sb("wi", [128, NO, 8], F32)
        ssm = sb("ssm", [128, 8], F32)
        eps_t = sb("eps_t", [128, 1], F32)

        P.op("pool", _L("memset", ident_f[:], 1.0), writes=["ident_f"])
        P.op("pool", _L("affine_select", out=ident_f[:], in_=ident_f[:], pattern=[[-1, 128]],
                                                compare_op=ALU.is_equal, fill=0.0, base=0, channel_multiplier=1),
             writes=["ident_f"])
        P.op("pool", _L("tensor_copy", out=ident_b[:], in_=ident_f[:]), reads=["ident_f"], writes=["ident_b"])
        P.op("pool", _L("memset", caus_f[:], 0.0), writes=["caus_f"])
        P.op("pool", _L("affine_select", out=caus_f[:], in_=caus_f[:], pattern=[[-1, 128]],
                                                compare_op=ALU.is_ge, fill=-1e30, base=0, channel_multiplier=1),
             writes=["caus_f"])
        P.op("pool", _L("tensor_scalar", out=caus_b[:], in0=caus_f[:], scalar1=MNEG, scalar2=None, op0=ALU.max),
             reads=["caus_f"], writes=["caus_b"])
        P.dma("sp", _L("dma_start", out=pb_f[:], in_=pb_d.ap()), writes=["pb_f"])
        P.op("pool", _L("tensor_scalar", out=pb_b[:], in0=pb_f[:], scalar1=MNEG, scalar2=None, op0=ALU.max),
             reads=["pb_f"], writes=["pb_b"])
        P.op("pool", _L("memset", eps_t[:], EPS), writes=["eps_t"])
        for n in range(NB + 1):
            P.op("pool", _L("memset", pow2c[:, n:n + 1], 2.0 ** -(n + 1)), writes=["pow2c"])

        with contextlib.ExitStack() as ph:
            def sbp(name, shape, dt):
                return sb(name, shape, dt, st=ph)

            G1m = sbp("G1m", [128, D], F32)
            SHm = sbp("SHm", [128, D], F32)

            hT = sbp("hT", [128, 8, S], BF16)
            xbuf = Rot(ph, nc, "xbuf", 3, [128, D], F32)
            tmpf = Rot(ph, nc, "tmpf", 2, [128, D], F32)
            hb = Rot(ph, nc, "hb", 3, [128, D], BF16)
            junk = sbp("junkb", [128, D], BF16)
            mkeys = ["SHm0", "SHm1", "G1m0", "G1m1"]
            ssm_r = Rot(ph, nc, "ssmr", 3, [128, 8], F32)

            def p1_s0(t, c):
                xt, xk = xbuf.next()
                P.dma("sp", _L("dma_start", out=xt[:], in_=x_d[t * 128:(t + 1) * 128, :]), writes=[xk])
                sm, smk = ssm_r.next()
                c["x"], c["sm"] = (xt, xk), (sm, smk)
                P.op("act", _L("activation", out=junk[:], in_=xt[:], func=AF.Square, accum_out=sm[:, 0:1]),
                     reads=[xk], writes=["junkb", smk])
                P.op("dve", _L("tensor_scalar", out=sm[:, 1:2], in0=sm[:, 0:1], scalar1=1.0 / D, scalar2=EPS,
                               op0=ALU.mult, op1=ALU.add), writes=[smk])
                P.op("act", _L("activation", out=sm[:, 2:3], in_=sm[:, 1:2], func=AF.Ln), writes=[smk])
                P.op("act", _L("activation", out=sm[:, 3:4], in_=sm[:, 2:3], func=AF.Exp, scale=-0.5), writes=[smk])

            def p1_s1(t, c):
                (xt, xk), (sm, smk) = c["x"], c["sm"]
                tf, tk = tmpf.next()
                P.op("dve", _L("scalar_tensor_tensor", out=tf[:], in0=xt[:], scalar=sm[:, 3:4], in1=G1m[:],
                               op0=ALU.mult, op1=ALU.mult), reads=[xk, smk] + mkeys, writes=[tk])
                hbt, hk = hb.next()
                P.op("dve", _L("tensor_tensor", out=hbt[:], in0=tf[:], in1=SHm[:], op=ALU.add),
                     reads=[tk] + mkeys, writes=[hk])
                c["hb"] = (hbt, hk)

            def p1_s2(t, c):
                hbt, hk = c["hb"]
                pt, pk = nxt("T")
                for k in range(8):
                    P.op("pe", _L("transpose", out=pt[:, k * 128:(k + 1) * 128], in_=hbt[:, k * 128:(k + 1) * 128],
                                  identity=ident_b[:]), reads=[hk, "ident_b"], writes=[pk])
                P.op("act", _L("copy", out=A(hT, t * 128, [[8 * S, 128], [S, 8], [1, 128]]),
                               in_=A(pt, 0, [[1024, 128], [128, 8], [1, 128]])), writes=[pk, ("hT", t)])

            with contextlib.ExitStack() as p0:
                wstg = Rot(p0, nc, "wstg0", 2, [128, 8 * 512], F32)
                sc_t = sb("sc_t", [128, 8], F32, st=p0)
                scb = sb("scb", [128, 8 * 128], F32, st=p0)
                bm = sb("bm", [128, 6 * D], F32, st=p0)
                gmx = sb("gmx", [128, D], F32, st=p0)
                gff = sb("gff", [128, D], F32, st=p0)
                mt = [sb(f"mt{i}", [128, D], F32, st=p0) for i in range(4)]
                P.dma("sp", _L("dma_start", out=sc_t[:], in_=cT_d.ap()), writes=["sc_t"])
                P.dma("sp", _L("dma_start", out=bm[:], in_=A(bmod_d, 0, [[0, 128], [1, 6 * D]])), writes=["bm"])
                P.dma("sp", _L("dma_start", out=gmx[:], in_=A(gmix_d, 0, [[0, 128], [1, D]])), writes=["gmx"])
                P.dma("sp", _L("dma_start", out=gff[:], in_=A(gffn_d, 0, [[0, 128], [1, D]])), writes=["gff"])
                P.op("act", _L("activation", out=sc_t[:], in_=sc_t[:], func=AF.Silu), reads=["sc_t"], writes=["sc_t"])
                P.op("dve", _L("tensor_copy", out=A(scb, 0, [[1024, 128], [128, 8], [1, 128]]),
                                                    in_=A(sc_t, 0, [[8, 128], [1, 8], [0, 128]])),
                     reads=["sc_t"], writes=["scb"])
                wm_src = wmod_d.ap().rearrange("(k p) n -> p k n", p=128)
                dest = {0: (SHm, "SHm", None), 1: (G1m, "G1m", gmx), 2: (mt[0], "mt0", None),
                        3: (mt[1], "mt1", None), 4: (mt[2], "mt2", gff), 5: (mt[3], "mt3", None)}
                def p0_chunks():
                  for nch in range(12):
                    stg, sk = wstg.next()
                    sv = A(stg, 0, [[4096, 128], [512, 8], [1, 512]])
                    P.dma("sp", _L("dma_start", out=sv, in_=wm_src[:, :, nch * 512:(nch + 1) * 512]),
                          writes=[sk])
                    ps, pk = nxt("A")
                    for k in range(8):
                        P.op("pe", _L("matmul",
                            ps[:, :], lhsT=scb[:, k * 128:(k + 1) * 128], rhs=stg[:, k * 512:(k + 1) * 512],
                            start=(k == 0), stop=(k == 7)), reads=[sk, "scb"], writes=[pk])
                    m, half = nch // 2, nch % 2
                    dt_, dk, gg = dest[m]
                    P.op("dve", _L("tensor_tensor",
                        out=dt_[:, half * 512:(half + 1) * 512], in0=ps[:, :], in1=bm[:, nch * 512:(nch + 1) * 512],
                        op=ALU.add), reads=["bm"], writes=[pk, dk + str(half)])
                    if gg is not None:
                        P.op("dve", _L("scalar_tensor_tensor",
                            out=dt_[:, half * 512:(half + 1) * 512], in0=dt_[:, half * 512:(half + 1) * 512], scalar=1.0,
                            in1=gg[:, half * 512:(half + 1) * 512], op0=ALU.add, op1=ALU.mult),
                            reads=["gmx", "gff"], writes=[dk + str(half)])
                    yield

                g0 = p0_chunks()
                for _ in range(4):
                    next(g0)
                interleave_all(g0, pipeline_gen(list(range(NT)), [p1_s0, p1_s1, p1_s2]))
                for i in range(4):
                    P.dma("sp", _L("dma_start", out=modd[i], in_=mt[i][:]), reads=[f"mt{i}0", f"mt{i}1"],
                          writes=[("modd", i)])
                P.barrier()
                chk("p0")

            wstg = Rot(ph, nc, "wstg", 1, [128, 8 * 512], F32)
            wbf = Rot(ph, nc, "wbf", 4, [128, 8 * 512], BF16)


            P.barrier()
            chk("p1")
            win_src = win_d.ap().rearrange("(k p) n -> p k n", p=128)
            xs_r = Rot(ph, nc, "xs", 4, [128, 512], F32)
            sq_r = Rot(ph, nc, "sq", 2, [128, 512], F32)
            xg_r = Rot(ph, nc, "xg", 2, [128, 512], F32)
            t1_r = Rot(ph, nc, "t1", 3, [128, 512], F32)
            t2_r = Rot(ph, nc, "t2", 3, [128, 512], F32)
            ob_r = Rot(ph, nc, "ob", 3, [128, 512], BF16)
            st_r = Rot(ph, nc, "stT", 3, [128, 512], BF16)
            rt_r = Rot(ph, nc, "rt", 4, [128, 256], F32)
            vst_r = Rot(ph, nc, "vst", 2, [128, 528], BF16)
            gbc = sbp("gbc", [128, 128], F32)
            ss4_r = Rot(ph, nc, "ss4r", 3, [128, 16], F32)
            for vt in vst_r.t:
                P.op("pool", _L("memset", vt[:], 0.0), writes=["vinit"])
                P.op("pool", _L("memset", A(vt, 128, [[528, 128], [132, 4], [1, 1]]), 1.0), writes=["vinit"])
            P.barrier()

            pref_w = {}

            def prefetch_w(*specs):
                for (col0, ncols) in specs:
                    pref_w[(col0, ncols)] = load_w(col0, ncols)

            def load_w(col0, ncols):
                if (col0, ncols) in pref_w:
                    return pref_w.pop((col0, ncols))
                stg, sk = wstg.next()
                wb, wk = wbf.next()
                sv = A(stg, 0, [[4096, 128], [ncols, 8], [1, ncols]])
                wv = A(wb, 0, [[4096, 128], [ncols, 8], [1, ncols]])
                P.dma("sp", _L("dma_start", out=sv, in_=win_src[:, :, col0:col0 + ncols]), writes=[sk])
                P.op("act", _L("copy", out=wv, in_=sv), reads=[sk], writes=[wk])
                return wb, wk

            def proj_tile(t, wb, wk, ncols):
                ps, pk = nxt("A")
                for k in range(8):
                    P.op("pe", _L("matmul", ps[:, 0:ncols], lhsT=hT[:, k, t * 128:(t + 1) * 128],
                                                       rhs=wb[:, k * ncols:(k + 1) * ncols], start=(k == 0), stop=(k == 7)),
                         reads=[("hT", t), wk], writes=[pk])
                return ps, pk

            def rope(xin, xk_, nh, hd, rt, rk, defer_add=False):
                hh = hd // 2
                t1, k1 = t1_r.next()
                t2, k2 = t2_r.next()
                ob, ok = ob_r.next()
                W = nh * hd
                nh1 = nh // 2 if nh >= 2 else nh
                P.op("dve", _L("tensor_tensor", out=A(t1, 0, [[512, 128], [hd, nh1], [1, hd]]),
                                                       in0=A(xin, 0, [[512, 128], [hd, nh1], [1, hd]]),
                                                       in1=A(rt, 0, [[256, 128], [0, nh1], [1, hd]]), op=ALU.mult),
                     reads=[xk_, rk], writes=[(k1, 0)])
                if nh1 < nh:
                    P.op("pool", _L("tensor_tensor", out=A(t1, nh1 * hd, [[512, 128], [hd, nh - nh1], [1, hd]]),
                                                           in0=A(xin, nh1 * hd, [[512, 128], [hd, nh - nh1], [1, hd]]),
                                                           in1=A(rt, 0, [[256, 128], [0, nh - nh1], [1, hd]]), op=ALU.mult),
                         reads=[xk_, rk], writes=[(k1, 1)])
                P.op("pool", _L("tensor_tensor", out=A(t2, 0, [[512, 128], [hd, nh], [1, hh]]),
                                                       in0=A(xin, hh, [[512, 128], [hd, nh], [1, hh]]),
                                                       in1=A(rt, hd, [[256, 128], [0, nh], [1, hh]]), op=ALU.mult),
                     reads=[xk_, rk], writes=[k2])
                P.op("pool", _L("tensor_tensor", out=A(t2, hh, [[512, 128], [hd, nh], [1, hh]]),
                                                       in0=A(xin, 0, [[512, 128], [hd, nh], [1, hh]]),
                                                       in1=A(rt, hd + hh, [[256, 128], [0, nh], [1, hh]]), op=ALU.mult),
                     reads=[xk_, rk], writes=[k2])
                def fin():
                    P.op("dve", _L("tensor_tensor", out=ob[:, 0:W], in0=t1[:, 0:W], in1=t2[:, 0:W], op=ALU.add),
                         reads=[(k1, 0), (k1, 1), k2], writes=[ok])
                    return ob, ok
                if defer_add:
                    return fin
                return fin()

            def transpose_store(ob, ok, nblk, dst_ap):
                pt, pk = nxt("T")
                for h in range(nblk):
                    P.op("pe", _L("transpose", out=pt[:, h * 128:(h + 1) * 128], in_=ob[:, h * 128:(h + 1) * 128],
                                                          identity=ident_b[:]), reads=[ok, "ident_b"], writes=[pk])
                stt, sk = st_r.next()
                P.op("act", _L("copy", out=stt[:, 0:nblk * 128], in_=pt[:, 0:nblk * 128]), writes=[pk, sk])
                P.dma("act", _L("dma_start", out=dst_ap, in_=stt[:, 0:nblk * 128]), reads=[sk], writes=[("scr", id(dst_ap))])

            qbanks = [(pA[0], "pA0"), (pA[1], "pA1"), (pB[0], "pB0"), (pB[1], "pB1")]
            qcnt = [0]

            def qk_group(col0, tiles, g_dram, dst, as_gen=False):
                wb, wk = load_w(col0, 512)
                P.dma("sp", _L("dma_start", out=gbc[:], in_=A(g_dram, 0, [[0, 128], [1, 128]])), writes=["gbc"])

                def s0(t, c):
                    ps, pk = qbanks[qcnt[0] % 4]
                    qcnt[0] += 1
                    for k in range(8):
                        P.op("pe", _L("matmul", ps[:, 0:512], lhsT=hT[:, k, t * 128:(t + 1) * 128],
                                      rhs=wb[:, k * 512:(k + 1) * 512], start=(k == 0), stop=(k == 7)),
                             reads=[("hT", t), wk], writes=[pk])
                    ss4, s4k = ss4_r.next()
                    for h in range(4):
                        P.op("act", _L("activation", out=junk[:, 0:128], in_=ps[:, h * 128:(h + 1) * 128], func=AF.Square,
                                       accum_out=ss4[:, h:h + 1]), writes=[pk, "junkb", s4k])
                    rt, rk = rt_r.next()
                    P.dma("sp", _L("dma_start", out=rt[:], in_=ropeH_d[t * 128:(t + 1) * 128, :]), writes=[rk])
                    c["ps"], c["rt"], c["ss4"] = (ps, pk), (rt, rk), (ss4, s4k)

                def s1(t, c):
                    ss4, s4k = c["ss4"]
                    P.op("act", _L("activation", out=ss4[:, 8:12], in_=ss4[:, 0:4], func=AF.Ln, scale=1.0 / 128,
                                   bias=eps_t[:, 0:1]), reads=["eps_t"], writes=[s4k])
                    P.op("act", _L("activation", out=ss4[:, 12:16], in_=ss4[:, 8:12], func=AF.Exp, scale=-0.5), writes=[s4k])

                def s2a(t, c):
                    (ps, pk), (ss4, s4k), (rt, rk) = c["ps"], c["ss4"], c["rt"]
                    xg, gk_ = xg_r.next()
                    for h in range(4):
                        P.op("dve", _L("scalar_tensor_tensor", out=xg[:, h * 128:(h + 1) * 128], in0=ps[:, h * 128:(h + 1) * 128],
                                       scalar=ss4[:, 12 + h:13 + h], in1=gbc[:], op0=ALU.mult, op1=ALU.mult),
                             reads=[s4k, "gbc"], writes=[pk, gk_])
                    c["rp"] = rope(xg, gk_, 4, 128, rt, rk, defer_add=True)

                def s2b(t, c):
                    c["ob"] = c["rp"]()

                def s3(t, c):
                    ob, ok = c["ob"]
                    transpose_store(ob, ok, 4, dst[t])

                if as_gen:
                    return pipeline_gen(list(tiles), [s0, s1, s2a, s2b, s3])
                run_pipeline(list(tiles), [s0, s1, s2a, s2b, s3])

            def v_group(col0, dst):
                wb, wk = load_w(col0, 512)
                for t in range(NT):
                    ps, pk = nxt("O")
                    for k in range(8):
                        P.op("pe", _L("matmul", ps[:, 0:512], lhsT=hT[:, k, t * 128:(t + 1) * 128],
                                      rhs=wb[:, k * 512:(k + 1) * 512], start=(k == 0), stop=(k == 7)),
                             reads=[("hT", t), wk], writes=[pk])
                    vt, vk = vst_r.next()
                    P.op("act", _L("copy", out=A(vt, 0, [[528, 128], [132, 4], [1, 128]]),
                                   in_=A(ps, 0, [[512, 128], [128, 4], [1, 128]])), writes=[pk, vk])
                    P.dma("act", _L("dma_start", out=dst[t], in_=vt[:]), reads=[vk], writes=[("vscr", t, id(dst))])
                    yield

            def qi_group():
                wb, wk = load_w(1536, 512)

                def s0(t, c):
                    ps, pk = proj_tile(t, wb, wk, 512)
                    xs, xk_ = xs_r.next()
                    P.op("act", _L("copy", out=xs[:], in_=ps[:, :]), writes=[pk, xk_])
                    rt, rk = rt_r.next()
                    P.dma("sp", _L("dma_start", out=rt[:, 0:128], in_=ropeI_d[t * 128:(t + 1) * 128, :]), writes=[rk])
                    c["xs"], c["rt"] = (xs, xk_), (rt, rk)

                def s1(t, c):
                    (xs, xk_), (rt, rk) = c["xs"], c["rt"]
                    c["ob"] = rope(xs, xk_, 8, 64, rt, rk)

                def s2(t, c):
                    ob, ok = c["ob"]
                    transpose_store(ob, ok, 4, qiT_d[t])

                return pipeline_gen(list(range(NO)), [s0, s1, s2])

            def kiwi_group():
                wb, wk = load_w(2048, 72)

                def s0(t, c):
                    ps, pk = proj_tile(t, wb, wk, 72)
                    xs, xk_ = xs_r.next()
                    P.op("act", _L("copy", out=xs[:, 0:72], in_=ps[:, 0:72]), writes=[pk, xk_])
                    if t < NO:
                        P.op("dve", _L("tensor_scalar", out=wi[:, t, :], in0=xs[:, 64:72], scalar1=IDX_W_SCALE,
                                       scalar2=None, op0=ALU.mult), reads=[xk_], writes=[("wi", t)])
                    rt, rk = rt_r.next()
                    P.dma("sp", _L("dma_start", out=rt[:, 0:128], in_=ropeI_d[t * 128:(t + 1) * 128, :]), writes=[rk])
                    c["xs"], c["rt"] = (xs, xk_), (rt, rk)

                def s1(t, c):
                    (xs, xk_), (rt, rk) = c["xs"], c["rt"]
                    ob, ok = rope(xs, xk_, 1, 64, rt, rk)
                    P.op("dve", _L("tensor_copy", out=ob[:, 64:128], in_=ob[:, 0:64]), reads=[ok], writes=[ok])
                    c["ob"] = (ob, ok)

                def s2(t, c):
                    ob, ok = c["ob"]
                    transpose_store(ob, ok, 1, kiT_d[t])

                return pipeline_gen(list(range(NT)), [s0, s1, s2])

            def gate_group(col0, dst):
                for hf in range(2):
                    wb, wk = load_w(col0 + hf * 512, 512)
                    for fcl in range(4):
                        fc = hf * 4 + fcl
                        for tg in range(4):
                            ps, pk = nxt("O")
                            for k in range(8):
                                P.op("pe", _L("matmul",
                                    ps[:, :], lhsT=wb[:, k * 512 + fcl * 128:k * 512 + (fcl + 1) * 128],
                                    rhs=hT[:, k, tg * 512:(tg + 1) * 512], start=(k == 0), stop=(k == 7)),
                                    reads=[wk] + [("hT", tg * 4 + j) for j in range(4)], writes=[pk])
                            stt, sk = st_r.next()
                            P.op("act", _L("activation", out=stt[:], in_=ps[:, :], func=AF.Sigmoid), writes=[pk, sk])
                            P.dma("act", _L("dma_start", out=dst[fc][:, tg * 512:(tg + 1) * 512], in_=stt[:]),
                                  reads=[sk], writes=[("gscr", fc, tg, id(dst))])
                            yield

            own = list(range(NO))
            alls = list(range(NT))
            def run_pf(gens, specs):
                for _ in range(3):
                    for g in gens:
                        next(g, None)
                prefetch_w(*specs)
                interleave_all(*gens)

            prefetch_w((0, 512))
            run_pf([qk_group(0, own, gq_d["a"], QT_d["a"], as_gen=True)], [(512, 512), (1024, 512)])
            chk("p2_1")
            run_pf([qk_group(512, alls, gk_d["a"], KT_d["a"], as_gen=True), v_group(1024, V_d["a"])], [(1536, 512)])
            chk("p2_2")
            run_pf([qi_group()], [(2048, 72)])
            chk("p2_4")
            run_pf([kiwi_group()], [(2120, 512)])
            chk("p2_5")
            run_pf([qk_group(2120, own, gq_d["b"], QT_d["b"], as_gen=True)], [(2632, 512), (3144, 512)])
            run_pf([qk_group(2632, alls, gk_d["b"], KT_d["b"], as_gen=True), v_group(3144, V_d["b"])],
                   [(3656, 512), (4168, 512)])
            run_pf([gate_group(3656, g_d["a"])], [(4680, 512), (5192, 512)])
            chk("p2a")
            run_pf([gate_group(4680, g_d["b"])], [])
            P.barrier()
            chk("p2")

        with contextlib.ExitStack() as ph:
            def sbp(name, shape, dt):
                return sb(name, shape, dt, st=ph)

            KTs = {"a": sbp("KTa", [128, NT, 512], BF16)}
            Vs = {"a": sbp("Va", [128, NT, 528], BF16)}
            kv_r = Rot(ph, nc, "kvb", 4, [128, 2 * 512 + 2 * 528], BF16)
            kiT = sbp("kiT", [128, NT * 128], BF16)
            sc_r = Rot(ph, nc, "sc", 2, [128, S], F32)
            oaT_r = Rot(ph, nc, "oaT", 2, [128, 512], BF16)
            MB_r = Rot(ph, nc, "MB", 3, [128, S], BF16)
            rl_r = Rot(ph, nc, "rl", 4, [128, 512], BF16)
            Dw_r = Rot(ph, nc, "Dw", 2, [128, 8 * 128], BF16)
            PT_r = Rot(ph, nc, "PT", 4, [128, 512], BF16)
            qt_r = Rot(ph, nc, "qt", 4, [128, 512], BF16)
            qi_r = Rot(ph, nc, "qit", 2, [128, 512], BF16)
            oa_r = Rot(ph, nc, "oa", 3, [128, 512], BF16)
            bs = sbp("bs", [128, 8], F32)
            Wn = sbp("Wn", [128, NB + 1], F32)
            rden = sbp("rden", [128, 8], F32)
            ksum = sbp("ksum", [128, 128], F32)
            kmf = sbp("kmf", [128, 64], F32)
            kmT = sbp("kmT", [128, 64], BF16)
            gsb = sbp("gsb", [128, 64], F32)
            m8 = sbp("m8", [128, 32], F32)
            mbk = sbp("mbk", [128, 64], F32)
            mbkb = sbp("mbkb", [128, 64], BF16)

            def load_kv(kind):
                order = [(s__ // 2) + 16 * (s__ % 2) for s__ in range(NT)]
                for s_ in order:
                    P.dma("sp", _L("dma_start", out=KTs[kind][:, s_, :], in_=KT_d[kind][s_]), writes=[("KT" + kind, s_)])
                for s_ in order:
                    P.dma("sp", _L("dma_start", out=Vs[kind][:, s_, :], in_=V_d[kind][s_]), writes=[("V" + kind, s_)])

            def slot_of(i, cb):
                return cb if cb <= i else 16 + (cb - (i + 1))

            def attention(i, qt, qk_, bias_fn, dstT):
                ncb = 2 * (i + 1)
                oa, ok = oa_r.next()
                units = []
                for h in range(4):
                    for g0 in range(0, ncb, 4):
                        units.append((h, list(range(g0, min(g0 + 4, ncb)))))
                Obank = {}
                pend_evac = []

                def emit_st(u):
                    h, cbs = u
                    ST, Sk = pB[0], "pB0"
                    for e_, cb in enumerate(cbs):
                        s_ = slot_of(i, cb)
                        bias = bias_fn(h, cb)
                        P.op("pe", _L("matmul", ST[:, e_ * 128:(e_ + 1) * 128], lhsT=KTs["a"][:, s_, h * 128:(h + 1) * 128],
                                      rhs=qt[:, h * 128:(h + 1) * 128], start=True, stop=(bias is None)),
                             reads=[("KTa", s_), qk_], writes=[Sk])
                        if bias is not None:
                            P.op("pe", _L("matmul", ST[:, e_ * 128:(e_ + 1) * 128], lhsT=bias[0], rhs=bias[1],
                                          start=False, stop=True), reads=bias[2], writes=[Sk])
                    n = len(cbs) * 128
                    pt, pk = PT_r.next()
                    P.op("act", _L("activation", out=pt[:, 0:n], in_=ST[:, 0:n], func=AF.Exp, scale=ATT_SCALE),
                         writes=[Sk, pk])
                    return pt, pk

                def emit_pv(u, ptk):
                    h, cbs = u
                    pt, pk = ptk
                    if h not in Obank:
                        Obank[h] = (pO[0], "pO0")
                    O, Ok = Obank[h]
                    for e_, cb in enumerate(cbs):
                        s_ = slot_of(i, cb)
                        P.op("pe", _L("matmul", O[:, 0:129], lhsT=pt[:, e_ * 128:(e_ + 1) * 128],
                                      rhs=Vs["a"][:, s_, h * 132:h * 132 + 129], start=(cb == 0), stop=(cb == ncb - 1)),
                             reads=[pk, ("Va", s_)], writes=[Ok])
                    if cbs[-1] == ncb - 1:
                        def evac(h=h, O=O, Ok=Ok):
                            P.op("dve", _L("reciprocal", out=rden[:, h:h + 1], in_=O[:, 128:129]), writes=[Ok, ("rden", h)])
                            P.op("dve", _L("tensor_scalar", out=oa[:, h * 128:(h + 1) * 128], in0=O[:, 0:128],
                                           scalar1=rden[:, h:h + 1], scalar2=None, op0=ALU.mult),
                                 reads=[("rden", h)], writes=[Ok, ok])
                        pend_evac.append(evac)
                        while pend_evac:
                            pend_evac.pop(0)()

                prev = None
                for u in units:
                    ptk = emit_st(u)
                    if prev is not None:
                        emit_pv(*prev)
                    prev = (u, ptk)
                    yield
                emit_pv(*prev)
                yield
                while pend_evac:
                    pend_evac.pop(0)()
                pt_, pk_ = nxt("T")
                for h in range(4):
                    P.op("pe", _L("transpose", out=pt_[:, h * 128:(h + 1) * 128], in_=oa[:, h * 128:(h + 1) * 128],
                                  identity=ident_b[:]), reads=[ok, "ident_b"], writes=[pk_])
                ost, osk = oaT_r.next()
                P.op("act", _L("copy", out=ost[:], in_=pt_[:, 0:512]), writes=[pk_, osk])
                P.dma("act", _L("dma_start", out=oT_d[dstT][i], in_=ost[:]), reads=[osk], writes=[("oT", dstT, i)])
                yield

            def interleave(*gens):
                gens = [g for g in gens if g is not None]
                while gens:
                    for g in list(gens):
                        try:
                            next(g)
                        except StopIteration:
                            gens.remove(g)

            load_kv("a")
            for s_ in range(NT):
                P.dma("sp", _L("dma_start", out=kiT[:, s_ * 128:(s_ + 1) * 128], in_=kiT_d[s_]), writes=[("ki", s_)])

            def dsa_scores(i, scres):
                L = (i + 1) * 128
                sc, sk = scres
                qit, qik = qi_r.next()
                P.dma("sp", _L("dma_start", out=qit[:], in_=qiT_d[i]), writes=[qik])
                Dw, Dk = Dw_r.next()
                for h in range(8):
                    P.op("pool", _L("tensor_scalar", out=Dw[:, h * 128:(h + 1) * 128], in0=ident_b[:],
                                   scalar1=wi[:, i, h:h + 1], scalar2=None, op0=ALU.mult),
                         reads=["ident_b", ("wi", i)], writes=[(Dk, h)])
                for seg in range(2):
                    base = seg * 2048
                    for c0 in range(0, L, 512):
                        n = min(512, L - c0)
                        kis = [("ki", seg * 16 + (c0 + j) // 128) for j in range(0, n, 128)]
                        acc, acck = nxt("C")
                        pend = []
                        for hp in range(4):
                            cur = []
                            for hf in range(2):
                                h = 2 * hp + hf
                                ps, pk = nxt("A")
                                P.op("pe", _L("matmul", ps[:, 0:n], lhsT=qit[64 * hf:64 * hf + 64, hp * 128:(hp + 1) * 128],
                                              rhs=kiT[64 * hf:64 * hf + 64, base + c0:base + c0 + n], start=True, stop=True),
                                     reads=[qik] + kis, writes=[pk])
                                cur.append((h, ps, pk))
                            for (h, ps, pk) in cur:
                                rl, rk = rl_r.next()
                                P.op("act", _L("activation", out=rl[:, 0:n], in_=ps[:, 0:n], func=AF.Relu), writes=[pk, rk])
                                pend.append((_L("matmul", acc[:, 0:n], lhsT=Dw[:, h * 128:(h + 1) * 128], rhs=rl[:, 0:n],
                                                start=(h == 0), stop=(h == 7)), [(Dk, h), rk]))
                            while len(pend) > 2:
                                e_, r_ = pend.pop(0)
                                P.op("pe", e_, reads=r_, writes=[acck])
                            if hp % 2 == 1:
                                yield
                        while pend:
                            e_, r_ = pend.pop(0)
                            P.op("pe", e_, reads=r_, writes=[acck])
                        d0 = seg * L + c0
                        P.op("act", _L("copy", out=sc[:, d0:d0 + n], in_=acc[:, 0:n]), writes=[acck, (sk, d0)])
                        yield

            def dsa_bisect(i, scres, res):
                L = (i + 1) * 128
                W2 = 2 * L
                mb, mk = res
                if i == 0:
                    P.op("pool", _L("tensor_copy", out=mb[:, 0:128], in_=caus_b[:]), reads=["caus_b"], writes=[mk])
                    P.op("pool", _L("tensor_copy", out=mb[:, 128:256], in_=pb_b[:]), reads=["pb_b"], writes=[mk])
                    return
                sc, sk = scres
                allsc = [(sk, seg * L + c0) for seg in range(2) for c0 in range(0, L, 512)]
                P.op("dve", _L("max", out=m8[:, 0:8], in_=sc[:, 0:W2]), reads=allsc, writes=["m8"])
                P.op("dve", _L("tensor_copy", out=bs[:, 0:1], in_=m8[:, 0:1]), reads=["m8"], writes=[("bs", 0)])
                P.op("dve", _L("tensor_reduce", out=bs[:, 1:2], in_=sc[:, 0:W2], axis=AX.X, op=ALU.min),
                     reads=allsc, writes=[("bs", 1)])
                yield
                P.op("dve", _L("tensor_tensor", out=bs[:, 2:3], in0=bs[:, 0:1], in1=bs[:, 1:2], op=ALU.subtract),
                     reads=[("bs", 0), ("bs", 1)], writes=["bs"])
                P.op("dve", _L("tensor_scalar", out=Wn[:], in0=pow2c[:], scalar1=bs[:, 2:3], scalar2=None, op0=ALU.mult),
                     reads=["pow2c", "bs"], writes=["Wn"])
                P.op("dve", _L("tensor_tensor", out=bs[:, 3:4], in0=bs[:, 1:2], in1=Wn[:, 0:1], op=ALU.add),
                     reads=["Wn", ("bs", 1)], writes=["bs"])
                P.op("dve", _L("tensor_tensor", out=sc[:, i * 128:(i + 1) * 128], in0=sc[:, i * 128:(i + 1) * 128],
                               in1=caus_f[:], op=ALU.add), reads=["caus_f", ("bs", 0), ("bs", 1)], writes=allsc)
                P.op("dve", _L("tensor_tensor", out=sc[:, L + i * 128:L + (i + 1) * 128],
                               in0=sc[:, L + i * 128:L + (i + 1) * 128], in1=pb_f[:], op=ALU.add),
                     reads=["pb_f"], writes=allsc)
                yield
                for n_ in range(NB):
                    P.op("dve", _L("tensor_scalar", out=mb[:, 0:W2], in0=sc[:, 0:W2], scalar1=bs[:, 3:4], scalar2=None,
                                   op0=ALU.is_ge, op1=ALU.add, accum_out=bs[:, 4:5]),
                         reads=allsc + ["bs"], writes=[mk, "bs"])
                    P.op("dve", _L("tensor_scalar", out=bs[:, 5:6], in0=bs[:, 4:5], scalar1=256.0, scalar2=-0.5,
                                   op0=ALU.is_ge, op1=ALU.add), writes=["bs"])
                    P.op("dve", _L("scalar_tensor_tensor", out=bs[:, 3:4], in0=bs[:, 5:6], scalar=Wn[:, n_:n_ + 1],
                                   in1=bs[:, 3:4], op0=ALU.mult, op1=ALU.add), reads=["Wn"], writes=["bs"])
                    yield
                P.op("dve", _L("tensor_tensor", out=bs[:, 6:7], in0=bs[:, 3:4], in1=Wn[:, NB:NB + 1], op=ALU.subtract),
                     reads=["Wn"], writes=["bs"])
                P.op("dve", _L("tensor_scalar", out=mb[:, 0:W2], in0=sc[:, 0:W2], scalar1=bs[:, 6:7], scalar2=MNEG,
                               op0=ALU.is_lt, op1=ALU.mult), reads=allsc + ["bs"], writes=[mk])
                yield

            def load_q(kind, i):
                qt, qk_ = qt_r.next()
                P.dma("sp", _L("dma_start", out=qt[:], in_=QT_d[kind][i]), writes=[qk_])
                return qt, qk_

            for s_ in range(NT):
                kv, kvk = kv_r.next()
                P.dma("sp", _L("dma_start", out=kv[:, 0:512], in_=KT_d["b"][s_]), writes=[(kvk, "k")])
                P.op("dve", _L("tensor_reduce", out=ksum[:, s_ * 4:(s_ + 1) * 4], in_=A(kv, 0, [[2080, 128], [128, 4], [1, 128]]),
                               axis=AX.X, op=ALU.add), reads=[(kvk, "k")], writes=[("ksum", s_)])
            P.op("dve", _L("tensor_tensor", out=A(kmf, 0, [[64, 128], [16, 4], [1, 16]]),
                           in0=A(ksum, 0, [[128, 128], [1, 4], [4, 16]]),
                           in1=A(ksum, 64, [[128, 128], [1, 4], [4, 16]]), op=ALU.add),
                 reads=[("ksum", s_) for s_ in range(NT)], writes=["kmf"])
            P.op("dve", _L("tensor_scalar", out=kmT[:], in0=kmf[:], scalar1=1.0 / 256, scalar2=None, op0=ALU.mult),
                 reads=["kmf"], writes=["kmT"])

            sel_r = Rot(ph, nc, "selr", 2, [128, 64], F32)
            acc_r = Rot(ph, nc, "accr", 8, [128, 132], F32)

            def moba_prep(i, qt, qk_):
                selt, selk = sel_r.next()
                if i < 4:
                    P.op("dve", _L("memset", selt[:], 1.0), writes=[selk])
                    return selt, selk
                ps, pk = nxt("A")
                for h in range(4):
                    P.op("pe", _L("matmul", ps[:, h * 16:(h + 1) * 16], lhsT=qt[:, h * 128:(h + 1) * 128],
                                  rhs=kmT[:, h * 16:(h + 1) * 16], start=True, stop=True), reads=[qk_, "kmT"], writes=[pk])
                P.op("dve", _L("tensor_copy", out=gsb[:], in_=ps[:, 0:64]), writes=[pk, "gsb"])
                P.op("dve", _L("memset", A(gsb, i, [[64, 128], [16, 4], [1, 16 - i]]), -1e30), writes=["gsb"])
                for h in range(4):
                    P.op("dve", _L("max", out=m8[:, h * 8:(h + 1) * 8], in_=gsb[:, h * 16:(h + 1) * 16]),
                         reads=["gsb"], writes=["m8"])
                P.op("dve", _L("tensor_tensor", out=A(selt, 0, [[64, 128], [16, 4], [1, 16]]),
                               in0=A(gsb, 0, [[64, 128], [16, 4], [1, 16]]),
                               in1=A(m8, 2, [[32, 128], [8, 4], [0, 16]]), op=ALU.is_ge), reads=["gsb", "m8"], writes=[selk])
                return selt, selk

            def moba_attention(i, qt, qk_, sel):
                selt, selk = sel
                oa, ok = oa_r.next()
                blocks = [("own", i)] + [("past", n) for n in range(i)]
                accs = [acc_r.next() for _ in range(4)]
                kvs = {}

                def get_kv(bl):
                    if bl not in kvs:
                        kv, kvk = kv_r.next()
                        n = bl[1]
                        P.dma("sp", _L("dma_start", out=A(kv, 0, [[2080, 128], [512, 2], [1, 512]]),
                                       in_=A(KT_d["b"], n * 128 * 512, [[512, 128], [16 * 128 * 512, 2], [1, 512]])),
                              writes=[(kvk, "k")])
                        P.dma("sp", _L("dma_start", out=A(kv, 1024, [[2080, 128], [528, 2], [1, 528]]),
                                       in_=A(V_d["b"], n * 128 * 528, [[528, 128], [16 * 128 * 528, 2], [1, 528]])),
                              writes=[(kvk, "v")])
                        kvs[bl] = (kv, kvk)
                    return kvs[bl]

                units = [(bl, h) for bl in blocks for h in range(4)]

                def emit_st(u):
                    bl, h = u
                    kv, kvk = get_kv(bl)
                    own = (bl[0] == "own")
                    ST, Sk = pB[1], "pB1"
                    for si in range(2):
                        P.op("pe", _L("matmul", ST[:, si * 128:(si + 1) * 128],
                                      lhsT=kv[:, si * 512 + h * 128:si * 512 + (h + 1) * 128],
                                      rhs=qt[:, h * 128:(h + 1) * 128], start=True, stop=(not own)),
                             reads=[(kvk, "k"), qk_], writes=[Sk])
                        if own:
                            bt, bk_ = (caus_b, "caus_b") if si == 0 else (pb_b, "pb_b")
                            P.op("pe", _L("matmul", ST[:, si * 128:(si + 1) * 128], lhsT=bt[:], rhs=ident_b[:],
                                          start=False, stop=True), reads=[bk_, "ident_b"], writes=[Sk])
                    pt, pk = PT_r.next()
                    P.op("act", _L("activation", out=pt[:, 0:256], in_=ST[:, 0:256], func=AF.Exp, scale=ATT_SCALE),
                         writes=[Sk, pk])
                    return pt, pk

                def emit_pv(u, ptk):
                    bl, h = u
                    pt, pk = ptk
                    kv, kvk = get_kv(bl)
                    acc, ack = accs[h]
                    O, Ok = pO[1], "pO1"
                    for si in range(2):
                        P.op("pe", _L("matmul", O[:, 0:129], lhsT=pt[:, si * 128:(si + 1) * 128],
                                      rhs=kv[:, 1024 + si * 528 + h * 132:1024 + si * 528 + h * 132 + 129],
                                      start=(si == 0), stop=(si == 1)), reads=[pk, (kvk, "v")], writes=[Ok])
                    if bl[0] == "own":
                        P.op("dve", _L("tensor_copy", out=acc[:, 0:129], in_=O[:, 0:129]), writes=[Ok, ack])
                    else:
                        j = h * 16 + bl[1]
                        P.op("dve", _L("scalar_tensor_tensor", out=acc[:, 0:129], in0=O[:, 0:129], scalar=selt[:, j:j + 1],
                                       in1=acc[:, 0:129], op0=ALU.mult, op1=ALU.add), reads=[selk], writes=[Ok, ack])
                    if bl == blocks[-1]:
                        P.op("dve", _L("reciprocal", out=rden[:, 4 + h:5 + h], in_=acc[:, 128:129]), reads=[ack],
                             writes=[("rdenb", h)])
                        P.op("dve", _L("tensor_scalar", out=oa[:, h * 128:(h + 1) * 128], in0=acc[:, 0:128],
                                       scalar1=rden[:, 4 + h:5 + h], scalar2=None, op0=ALU.mult),
                             reads=[("rdenb", h), ack], writes=[ok])

                prev = None
                for u in units:
                    ptk = emit_st(u)
                    if prev is not None:
                        emit_pv(*prev)
                    prev = (u, ptk)
                    yield
                emit_pv(*prev)
                pt_, pk_ = nxt("T")
                for h in range(4):
                    P.op("pe", _L("transpose", out=pt_[:, h * 128:(h + 1) * 128], in_=oa[:, h * 128:(h + 1) * 128],
                                  identity=ident_b[:]), reads=[ok, "ident_b"], writes=[pk_])
                ost, osk = oaT_r.next()
                P.op("act", _L("copy", out=ost[:], in_=pt_[:, 0:512]), writes=[pk_, osk])
                P.dma("act", _L("dma_start", out=oT_d["b"][i], in_=ost[:]), reads=[osk], writes=[("oT", "b", i)])
                yield

            scs, mbs = {}, {}
            mprep = {}
            for t in range(-2, NO):
                gens = []
                j = t + 2
                if 1 <= j < NO:
                    scs[j] = sc_r.next()
                    gens.append(dsa_scores(j, scs[j]))
                j = t + 1
                if 0 <= j < NO:
                    mbs[j] = MB_r.next()
                    gens.append(dsa_bisect(j, scs.get(j), mbs[j]))
                    qb = load_q("b", j)
                    mprep[j] = (qb, moba_prep(j, *qb))
                if t >= 0:
                    qt, qk_ = load_q("a", t)
                    mb, mk = mbs[t]
                    gens.append(attention(t, qt, qk_,
                                          lambda h, cb, mb=mb, mk=mk: (mb[:, cb * 128:(cb + 1) * 128], ident_b[:], [mk, "ident_b"]),
                                          "a"))
                    qb, pb_ = mprep[t]
                    gens.append(moba_attention(t, qb[0], qb[1], pb_))
                interleave(*gens)
            P.barrier()
            chk("p5")

        if True:
            h2T = sb("h2T", [128, 8, NO * 128], BF16)
            with contextlib.ExitStack() as p6:
                def sb6(name, shape, dt):
                    return sb(name, shape, dt, st=p6)
                wstg = Rot(p6, nc, "wstg6", 2, [128, 1024], F32)
                wbr = {"a": sb6("wbra", [128, 4, D], BF16), "b": sb6("wbrb", [128, 4, D], BF16)}
                wout = sb6("wout", [128, 8, D], BF16)
                GTm = sb6("GTm", [128, D], F32)
                SHf = sb6("SHf", [128, D], F32)
                G1f = sb6("G1f", [128, D], F32)
                gt_r = {"a": Rot(p6, nc, "gta", 1, [128, 8 * 512], BF16), "b": Rot(p6, nc, "gtb", 1, [128, 8 * 512], BF16)}
                os_r = {"a": Rot(p6, nc, "osa", 2, [128, 4 * 512], BF16), "b": Rot(p6, nc, "osb", 2, [128, 4 * 512], BF16)}
                m1_r = Rot(p6, nc, "m1", 2, [128, 512], F32)
                m2_r = Rot(p6, nc, "m2", 2, [128, 512], F32)
                mTall = sb6("mTall", [128, 8 * NO * 128], BF16)
                xbuf = Rot(p6, nc, "xb6", 3, [128, D], F32)
                yt_r = Rot(p6, nc, "yt", 2, [128, D], F32)
                x1_r = Rot(p6, nc, "x1t", 3, [128, D], F32)
                tmpf = Rot(p6, nc, "tmpf6", 2, [128, D], F32)
                hb = Rot(p6, nc, "hb6", 3, [128, D], BF16)
                junk = sb6("junk6", [128, D], BF16)

                def load_rows(dst3, dkey, src_d, r0, nrows_chunks):
                    for c in range(nrows_chunks):
                        stg, sk = wstg.next()
                        P.dma("sp", _L("dma_start", out=stg[:], in_=src_d[r0 + c * 128:r0 + (c + 1) * 128, :]),
                              writes=[sk])
                        P.op("act", _L("copy", out=dst3[:, c, :], in_=stg[:]), reads=[sk],
                             writes=[(dkey, c)])

                load_rows(wbr["a"], "wbra", wbr_d["a"], 0, 4)
                load_rows(wbr["b"], "wbrb", wbr_d["b"], 0, 4)
                load_rows(wout, "wout", wout_d, 0, 8)
                P.dma("sp", _L("dma_start", out=GTm[:], in_=modd[0]), reads=[("modd", 0)], writes=["GTm"])
                P.dma("sp", _L("dma_start", out=SHf[:], in_=modd[1]), reads=[("modd", 1)], writes=["SHf"])
                P.dma("sp", _L("dma_start", out=G1f[:], in_=modd[2]), reads=[("modd", 2)], writes=["G1f"])

                def norm_mod_T6(xt, xk, col0):
                    P.op("act", _L("activation", out=junk[:], in_=xt[:], func=AF.Square, accum_out=ssm[:, 0:1]),
                         reads=[xk], writes=["junk6", "ssm"])
                    P.op("dve", _L("tensor_scalar", out=ssm[:, 1:2], in0=ssm[:, 0:1], scalar1=1.0 / D, scalar2=EPS,
                                                          op0=ALU.mult, op1=ALU.add), writes=["ssm"])
                    P.op("act", _L("activation", out=ssm[:, 2:3], in_=ssm[:, 1:2], func=AF.Ln), writes=["ssm"])
                    P.op("act", _L("activation", out=ssm[:, 3:4], in_=ssm[:, 2:3], func=AF.Exp, scale=-0.5), writes=["ssm"])
                    tf, tk = tmpf.next()
                    P.op("dve", _L("scalar_tensor_tensor", out=tf[:], in0=xt[:], scalar=ssm[:, 3:4], in1=G1f[:],
                                                                 op0=ALU.mult, op1=ALU.mult), reads=[xk, "ssm", "G1f"], writes=[tk])
                    hbt, hk = hb.next()
                    P.op("dve", _L("tensor_tensor", out=hbt[:], in0=tf[:], in1=SHf[:], op=ALU.add),
                         reads=[tk, "SHf"], writes=[hk])
                    pt, pk = nxt("T")
                    for k in range(8):
                        P.op("pe", _L("transpose", out=pt[:, k * 128:(k + 1) * 128], in_=hbt[:, k * 128:(k + 1) * 128],
                                                              identity=ident_b[:]), reads=[hk, "ident_b"], writes=[pk])
                    P.op("act", _L("copy", out=A(h2T, col0, [[8 * NO * 128, 128], [NO * 128, 8], [1, 128]]),
                                                 in_=A(pt, 0, [[1024, 128], [128, 8], [1, 128]])),
                         writes=[pk, ("h2T", col0 // 128)])

                for tg in range(4):
                    gts = {}
                    for kind in ("a", "b"):
                        gt, gk_ = gt_r[kind].next()
                        for fc in range(8):
                            P.dma("sp", _L("dma_start",
                                out=gt[:, fc * 512:(fc + 1) * 512], in_=g_d[kind][fc][:, tg * 512:(tg + 1) * 512]),
                                writes=[(gk_, fc)])
                        gts[kind] = (gt, gk_)
                    oss = {}
                    for kind in ("a", "b"):
                        ot_, otk_ = os_r[kind].next()
                        for tl in range(4):
                            P.dma("sp", _L("dma_start", out=A(ot_, tl * 128, [[2048, 128], [512, 4], [1, 128]]),
                                           in_=A(oT_d[kind], (tg * 4 + tl) * 128 * 512, [[512, 128], [128, 4], [1, 128]])),
                                  reads=[("oT", kind, tg * 4 + tl)], writes=[(otk_, tl)])
                        oss[kind] = (ot_, otk_)
                    mT, mTk = mTall, ("mT", tg)
                    for fc in range(8):
                        pss = {}
                        for kind in ("a", "b"):
                            ps, pk = nxt("O")
                            for h in range(4):
                                P.op("pe", _L("matmul",
                                    ps[:, :], lhsT=wbr[kind][:, h, fc * 128:(fc + 1) * 128],
                                    rhs=oss[kind][0][:, h * 512:(h + 1) * 512], start=(h == 0), stop=(h == 3)),
                                    reads=[("wbr" + kind, h)] + [(oss[kind][1], j) for j in range(4)], writes=[pk])
                            pss[kind] = (ps, pk)
                        m1, k1 = m1_r.next()
                        m2, k2 = m2_r.next()
                        P.op("dve", _L("tensor_tensor", out=m1[:], in0=pss["a"][0][:, :],
                                                                           in1=gts["a"][0][:, fc * 512:(fc + 1) * 512], op=ALU.mult),
                             reads=[(gts["a"][1], fc)], writes=[pss["a"][1], k1])
                        P.op("dve", _L("tensor_tensor", out=m2[:], in0=pss["b"][0][:, :],
                                                                           in1=gts["b"][0][:, fc * 512:(fc + 1) * 512], op=ALU.mult),
                             reads=[(gts["b"][1], fc)], writes=[pss["b"][1], k2])
                        P.op("pool", _L("tensor_tensor",
                            out=mT[:, fc * 2048 + tg * 512:fc * 2048 + (tg + 1) * 512], in0=m1[:], in1=m2[:], op=ALU.add), reads=[k1, k2], writes=[(mTk, fc)])
                ssm_r6 = Rot(p6, nc, "ssm6", 3, [128, 8], F32)

                def t_s0(tile, c):
                    tg = tile // 4
                    xt, xk = xbuf.next()
                    P.dma("sp", _L("dma_start", out=xt[:], in_=x_d[tile * 128:(tile + 1) * 128, :]), writes=[xk])
                    yt, yk = yt_r.next()
                    for hf in range(2):
                        ps, pk = nxt("A")
                        for k in range(8):
                            P.op("pe", _L("matmul", ps[:, :], lhsT=mTall[:, k * 2048 + tile * 128:k * 2048 + (tile + 1) * 128],
                                          rhs=wout[:, k, hf * 512:(hf + 1) * 512], start=(k == 0), stop=(k == 7)),
                                 reads=[(("mT", tg), k), ("wout", k)], writes=[pk])
                        P.op("dve", _L("tensor_tensor", out=yt[:, hf * 512:(hf + 1) * 512], in0=ps[:, :],
                                       in1=GTm[:, hf * 512:(hf + 1) * 512], op=ALU.mult), reads=["GTm"], writes=[pk, (yk, hf)])
                    c["x"], c["y"] = (xt, xk), (yt, yk)

                def t_s1(tile, c):
                    (xt, xk), (yt, yk) = c["x"], c["y"]
                    x1t, x1k = x1_r.next()
                    P.op("pool", _L("tensor_tensor", out=x1t[:], in0=yt[:], in1=xt[:], op=ALU.add),
                         reads=[(yk, 0), (yk, 1), xk], writes=[x1k])
                    P.dma("sp", _L("dma_start", out=x1_d[tile * 128:(tile + 1) * 128, :], in_=x1t[:]),
                          reads=[x1k], writes=[("x1d", tile)])
                    sm, smk = ssm_r6.next()
                    P.op("act", _L("activation", out=junk[:], in_=x1t[:], func=AF.Square, accum_out=sm[:, 0:1]),
                         reads=[x1k], writes=["junk6", smk])
                    P.op("dve", _L("tensor_scalar", out=sm[:, 1:2], in0=sm[:, 0:1], scalar1=1.0 / D, scalar2=EPS,
                                   op0=ALU.mult, op1=ALU.add), writes=[smk])
                    P.op("act", _L("activation", out=sm[:, 2:3], in_=sm[:, 1:2], func=AF.Ln), writes=[smk])
                    P.op("act", _L("activation", out=sm[:, 3:4], in_=sm[:, 2:3], func=AF.Exp, scale=-0.5), writes=[smk])
                    c["x1"], c["sm"] = (x1t, x1k), (sm, smk)

                def t_s2(tile, c):
                    (x1t, x1k), (sm, smk) = c["x1"], c["sm"]
                    tf, tk = tmpf.next()
                    P.op("dve", _L("scalar_tensor_tensor", out=tf[:], in0=x1t[:], scalar=sm[:, 3:4], in1=G1f[:],
                                   op0=ALU.mult, op1=ALU.mult), reads=[x1k, smk, "G1f"], writes=[tk])
                    hbt, hk = hb.next()
                    P.op("dve", _L("tensor_tensor", out=hbt[:], in0=tf[:], in1=SHf[:], op=ALU.add), reads=[tk, "SHf"], writes=[hk])
                    c["hb"] = (hbt, hk)

                def t_s3(tile, c):
                    hbt, hk = c["hb"]
                    pt, pk = nxt("T")
                    for k in range(8):
                        P.op("pe", _L("transpose", out=pt[:, k * 128:(k + 1) * 128], in_=hbt[:, k * 128:(k + 1) * 128],
                                      identity=ident_b[:]), reads=[hk, "ident_b"], writes=[pk])
                    P.op("act", _L("copy", out=A(h2T, tile * 128, [[8 * NO * 128, 128], [NO * 128, 8], [1, 128]]),
                                   in_=A(pt, 0, [[1024, 128], [128, 8], [1, 128]])), writes=[pk, ("h2T", tile)])

                run_pipeline(list(range(NO)), [t_s0, t_s1, t_s2, t_s3])
                P.barrier()
                chk("p6")

            aT = sb("aT", [128, 22, NO * 128], BF16)
            wd = sb("wd", [128, 22, D], BF16)
            wstg8 = Rot(top, nc, "wstg8", 2, [128, 1024], F32)

            def load_wd(c):
                stg, sk = wstg8.next()
                P.dma("sp", _L("dma_start", out=stg[:], in_=wdn_d[c * 128:(c + 1) * 128, :]), writes=[sk])
                P.op("pool", _L("tensor_copy", out=wd[:, c, :], in_=stg[:]), reads=[sk], writes=[("wd", c)])

            with contextlib.ExitStack() as p7:
                wstg = Rot(p7, nc, "wstg7", 2, [128, 8 * 256], F32)
                wgu = Rot(p7, nc, "wgu", 2, [128, 8 * 256], BF16)
                sg_r = Rot(p7, nc, "sg", 2, [128, 512], F32)
                gu_src = wgu_d.ap().rearrange("(k p) n -> p k n", p=128)
                for c in range(22):
                    stg, sk = wstg.next()
                    wb, wk = wgu.next()
                    P.dma("sp", _L("dma_start", out=A(stg, 0, [[2048, 128], [256, 8], [1, 128]]),
                                                                   in_=gu_src[:, :, c * 128:(c + 1) * 128]), writes=[(sk, 0)])
                    P.dma("sp", _L("dma_start", out=A(stg, 128, [[2048, 128], [256, 8], [1, 128]]),
                                                                   in_=gu_src[:, :, DFF + c * 128:DFF + (c + 1) * 128]), writes=[(sk, 1)])
                    P.op("pool", _L("tensor_copy", out=wb[:], in_=stg[:]), reads=[(sk, 0), (sk, 1)], writes=[wk])
                    for tg in range(4):
                        pg, pgk = nxt("A")
                        pu, puk = nxt("O")
                        hk = [("h2T", tg * 4 + j) for j in range(4)]
                        for k in range(8):
                            P.op("pe", _L("matmul",
                                pg[:, :], lhsT=wb[:, k * 256:k * 256 + 128], rhs=h2T[:, k, tg * 512:(tg + 1) * 512],
                                start=(k == 0), stop=(k == 7)), reads=[wk] + hk, writes=[pgk])
                        for k in range(8):
                            P.op("pe", _L("matmul",
                                pu[:, :], lhsT=wb[:, k * 256 + 128:k * 256 + 256], rhs=h2T[:, k, tg * 512:(tg + 1) * 512],
                                start=(k == 0), stop=(k == 7)), reads=[wk] + hk, writes=[puk])
                        sg, sgk = sg_r.next()
                        P.op("act", _L("activation", out=sg[:], in_=pg[:, :], func=AF.Silu), writes=[pgk, sgk])
                        P.op("dve", _L("tensor_tensor",
                            out=aT[:, c, tg * 512:(tg + 1) * 512], in0=pu[:, :], in1=sg[:], op=ALU.mult),
                            reads=[sgk], writes=[puk, ("aT", c, tg)])
                    load_wd(c)
                P.barrier()
                chk("p7")

            with contextlib.ExitStack() as p8:
                GTf = sb("GTf", [128, D], F32, st=p8)
                xbuf = Rot(p8, nc, "xb8", 2, [128, D], F32)
                yt_r = Rot(p8, nc, "yt8", 1, [128, D], F32)
                ot_r = Rot(p8, nc, "ot8", 2, [128, D], F32)
                P.dma("sp", _L("dma_start", out=GTf[:], in_=modd[3]), reads=[("modd", 3)], writes=["GTf"])
                for tile in range(NO):
                    xt, xk = xbuf.next()
                    P.dma("sp", _L("dma_start", out=xt[:], in_=x1_d[tile * 128:(tile + 1) * 128, :]),
                          reads=[("x1d", tile)], writes=[xk])
                    yt, yk = yt_r.next()
                    for hf in range(2):
                        ps, pk = nxt("A")
                        for c in range(22):
                            P.op("pe", _L("matmul",
                                ps[:, :], lhsT=aT[:, c, tile * 128:(tile + 1) * 128], rhs=wd[:, c, hf * 512:(hf + 1) * 512],
                                start=(c == 0), stop=(c == 21)), reads=[("aT", c, tile // 4), ("wd", c)], writes=[pk])
                        P.op("dve", _L("tensor_tensor",
                            out=yt[:, hf * 512:(hf + 1) * 512], in0=ps[:, :], in1=GTf[:, hf * 512:(hf + 1) * 512], op=ALU.mult),
                            reads=["GTf"], writes=[pk, (yk, hf)])
                    ot, otk = ot_r.next()
                    P.op("pool", _L("tensor_tensor", out=ot[:], in0=yt[:], in1=xt[:], op=ALU.add),
                         reads=[(yk, 0), (yk, 1), xk], writes=[otk])
                    P.dma("sp", _L("dma_start", out=out_d[tile * 128:(tile + 1) * 128, :], in_=ot[:]),
                          reads=[otk], writes=[("outd", tile)])
                P.barrier()
        P.finalize()


def _rope_table(pos, dim):
    inv = (np.float32(10000.0) ** (-np.arange(0, dim, 2, dtype=np.float32) / np.float32(dim))).astype(np.float32)
    ang = pos.astype(np.float32)[:, None] * inv[None, :]
    c = np.cos(ang).astype(np.float32)
    s = np.sin(ang).astype(np.float32)
    return np.concatenate([c, c, -s, s], axis=1).astype(np.float32)


def kernel(x, c, w_mod, b_mod, g_mix_norm, g_ffn_norm, w_in, g_q_dsa, g_k_dsa, g_q_moba, g_k_moba,
           w_br_dsa, w_br_moba, w_out, w_gate_up, w_down, _ret_maps=False):
    f = lambda a: np.ascontiguousarray(np.asarray(a, dtype=np.float32))
    x = f(x); c = f(c)
    shared = {
        "w_mod": f(w_mod[0]), "b_mod": f(b_mod[0]).reshape(1, -1), "g_mix": f(g_mix_norm[0]).reshape(1, -1),
        "g_ffn": f(g_ffn_norm[0]).reshape(1, -1), "w_in": f(w_in[0]),
        "g_q_dsa": f(g_q_dsa[0]).reshape(1, -1), "g_k_dsa": f(g_k_dsa[0]).reshape(1, -1),
        "g_q_moba": f(g_q_moba[0]).reshape(1, -1), "g_k_moba": f(g_k_moba[0]).reshape(1, -1),
        "w_br_dsa": f(w_br_dsa[0]), "w_br_moba": f(w_br_moba[0]), "w_out": f(w_out[0]),
        "w_gate_up": f(w_gate_up[0]), "w_down": f(w_down[0]),
    }
    in_maps = []
    perms = []
    for core in range(8):
        b, p = core // 2, core % 2
        tiles = [2 * i + p for i in range(NO)] + [2 * i + 1 - p for i in range(NO)]
        pos = np.concatenate([np.arange(t * 128, (t + 1) * 128) for t in tiles])
        perms.append(pos)
        m = dict(shared)
        m["x"] = np.ascontiguousarray(x[b][pos])
        m["cT"] = np.ascontiguousarray(c[b].reshape(8, 128).T)
        m["ropeH"] = _rope_table(pos, 128)
        m["ropeI"] = _rope_table(pos, 64)
        m["pb"] = np.full((128, 128), 0.0 if p == 1 else -1e30, dtype=np.float32)
        in_maps.append(m)
    if _ret_maps:
        return in_maps, perms
    nc = build()
    res = run_bass_kernel_spmd(nc, in_maps, core_ids=list(range(8)))
    out = np.empty((4, S, D), dtype=np.float32)
    for core in range(8):
        b = core // 2
        out[b, perms[core][:NO * 128]] = res.results[core]["out"]
    return out
```
